# Optimizing a Trainium2 kernel written in Bass

```python
import math
import jax, jax.numpy as jnp
from jax import lax
import numpy as np

D_MODEL = 2048
BATCH = 4
SEQ = 8192
DEPTH = 2

CHUNK = 64
N_META = 16
MIX_WIDTH = D_MODEL
HEAD_DIM = 64
ATTN_WIDTH = MIX_WIDTH // 4
ATTN_HEADS = ATTN_WIDTH // HEAD_DIM
SSM_WIDTH = MIX_WIDTH // 4
SSM_GROUP = 16
SSM_GROUPS = SSM_WIDTH // SSM_GROUP
SSM_STATE = 64
CONV_WIDTH = MIX_WIDTH - ATTN_WIDTH - SSM_WIDTH
DW_CONV_SIZE = 31
IN_WIDTH = 3 * ATTN_WIDTH + SSM_WIDTH + 2 * CONV_WIDTH
D_FF = 5632
Q_BLOCK = 128
RMS_EPS = 1e-6
LN_EPS = 1e-5
DT_MIN = 1e-3
DT_MAX = 1e-1
LAMBDA_RE_MAX = -1e-4

kernel_name = 'hybrid_stickbreak_s5_conformer_trunk'


def rms_norm(x, g):
    x32 = x.astype(jnp.float32)
    y = x32 * lax.rsqrt(jnp.mean(x32 * x32, axis=-1, keepdims=True) + RMS_EPS)
    return (y * g.astype(jnp.float32)).astype(x.dtype)


def layer_norm(x, g, b):
    x32 = x.astype(jnp.float32)
    xc = x32 - jnp.mean(x32, axis=-1, keepdims=True)
    y = xc * lax.rsqrt(jnp.mean(xc * xc, axis=-1, keepdims=True) + LN_EPS)
    return (y * g.astype(jnp.float32) + b.astype(jnp.float32)).astype(x.dtype)


def swiglu_ffn(x, w_gate, w_up, w_down):
    return (jax.nn.silu(x @ w_gate) * (x @ w_up)) @ w_down


def stick_breaking_attention(q, k, v):
    b, lp, h, dh = q.shape
    nb = lp // Q_BLOCK
    scale = dh ** -0.5
    k32 = k.astype(jnp.float32)
    v32 = v.astype(jnp.float32)
    kpos = jnp.arange(lp)
    q_blocks = q.reshape(b, nb, Q_BLOCK, h, dh).swapaxes(0, 1)

    def one_block(args):
        q_blk, blk = args
        qpos = blk * Q_BLOCK + jnp.arange(Q_BLOCK)
        earlier = kpos[None, :] < qpos[:, None]
        z = jnp.einsum('bqhd,bkhd->bhqk', q_blk.astype(jnp.float32), k32) * scale
        log_keep = jnp.where(earlier, jax.nn.log_sigmoid(-z), 0.0)
        log_survive = lax.cumsum(log_keep, axis=3, reverse=True) - log_keep
        att = jnp.where(earlier, jnp.exp(jax.nn.log_sigmoid(z) + log_survive), 0.0)
        return jnp.einsum('bhqk,bkhd->bqhd', att, v32)

    out = lax.map(one_block, (q_blocks, jnp.arange(nb)))
    return out.swapaxes(0, 1).reshape(b, lp, h, dh).astype(q.dtype)


def s5_ssm(u, lam_re, lam_im, log_step, b_re, b_im, c_re, c_im, d_skip):
    f32 = jnp.float32
    lam = lax.complex(jnp.minimum(lam_re.astype(f32), LAMBDA_RE_MAX), lam_im.astype(f32))
    step = jnp.exp(log_step.astype(f32))[:, None]
    lam_bar = jnp.exp(lam * step)
    b_cplx = lax.complex(b_re.astype(f32), b_im.astype(f32))
    b_bar = ((lam_bar - 1.0) / lam)[:, :, None] * b_cplx
    c_cplx = lax.complex(c_re.astype(f32), c_im.astype(f32))
    u32 = u.astype(f32)
    bu = jnp.einsum('gph,blgh->blgp', b_bar, u32.astype(jnp.complex64))
    decay = jnp.broadcast_to(lam_bar, bu.shape)

    def combine(left, right):
        a_l, s_l = left
        a_r, s_r = right
        return a_l * a_r, a_r * s_l + s_r

    _, states = lax.associative_scan(combine, (decay, bu), axis=1)
    return jnp.real(jnp.einsum('ghp,blgp->blgh', c_cplx, states)) + d_skip.astype(f32) * u32


def conformer_conv(a, gate, w_dw, b_dw, ln_g, ln_b, w_pw, b_pw):
    hidden = a * jax.nn.sigmoid(gate)
    hidden = jnp.pad(hidden, ((0, 0), (DW_CONV_SIZE - 1, 0), (0, 0)))
    hidden = lax.conv_general_dilated(
        hidden, w_dw[:, None, :], window_strides=(1,), padding='VALID',
        dimension_numbers=('NWC', 'WIO', 'NWC'), feature_group_count=CONV_WIDTH) + b_dw
    hidden = jax.nn.silu(layer_norm(hidden, ln_g, ln_b))
    return hidden @ w_pw + b_pw


def setup_inputs(seed: int = 0) -> dict:
    key = jax.random.key(seed)
    ks = iter(jax.random.split(key, 40))
    f32 = jnp.float32

    def nrm(shape, scale):
        return scale * jax.random.normal(next(ks), shape, f32)

    def gain(shape):
        return 1.0 + 0.02 * jax.random.normal(next(ks), shape, f32)

    lam_im_base = jnp.pi * jnp.arange(SSM_STATE, dtype=f32)
    return {
        'x': nrm((BATCH, SEQ, D_MODEL), 1.0),
        'meta': nrm((N_META, D_MODEL), 1.0),
        'ffn1_norm': gain((DEPTH, D_MODEL)),
        'ffn1_w_gate': nrm((DEPTH, D_MODEL, D_FF), D_MODEL ** -0.5),
        'ffn1_w_up': nrm((DEPTH, D_MODEL, D_FF), D_MODEL ** -0.5),
        'ffn1_w_down': nrm((DEPTH, D_FF, D_MODEL), D_FF ** -0.5),
        'mix_norm': gain((DEPTH, D_MODEL)),
        'w_in': nrm((DEPTH, D_MODEL, IN_WIDTH), D_MODEL ** -0.5),
        'q_norm': gain((DEPTH, HEAD_DIM)),
        'k_norm': gain((DEPTH, HEAD_DIM)),
        'ssm_lambda_re': -0.5 + nrm((DEPTH, SSM_GROUPS, SSM_STATE), 0.01),
        'ssm_lambda_im': lam_im_base + nrm((DEPTH, SSM_GROUPS, SSM_STATE), 0.01),
        'ssm_log_step': jax.random.uniform(next(ks), (DEPTH, SSM_GROUPS), f32,
                                           math.log(DT_MIN), math.log(DT_MAX)),
        'ssm_b_re': nrm((DEPTH, SSM_GROUPS, SSM_STATE, SSM_GROUP), (2 * SSM_GROUP) ** -0.5),
        'ssm_b_im': nrm((DEPTH, SSM_GROUPS, SSM_STATE, SSM_GROUP), (2 * SSM_GROUP) ** -0.5),
        'ssm_c_re': nrm((DEPTH, SSM_GROUPS, SSM_GROUP, SSM_STATE), (2 * SSM_STATE) ** -0.5),
        'ssm_c_im': nrm((DEPTH, SSM_GROUPS, SSM_GROUP, SSM_STATE), (2 * SSM_STATE) ** -0.5),
        'ssm_d': nrm((DEPTH, SSM_GROUPS, SSM_GROUP), 1.0),
        'ssm_w_glu': nrm((DEPTH, SSM_WIDTH, SSM_WIDTH), SSM_WIDTH ** -0.5),
        'ssm_b_glu': nrm((DEPTH, SSM_WIDTH), 0.02),
        'conv_w_dw': nrm((DEPTH, DW_CONV_SIZE, CONV_WIDTH), DW_CONV_SIZE ** -0.5),
        'conv_b_dw': nrm((DEPTH, CONV_WIDTH), 0.02),
        'conv_ln_g': gain((DEPTH, CONV_WIDTH)),
        'conv_ln_b': nrm((DEPTH, CONV_WIDTH), 0.02),
        'conv_w_pw': nrm((DEPTH, CONV_WIDTH, CONV_WIDTH), CONV_WIDTH ** -0.5),
        'conv_b_pw': nrm((DEPTH, CONV_WIDTH), 0.02),
        'w_out': nrm((DEPTH, MIX_WIDTH, D_MODEL), MIX_WIDTH ** -0.5),
        'ffn2_norm': gain((DEPTH, D_MODEL)),
        'ffn2_w_gate': nrm((DEPTH, D_MODEL, D_FF), D_MODEL ** -0.5),
        'ffn2_w_up': nrm((DEPTH, D_MODEL, D_FF), D_MODEL ** -0.5),
        'ffn2_w_down': nrm((DEPTH, D_FF, D_MODEL), D_FF ** -0.5),
        'post_norm': gain((DEPTH, D_MODEL)),
    }


def reference(x, meta, ffn1_norm, ffn1_w_gate, ffn1_w_up, ffn1_w_down, mix_norm, w_in,
              q_norm, k_norm, ssm_lambda_re, ssm_lambda_im, ssm_log_step, ssm_b_re, ssm_b_im,
              ssm_c_re, ssm_c_im, ssm_d, ssm_w_glu, ssm_b_glu, conv_w_dw, conv_b_dw,
              conv_ln_g, conv_ln_b, conv_w_pw, conv_b_pw, w_out, ffn2_norm, ffn2_w_gate,
              ffn2_w_up, ffn2_w_down, post_norm):
    batch = x.shape[0]
    meta_b = jnp.broadcast_to(meta.astype(x.dtype)[None], (batch, N_META, D_MODEL))
    h = jnp.concatenate([meta_b, x], axis=1)
    length = h.shape[1]
    pad = (-length) % Q_BLOCK
    split_at = [ATTN_WIDTH, 2 * ATTN_WIDTH, 3 * ATTN_WIDTH,
                3 * ATTN_WIDTH + SSM_WIDTH, 3 * ATTN_WIDTH + SSM_WIDTH + CONV_WIDTH]
    seq_pad = ((0, 0), (0, pad), (0, 0), (0, 0))

    for l in range(DEPTH):
        h = h + 0.5 * swiglu_ffn(rms_norm(h, ffn1_norm[l]), ffn1_w_gate[l], ffn1_w_up[l], ffn1_w_down[l])

        proj = rms_norm(h, mix_norm[l]) @ w_in[l]
        q, k, v, u, conv_a, conv_g = jnp.split(proj, split_at, axis=-1)

        q = rms_norm(q.reshape(batch, length, ATTN_HEADS, HEAD_DIM), q_norm[l])
        k = rms_norm(k.reshape(batch, length, ATTN_HEADS, HEAD_DIM), k_norm[l])
        v = v.reshape(batch, length, ATTN_HEADS, HEAD_DIM)
        attn = stick_breaking_attention(jnp.pad(q, seq_pad), jnp.pad(k, seq_pad), jnp.pad(v, seq_pad))
        attn = attn[:, :length].reshape(batch, length, ATTN_WIDTH)

        y = s5_ssm(u.reshape(batch, length, SSM_GROUPS, SSM_GROUP),
                   ssm_lambda_re[l], ssm_lambda_im[l], ssm_log_step[l], ssm_b_re[l], ssm_b_im[l],
                   ssm_c_re[l], ssm_c_im[l], ssm_d[l])
        y = jax.nn.gelu(y.reshape(batch, length, SSM_WIDTH).astype(h.dtype))
        ssm_out = y * jax.nn.sigmoid(y @ ssm_w_glu[l] + ssm_b_glu[l])

        conv_out = conformer_conv(conv_a, conv_g, conv_w_dw[l], conv_b_dw[l],
                                  conv_ln_g[l], conv_ln_b[l], conv_w_pw[l], conv_b_pw[l])

        mixed = jnp.concatenate([attn, ssm_out, conv_out], axis=-1) @ w_out[l]
        h = h + mixed

        h = h + 0.5 * swiglu_ffn(rms_norm(h, ffn2_norm[l]), ffn2_w_gate[l], ffn2_w_up[l], ffn2_w_down[l])
        h = rms_norm(h, post_norm[l])

    return h[:, N_META:]
```

```python
import numpy as np
from contextlib import ExitStack
import concourse.bass as bass
import concourse.mybir as mybir
from concourse.bass_utils import run_bass_kernel_spmd

F32 = mybir.dt.float32
BF16 = mybir.dt.bfloat16
ALU = mybir.AluOpType
AF = mybir.ActivationFunctionType

ENGS = ("tensor", "vector", "scalar", "gpsimd", "sync")


class Buf:
    __slots__ = ("name", "last_w", "readers")

    def __init__(self, name):
        self.name = name
        self.last_w = None
        self.readers = []


class Prog:
    def __init__(self, nc):
        self.nc = nc
        self.ops = {e: [] for e in ENGS}
        self.dma_cnt = {}
        self.nbuf = 0

    def buf(self, name=None):
        self.nbuf += 1
        return Buf(name or f"b{self.nbuf}")

    def bufs(self, n, name="b"):
        return [self.buf(f"{name}{i}") for i in range(n)]

    def _deps(self, reads, writes):
        deps = []
        for b in reads:
            if b.last_w is not None:
                deps.append(b.last_w)
        for b in writes:
            if b.last_w is not None:
                deps.append(b.last_w)
            deps.extend(b.readers)
        return deps

    def _commit(self, tok, reads, writes):
        for b in reads:
            if tok[0] == "c":
                b.readers = [t for t in b.readers if not (t[0] == "c" and t[1] == tok[1])]
            b.readers.append(tok)
        for b in writes:
            b.last_w = tok
            b.readers = []

    def op(self, eng, fn, reads=(), writes=()):
        deps = self._deps(reads, writes)
        if eng == "tensor":
            deps = [t for t in deps if not (t[0] == "c" and t[1] == "tensor")]
        idx = len(self.ops[eng])
        tok = ("c", eng, idx)
        self.ops[eng].append([fn, deps, None])
        self._commit(tok, reads, writes)
        return tok

    def dma(self, eng, sem, fn, reads=(), writes=()):
        deps = self._deps(reads, writes)
        v = self.dma_cnt.get(sem, 0) + 16
        self.dma_cnt[sem] = v
        tok = ("d", sem, v)
        self.ops[eng].append([fn, deps, sem])
        self._commit(tok, reads, writes)
        return tok

    def barrier(self, bufs=(), exclude=()):
        toks = []
        for e in ENGS:
            if self.ops[e]:
                for i in range(len(self.ops[e]) - 1, -1, -1):
                    if self.ops[e][i][2] is None and self.ops[e][i][0] is not None:
                        toks.append(("c", e, i))
                        break
        for s, v in self.dma_cnt.items():
            if s not in exclude:
                toks.append(("d", s, v))
        for e in ENGS:
            self.ops[e].append([None, list(toks), None])

    def emit(self):
        nc = self.nc
        need = {e: set() for e in ENGS}
        for e in ENGS:
            for fn, deps, sem in self.ops[e]:
                for t in deps:
                    if t[0] == "c":
                        if t[1] == e and e == "tensor":
                            continue
                        need[t[1]].add(t[2])
        ms = {e: {} for e in ENGS}
        for e in ENGS:
            for k, i in enumerate(sorted(need[e])):
                ms[e][i] = k + 1
        self.ms_max = {e: len(ms[e]) for e in ENGS}
        with ExitStack() as st:
            csem = {e: st.enter_context(nc.semaphore(f"c_{e}")) for e in ENGS}
            dsem = {s: st.enter_context(nc.semaphore(f"d_{s}")) for s in self.dma_cnt}
            block = st.enter_context(nc.Block())

            def run(e, eng):
                waited = {}
                for i, (fn, deps, sem) in enumerate(self.ops[e]):
                    for t in deps:
                        if t[0] == "c":
                            if t[1] == e and e == "tensor":
                                continue
                            key, val, h = ("c", t[1]), ms[t[1]][t[2]], csem[t[1]]
                        else:
                            key, val, h = ("d", t[1]), t[2], dsem[t[1]]
                        if waited.get(key, 0) >= val:
                            continue
                        waited[key] = val
                        eng.wait_ge(h, val)
                    if fn is None:
                        continue
                    inst = fn(eng)
                    if sem is not None:
                        inst.then_inc(dsem[sem], 16)
                    elif i in ms[e]:
                        inst.then_inc(csem[e], 1)

            @block.tensor
            def _(eng):
                run("tensor", eng)

            @block.vector
            def _(eng):
                run("vector", eng)

            @block.scalar
            def _(eng):
                run("scalar", eng)

            @block.gpsimd
            def _(eng):
                run("gpsimd", eng)

            @block.sync
            def _(eng):
                run("sync", eng)


class Cfg:
    def __init__(self, D=2048, DFF=5632, H=8, G=32, CW=1024, LP=8320, depth=2):
        self.D, self.DFF, self.H, self.G, self.CW, self.LP, self.depth = D, DFF, H, G, CW, LP, depth
        self.KT = D // 128
        self.MT = DFF // 128
        self.AW = H * 64
        self.AT = self.AW // 128
        self.SW = G * 16
        self.ST = self.SW // 128
        self.CT = CW // 128
        self.IN = 3 * self.AW + self.SW + 2 * CW
        self.MIX = self.AW + self.SW + CW
        assert self.MIX == D
        self.NTT = LP // 128

    def blocks(self, T=1024):
        out, s = [], 0
        while s < self.LP:
            n = min(T, self.LP - s)
            out.append((s, n))
            s += n
        return out


def subtiles(n, maxn=512):
    k = (n + maxn - 1) // maxn
    base = n // k
    assert base * k == n
    return [(i * base, base) for i in range(k)]


PI = 3.14159265358979
TWO_PI = 6.28318530717959
SC2PI = 6.28316
HALFPI_ = 1.57079


class K:
    def __init__(self, nc, cfg, st, debug=False):
        self.nc, self.cfg, self.st, self.debug = nc, cfg, st, debug
        self.P = Prog(nc)
        self.pq = [st.enter_context(nc.psum_tensor(f"pq{i}", [128, 1024], F32)) for i in range(4)]
        self.ps = [self.pq[i // 2][:, (i % 2) * 512:(i % 2 + 1) * 512] for i in range(8)]
        self.ps_b = self.P.bufs(8, "ps")

    def sb(self, st, name, shape, dt):
        self.uid = getattr(self, "uid", 0) + 1
        return st.enter_context(self.nc.sbuf_tensor(f"{name}_{self.uid}", shape, dt))

    def MM(self, out, lhsT, rhs, start, stop, R, W):
        self.P.op("tensor", lambda e: e.matmul(out=out, lhsT=lhsT, rhs=rhs, start=start, stop=stop), R, W)

    def TR(self, out, in_, ident, R, W):
        self.P.op("tensor", lambda e: e.transpose(out=out, in_=in_, identity=ident), R, W)

    def ACT(self, out, in_, func, R, W, bias=None, scale=None):
        kw = {}
        if bias is not None:
            kw["bias"] = bias
        if scale is not None:
            kw["scale"] = scale
        self.P.op("scalar", lambda e: e.activation(out=out, in_=in_, func=func, **kw), R, W)

    def TS(self, eng, out, in0, s1, s2, op0, op1, R, W):
        if op1 is None:
            self.P.op(eng, lambda e: e.tensor_scalar(out=out, in0=in0, scalar1=s1, scalar2=None, op0=op0), R, W)
        else:
            self.P.op(eng, lambda e: e.tensor_scalar(out=out, in0=in0, scalar1=s1, scalar2=s2, op0=op0, op1=op1), R, W)

    def RSQ(self, out, in_, eps, R, W):
        self.P.op("scalar", lambda e: e.activation(out=out, in_=in_, func=AF.Sqrt, bias=self.epsb[eps][:, 0:1]), R + [self.cb], W)
        self.P.op("vector", lambda e: e.reciprocal(out=out, in_=out), W, W)

    def FRAC(self, out, y, ti, tf, R, W, e1="gpsimd"):
        self.CP(e1, ti, y, R + W, W)
        self.CP(e1, tf, ti, W, W)
        self.TT(e1, tf, y, tf, ALU.subtract, R + W, W)
        self.STT("vector", out, tf, 0.5, tf, ALU.is_gt, ALU.subtract, W, W)
        self.STT("vector", out, out, 0.5, out, ALU.is_gt, ALU.subtract, W, W)

    def TT(self, eng, out, in0, in1, op, R, W):
        self.P.op(eng, lambda e: e.tensor_tensor(out=out, in0=in0, in1=in1, op=op), R, W)

    def STT(self, eng, out, in0, scalar, in1, op0, op1, R, W):
        self.P.op(eng, lambda e: e.scalar_tensor_tensor(out=out, in0=in0, scalar=scalar, in1=in1, op0=op0, op1=op1), R, W)

    def CP(self, eng, out, in_, R, W):
        if eng == "scalar":
            self.P.op(eng, lambda e: e.activation(out=out, in_=in_, func=AF.Copy), R, W)
        else:
            self.P.op(eng, lambda e: e.tensor_copy(out=out, in_=in_), R, W)

    def MS(self, eng, ap, val, W):
        self.P.op(eng, lambda e: e.memset(ap, val), (), W)

    def DMA(self, q, sem, out, in_, R, W, slow=False):
        if slow:
            self.P.dma(q, sem, lambda e: e.dma_start(out=out, in_=in_, allow_slow_non_contiguous=True), R, W)
        else:
            self.P.dma(q, sem, lambda e: e.dma_start(out=out, in_=in_), R, W)

    def consts(self):
        nc, P, st, cfg = self.nc, self.P, self.st, self.cfg
        c = self.c = {}
        cb = self.cb = P.buf("consts")
        W = [cb]
        self.epsb = {}
        for ev in (1e-6, 1e-5):
            t = self.sb(st, f"eps{len(self.epsb)}", [128, 1], F32)
            self.MS("vector", t[:, :], ev, W)
            self.epsb[ev] = t
        self.hpib = self.sb(st, "hpib", [128, 1], F32)
        self.MS("vector", self.hpib[:, :], HALFPI_, W)
        onesf = self.sb(st, "onesf", [128, 512], F32)
        self.MS("vector", onesf[:, :], 1.0, W)
        identf = c["identf"] = self.sb(st, "identf", [128, 128], F32)
        P.op("gpsimd", lambda e: e.affine_select(out=identf[:, :], in_=onesf[:, :128], pattern=[[-1, 128]],
                                                compare_op=ALU.is_equal, fill=0.0, base=0, channel_multiplier=1), [cb], W)
        identb = c["identb"] = self.sb(st, "identb", [128, 128], BF16)
        self.CP("vector", identb[:, :], identf[:, :], [cb], W)
        for nm, val in (("onesD", 1.0 / cfg.D), ("onesCW", 1.0 / cfg.CW), ("negones", -1.0)):
            t = c[nm] = self.sb(st, nm, [128, 128], BF16)
            self.MS("vector", t[:, :], val, W)
        o64 = c["ones64"] = self.sb(st, "ones64", [128, 128], BF16)
        self.MS("vector", o64[:, :], 0.0, W)
        self.MS("vector", o64[0:64, 0:64], 1.0 / 64, W)
        self.MS("vector", o64[64:128, 64:128], 1.0 / 64, W)
        tnf = self.sb(st, "tnf", [128, 128], F32)
        P.op("gpsimd", lambda e: e.affine_select(out=tnf[:, :], in_=onesf[:, :128], pattern=[[-1, 128]],
                                                compare_op=ALU.is_ge, fill=0.0, base=0, channel_multiplier=1), [cb], W)
        tneg = c["tneg"] = self.sb(st, "tneg", [128, 128], BF16)
        self.TS("vector", tneg[:, :], tnf[:, :], -1.0, None, ALU.mult, None, [cb], W)
        c["keep"], c["negm"] = [], []
        kf = self.sb(st, "keepf", [128, 512], F32)
        for r in range(4):
            P.op("gpsimd", lambda e, r=r: e.affine_select(out=kf[:, :], in_=onesf[:, :], pattern=[[1, 512]],
                                                         compare_op=ALU.is_gt, fill=0.0, base=-r * 128,
                                                         channel_multiplier=-1), [cb], W)
            kp = self.sb(st, f"keep{r}", [128, 512], BF16)
            ng = self.sb(st, f"negm{r}", [128, 512], BF16)
            self.CP("vector", kp[:, :], kf[:, :], [cb], W)
            self.TS("vector", ng[:, :], kf[:, :], -1.0, 30000.0, ALU.add, ALU.mult, [cb], W)
            c["keep"].append(kp)
            c["negm"].append(ng)

    def norm_block(self, hT, hbs, s0, n, gain, xT, xT_b, rstd, rstd_b, hp, hp_b, sq, sq_b, cnt, tag, f32out=None):
        cfg = self.cfg
        KT = cfg.KT
        ps, ps_b, c, cb = self.ps, self.ps_b, self.c, self.cb
        subs = subtiles(n)
        NHP = len(hp)
        for k in range(KT):
            i = cnt["hp"] % NHP; cnt["hp"] += 1
            self.DMA("gpsimd", f"Fhp{i}", hp[i][:, :n], hT[k * 128:(k + 1) * 128, s0:s0 + n], hbs, [hp_b[i]])
            j = cnt["sq"] % 2; cnt["sq"] += 1
            self.ACT(sq[j][:, :n], hp[i][:, :n], AF.Square, [hp_b[i]], [sq_b[j]])
            for si, (o, w) in enumerate(subs):
                self.MM(ps[si][:, :w], c["onesD"][:, :], sq[j][:, o:o + w], k == 0, k == KT - 1, [sq_b[j], cb], [ps_b[si]])
        for si, (o, w) in enumerate(subs):
            self.RSQ(rstd[:, o:o + w], ps[si][:, :w], 1e-6, [ps_b[si]], [rstd_b])
        for k in range(KT):
            i = cnt["hp"] % NHP; cnt["hp"] += 1
            self.DMA("gpsimd", f"Fhp{i}", hp[i][:, :n], hT[k * 128:(k + 1) * 128, s0:s0 + n], hbs, [hp_b[i]])
            dst = xT[:, k, :n]
            self.STT("vector", dst, hp[i][:, :n], gain[:, k:k + 1], rstd[:, :n], ALU.mult, ALU.mult,
                     [hp_b[i], rstd_b, self.pv_b], [xT_b])

    def ffn(self, hin, hin_b, hout, hout_b, wg_s, wu_s, wd_s, w_b, gain, tag, T=512):
        nc, P, cfg = self.nc, self.P, self.cfg
        KT, MT = cfg.KT, cfg.MT
        ps, ps_b = self.ps, self.ps_b
        TM = min(T, cfg.LP)
        MC = 2 if MT % 2 == 0 else 1
        with ExitStack() as st:
            xTs = [self.sb(st, f"xT{tag}{i}", [128, KT, TM], BF16) for i in range(2)]
            hid = self.sb(st, f"hid{tag}", [128, MT, TM], BF16)
            rstd = self.sb(st, f"rstd{tag}", [128, TM], F32)
            NHP, NW = 3, 2
            hp = [self.sb(st, f"T{tag}hp{i}", [128, TM], F32) for i in range(NHP)]
            sq = [self.sb(st, f"sq{tag}{i}", [128, TM], BF16) for i in range(2)]
            sg = [self.sb(st, f"sg{tag}{i}", [128, TM], BF16) for i in range(2)]
            ob = [self.sb(st, f"T{tag}ob{i}", [128, TM], F32) for i in range(2)]
            wg = [self.sb(st, f"T{tag}wg{i}", [128, KT, MC * 128], BF16) for i in range(NW)]
            wu = [self.sb(st, f"T{tag}wu{i}", [128, KT, MC * 128], BF16) for i in range(NW)]
            wd = [self.sb(st, f"T{tag}wd{i}", [128, MT, 128], BF16) for i in range(NW)]
            xTs_b, hid_b, rstd_b = P.bufs(2), P.buf(), P.buf()
            hp_b, sq_b, sg_b, ob_b = P.bufs(NHP), P.bufs(2), P.bufs(2), P.bufs(2)
            wg_b, wu_b, wd_b = P.bufs(NW), P.bufs(NW), P.bufs(NW)
            cnt = dict(hp=0, sq=0, sg=0, ob=0, wgu=0, wd=0)
            blocks = cfg.blocks(T)

            def norm(bi):
                s0, n = blocks[bi]
                self.norm_block(hin, [], s0, n, gain, xTs[bi % 2], xTs_b[bi % 2], rstd, rstd_b, hp, hp_b, sq, sq_b, cnt, tag)

            def gateup(bi):
                s0, n = blocks[bi]
                subs = subtiles(n)
                xT, xT_b = xTs[bi % 2], xTs_b[bi % 2]
                for mc in range(MT // MC):
                    wi = cnt["wgu"] % NW; cnt["wgu"] += 1
                    sl = slice(mc * MC * 128, (mc + 1) * MC * 128)
                    self.DMA("sync", f"Fwg{wi}", wg[wi][:, :, :], wg_s[:, :, sl], [w_b], [wg_b[wi]])
                    self.DMA("sync", f"Fwu{wi}", wu[wi][:, :, :], wu_s[:, :, sl], [w_b], [wu_b[wi]])
                    for mi in range(MC):
                        m = mc * MC + mi
                        pb = (m % 2) * 4
                        for which, (wt, wtb) in enumerate(((wg, wg_b), (wu, wu_b))):
                            for k in range(KT):
                                for si, (o, w) in enumerate(subs):
                                    bank = pb + which * 2 + si
                                    self.MM(ps[bank][:, :w], wt[wi][:, k, mi * 128:(mi + 1) * 128], xT[:, k, o:o + w],
                                            k == 0, k == KT - 1, [wtb[wi], xT_b], [ps_b[bank]])
                        j = cnt["sg"] % 2; cnt["sg"] += 1
                        for si, (o, w) in enumerate(subs):
                            self.ACT(sg[j][:, o:o + w], ps[pb + si][:, :w], AF.Silu, [ps_b[pb + si]], [sg_b[j]])
                        for si, (o, w) in enumerate(subs):
                            self.TT("vector", hid[:, m, o:o + w], ps[pb + 2 + si][:, :w], sg[j][:, o:o + w], ALU.mult,
                                    [ps_b[pb + 2 + si], sg_b[j]], [hid_b])

            def down(bi):
                s0, n = blocks[bi]
                subs = subtiles(n)
                for d in range(KT):
                    wi = cnt["wd"] % NW; cnt["wd"] += 1
                    self.DMA("sync", f"Fwd{wi}", wd[wi][:, :, :], wd_s[:, :, d * 128:(d + 1) * 128], [w_b], [wd_b[wi]])
                    i = cnt["hp"] % NHP; cnt["hp"] += 1
                    self.DMA("gpsimd", f"Fhp{i}", hp[i][:, :n], hin[d * 128:(d + 1) * 128, s0:s0 + n], [], [hp_b[i]])
                    pb = 1 + (d % 2) * 4
                    for m in range(MT):
                        for si, (o, w) in enumerate(subs):
                            self.MM(ps[pb + si][:, :w], wd[wi][:, m, :], hid[:, m, o:o + w], m == 0, m == MT - 1,
                                    [wd_b[wi], hid_b], [ps_b[pb + si]])
                    oi = cnt["ob"] % 2; cnt["ob"] += 1
                    for si, (o, w) in enumerate(subs):
                        self.STT("vector", ob[oi][:, o:o + w], ps[pb + si][:, :w], 0.5, hp[i][:, o:o + w], ALU.mult, ALU.add,
                                 [ps_b[pb + si], hp_b[i]], [ob_b[oi]])
                    self.DMA("gpsimd", f"Fob{oi}", hout[d * 128:(d + 1) * 128, s0:s0 + n], ob[oi][:, :n], [ob_b[oi]], [])

            norm(0)
            for bi in range(len(blocks)):
                gateup(bi)
                if bi + 1 < len(blocks):
                    norm(bi + 1)
                down(bi)
            P.barrier()

    def load_params(self, ins):
        nc, P, st, cfg = self.nc, self.P, self.st, self.cfg
        c, cb = self.c, self.cb
        KT, ST, CT = cfg.KT, cfg.ST, cfg.CT
        names = [("ffn1_norm", KT), ("mix_norm", KT), ("ffn2_norm", KT), ("post_norm", KT),
                 ("ssm_b_glu", ST), ("ssm_d", ST), ("conv_b_dw", CT), ("conv_ln_g", CT), ("conv_ln_b", CT),
                 ("conv_b_pw", CT)]
        R = sum(n for _, n in names)
        assert R <= 128
        self.pvcol = {}
        r = 0
        for nm, n in names:
            self.pvcol[nm] = r
            r += n
        self.pv, self.wdwT, self.qkg = [], [], []
        self.pv_b = P.buf("pv")
        ps, ps_b = self.ps, self.ps_b
        for l in range(cfg.depth):
            rows = self.sb(st, f"pvrows{l}", [128, 128], F32)
            rows_b = P.buf()
            self.MS("vector", rows[:, :], 0.0, [rows_b])
            for nm, n in names:
                src = ins[nm][l]
                if nm == "ssm_d":
                    src = src.rearrange("g h -> (g h)")
                r0 = self.pvcol[nm]
                self.DMA("sync", f"pvl", rows[r0:r0 + n, :], src.rearrange("(n p) -> n p", p=128), [], [rows_b])
            self.TR(ps[0][:, :128], rows[:, :], c["identf"][:, :], [rows_b, cb], [ps_b[0]])
            pv = self.sb(st, f"pv{l}", [128, 128], F32)
            self.CP("vector", pv[:, :], ps[0][:, :128], [ps_b[0]], [self.pv_b])
            self.pv.append(pv)
            wrows = self.sb(st, f"wdwrows{l}", [32, cfg.CW], F32)
            wrows_b = P.buf()
            self.DMA("sync", "pvw", wrows[0:31, :], ins["conv_w_dw"][l], [], [wrows_b])
            wT = self.sb(st, f"wdwT{l}", [128, CT, 32], F32)
            for ct in range(CT):
                self.TR(ps[1][:, :31], wrows[0:31, ct * 128:(ct + 1) * 128], c["identf"][0:31, 0:31], [wrows_b, cb], [ps_b[1]])
                self.CP("vector", wT[:, ct, 0:31], ps[1][:, :31], [ps_b[1]], [self.pv_b])
            self.wdwT.append(wT)
            qg = self.sb(st, f"qkg{l}", [128, 2], F32)
            for hf in range(2):
                self.DMA("sync", "pvq", qg[hf * 64:(hf + 1) * 64, 0:1], ins["q_norm"][l].rearrange("(p o) -> p o", o=1), [], [self.pv_b])
                self.DMA("sync", "pvq", qg[hf * 64:(hf + 1) * 64, 1:2], ins["k_norm"][l].rearrange("(p o) -> p o", o=1), [], [self.pv_b])
            self.TS("vector", qg[:, 0:1], qg[:, 0:1], 0.125, None, ALU.mult, None, [self.pv_b], [self.pv_b])
            self.qkg.append(qg)

    def gain(self, l, nm):
        c0 = self.pvcol[nm]
        return self.pv[l][:, c0:128]

    def cast_weights(self, ins):
        nc, cfg = self.nc, self.cfg
        self.w = []
        self.w_b = self.P.buf("wcast")
        specs = [("ffn1_w_gate", cfg.KT, cfg.DFF), ("ffn1_w_up", cfg.KT, cfg.DFF), ("ffn1_w_down", cfg.MT, cfg.D),
                 ("w_in", cfg.KT, cfg.IN), ("ssm_w_glu", cfg.ST, cfg.SW), ("conv_w_pw", cfg.CT, cfg.CW),
                 ("w_out", cfg.KT, cfg.D),
                 ("ffn2_w_gate", cfg.KT, cfg.DFF), ("ffn2_w_up", cfg.KT, cfg.DFF), ("ffn2_w_down", cfg.MT, cfg.D)]
        self.w_b0 = self.P.buf("wcast0")
        for l in range(cfg.depth):
            d = {}
            for si_, (nm, kt, n) in enumerate(specs):
                t = nc.dram_tensor(f"s_{nm}{l}", [128, kt, n], BF16).ap()
                d[nm] = t
                first = (l == 0 and si_ < 3)
                for k in range(kt):
                    self.DMA("gpsimd", "wcast0" if first else "wcast", t[:, k, :], ins[nm][l, k * 128:(k + 1) * 128, :], [],
                             [self.w_b0 if first else self.w_b])
            self.w.append(d)

    def transpose_in(self, x0, hT):
        nc, P, cfg = self.nc, self.P, self.cfg
        ps, ps_b, c, cb = self.ps, self.ps_b, self.c, self.cb
        KT = cfg.KT
        with ExitStack() as st:
            xin = [self.sb(st, f"xin{i}", [128, cfg.D], F32) for i in range(2)]
            xo = [self.sb(st, f"xo{i}", [128, KT, 512], F32) for i in range(2)]
            xin_b, xo_b = P.bufs(2), P.bufs(2)
            nb = 0
            for (s0, n) in cfg.blocks(512):
                oi = nb % 2; nb += 1
                for tt in range(n // 128):
                    i = (s0 // 128 + tt) % 2
                    self.DMA("sync", f"xin{i}", xin[i][:, :], x0[s0 + tt * 128:s0 + (tt + 1) * 128, :], [], [xin_b[i]])
                    for k in range(KT):
                        bank = k % 8
                        self.TR(ps[bank][:, :128], xin[i][:, k * 128:(k + 1) * 128], c["identf"][:, :], [xin_b[i], cb], [ps_b[bank]])
                        self.CP("vector" if k % 2 == 0 else "scalar", xo[oi][:, k, tt * 128:(tt + 1) * 128], ps[bank][:, :128],
                                [ps_b[bank]], [xo_b[oi]])
                self.DMA("sync", f"xo{oi}", hT[:, s0:s0 + n].rearrange("(k p) t -> p k t", p=128), xo[oi][:, :, :n], [xo_b[oi]], [])
            P.barrier(exclude=("wcast", "wcast0"))

    def post_norm(self, hin, gain, hout=None, out_tok=None):
        nc, P, cfg = self.nc, self.P, self.cfg
        ps, ps_b, c, cb = self.ps, self.ps_b, self.c, self.cb
        KT = cfg.KT
        T = 512
        with ExitStack() as st:
            rstd = self.sb(st, "pn_rstd", [128, T], F32)
            hp = [self.sb(st, f"pn_hp{i}", [128, T], F32) for i in range(3)]
            sq = [self.sb(st, f"pn_sq{i}", [128, T], BF16) for i in range(2)]
            yb = [self.sb(st, f"pn_y{i}", [128, T], F32) for i in range(3)]
            ot = [self.sb(st, f"pn_ot{i}", [128, cfg.D], F32) for i in range(4)]
            rstd_b = P.buf()
            hp_b, sq_b, yb_b, ot_b = P.bufs(3), P.bufs(2), P.bufs(3), P.bufs(4)
            cnt = dict(hp=0, sq=0, y=0)
            for (s0, n) in cfg.blocks(T):
                subs = subtiles(n)
                for k in range(KT):
                    i = cnt["hp"] % 3; cnt["hp"] += 1
                    self.DMA("sync", f"pnhp{i}", hp[i][:, :n], hin[k * 128:(k + 1) * 128, s0:s0 + n], [], [hp_b[i]])
                    j = cnt["sq"] % 2; cnt["sq"] += 1
                    self.ACT(sq[j][:, :n], hp[i][:, :n], AF.Square, [hp_b[i]], [sq_b[j]])
                    self.MM(ps[0][:, :n], c["onesD"][:, :], sq[j][:, :n], k == 0, k == KT - 1, [sq_b[j], cb], [ps_b[0]])
                self.RSQ(rstd[:, :n], ps[0][:, :n], 1e-6, [ps_b[0]], [rstd_b])
                ntt = n // 128
                for k in range(KT):
                    i = cnt["hp"] % 3; cnt["hp"] += 1
                    self.DMA("sync", f"pnhp{i}", hp[i][:, :n], hin[k * 128:(k + 1) * 128, s0:s0 + n], [], [hp_b[i]])
                    yi = cnt["y"] % 3; cnt["y"] += 1
                    self.STT("vector", yb[yi][:, :n], hp[i][:, :n], gain[:, k:k + 1], rstd[:, :n], ALU.mult, ALU.mult,
                             [hp_b[i], rstd_b, self.pv_b], [yb_b[yi]])
                    if hout is not None:
                        self.DMA("gpsimd", f"pny{yi}", hout[k * 128:(k + 1) * 128, s0:s0 + n], yb[yi][:, :n], [yb_b[yi]], [])
                    if out_tok is not None:
                        for tt in range(ntt):
                            bank = 1 + (k * ntt + tt) % 7
                            self.TR(ps[bank][:, :128], yb[yi][:, tt * 128:(tt + 1) * 128], c["identf"][:, :], [yb_b[yi], cb], [ps_b[bank]])
                            self.CP("scalar" if tt % 2 == 0 else "vector", ot[tt][:, k * 128:(k + 1) * 128], ps[bank][:, :128],
                                    [ps_b[bank]], [ot_b[tt]])
                if out_tok is not None:
                    for tt in range(ntt):
                        self.DMA("gpsimd", f"pnot{tt}", out_tok[s0 + tt * 128:s0 + (tt + 1) * 128, :], ot[tt][:, :], [ot_b[tt]], [])
            P.barrier()

    def in_proj(self, l, hin, S):
        nc, P, cfg = self.nc, self.P, self.cfg
        ps, ps_b, c, cb = self.ps, self.ps_b, self.c, self.cb
        KT, AT, ST, CT, AW, SW, CW = cfg.KT, cfg.AT, cfg.ST, cfg.CT, cfg.AW, cfg.SW, cfg.CW
        T = 1024
        TM = min(T, cfg.LP)
        win = self.w[l]["w_in"]
        gain = self.gain(l, "mix_norm")
        qg = self.qkg[l]
        tag = f"ip{l}"
        with ExitStack() as st:
            xT = self.sb(st, "ip_xT", [128, KT, TM], BF16)
            rstd = self.sb(st, "ip_rstd", [128, TM], F32)
            hp = [self.sb(st, f"ip_hp{i}", [128, TM], F32) for i in range(3)]
            sq = [self.sb(st, f"ip_sq{i}", [128, TM], BF16) for i in range(2)]
            wt = [self.sb(st, f"ip_w{i}", [128, KT, 256], BF16) for i in range(3)]
            rs = [self.sb(st, f"ip_rs{i}", [128, TM], F32) for i in range(2)]
            ob = [self.sb(st, f"ip_ob{i}", [128, TM], BF16) for i in range(3)]
            ab = [self.sb(st, f"ip_ab{i}", [128, TM], F32) for i in range(2)]
            sg = [self.sb(st, f"ip_sg{i}", [128, TM], F32) for i in range(2)]
            vb = [self.sb(st, f"ip_vb{i}", [128, 256], BF16) for i in range(3)]
            xT_b, rstd_b = P.buf(), P.buf()
            hp_b, sq_b, wt_b, rs_b, ob_b, ab_b, sg_b, vb_b = (P.bufs(3), P.bufs(2), P.bufs(3), P.bufs(2), P.bufs(3),
                                                               P.bufs(2), P.bufs(2), P.bufs(3))
            cnt = dict(hp=0, sq=0, w=0, rs=0, ob=0, sg=0, vb=0, t=0)

            def load_w(col0):
                wi = cnt["w"] % 3; cnt["w"] += 1
                self.DMA("sync", f"ipw{wi}", wt[wi][:, :, :], win[:, :, col0:col0 + 256], [self.w_b], [wt_b[wi]])
                return wi

            for (s0, n) in cfg.blocks(T):
                subs = subtiles(n)
                self.norm_block(hin, [], s0, n, gain, xT, xT_b, rstd, rstd_b, hp, hp_b, sq, sq_b, cnt, tag)

                def col_tile(wi, mi):
                    pb = (cnt["t"] % 2) * 2; cnt["t"] += 1
                    for k in range(KT):
                        for si, (o, w) in enumerate(subs):
                            self.MM(ps[pb + si][:, :w], wt[wi][:, k, mi * 128:(mi + 1) * 128], xT[:, k, o:o + w],
                                    k == 0, k == KT - 1, [wt_b[wi], xT_b], [ps_b[pb + si]])
                    return pb

                def new_ob():
                    oi = cnt["ob"] % 3; cnt["ob"] += 1
                    return oi

                for which, (dst, col0) in enumerate(((S["qT"], 0), (S["kT"], AW))):
                    for ch in range(AT // 2 if AT >= 2 else 1):
                        ntile = 2 if AT >= 2 else 1
                        wi = load_w(col0 + ch * 256) if ntile == 2 else load_w(col0)
                        for mi in range(ntile):
                            tix = ch * 2 + mi
                            pb = col_tile(wi, mi)
                            j = cnt["sq"] % 2; cnt["sq"] += 1
                            for si, (o, w) in enumerate(subs):
                                self.ACT(sq[j][:, o:o + w], ps[pb + si][:, :w], AF.Square, [ps_b[pb + si]], [sq_b[j]])
                            ri = cnt["rs"] % 2; cnt["rs"] += 1
                            for si, (o, w) in enumerate(subs):
                                self.MM(ps[4 + pb + si][:, :w], c["ones64"][:, :], sq[j][:, o:o + w], True, True,
                                        [sq_b[j], cb], [ps_b[4 + pb + si]])
                                self.RSQ(rs[ri][:, o:o + w], ps[4 + pb + si][:, :w], 1e-6, [ps_b[4 + pb + si]], [rs_b[ri]])
                            oi = new_ob()
                            for si, (o, w) in enumerate(subs):
                                self.STT("vector", ob[oi][:, o:o + w], ps[pb + si][:, :w], qg[:, which:which + 1], rs[ri][:, o:o + w],
                                         ALU.mult, ALU.mult, [ps_b[pb + si], rs_b[ri], self.pv_b], [ob_b[oi]])
                            self.DMA("gpsimd", f"ipob{oi}", dst[tix * 128:(tix + 1) * 128, s0:s0 + n], ob[oi][:, :n], [ob_b[oi]], [])
                for ch in range(max(1, AW // 256)):
                    ncol = min(256, AW)
                    wi = load_w(2 * AW + ch * 256)
                    for tt in range(n // 128):
                        bank = cnt["vb"] % 4
                        for k in range(KT):
                            self.MM(ps[bank][:, :ncol], xT[:, k, tt * 128:(tt + 1) * 128], wt[wi][:, k, :ncol],
                                    k == 0, k == KT - 1, [wt_b[wi], xT_b], [ps_b[bank]])
                        vi = cnt["vb"] % 3; cnt["vb"] += 1
                        self.ACT(vb[vi][:, :ncol], ps[bank][:, :ncol], AF.Copy, [ps_b[bank]], [vb_b[vi]])
                        self.DMA("gpsimd", f"ipvb{vi}", S["v"][s0 + tt * 128:s0 + (tt + 1) * 128, ch * 256:ch * 256 + ncol],
                                 vb[vi][:, :ncol], [vb_b[vi]], [])
                for ch in range(max(1, SW // 256)):
                    ntile = min(2, ST)
                    wi = load_w(3 * AW + ch * 256)
                    for mi in range(ntile):
                        tix = ch * 2 + mi
                        pb = col_tile(wi, mi)
                        oi = new_ob()
                        for si, (o, w) in enumerate(subs):
                            self.ACT(ob[oi][:, o:o + w], ps[pb + si][:, :w], AF.Copy, [ps_b[pb + si]], [ob_b[oi]])
                        self.DMA("gpsimd", f"ipob{oi}", S["uT"][tix * 128:(tix + 1) * 128, s0:s0 + n], ob[oi][:, :n], [ob_b[oi]], [])
                for ch in range(CT // 2):
                    wa = load_w(3 * AW + SW + ch * 256)
                    for mi in range(2):
                        pb = col_tile(wa, mi)
                        for si, (o, w) in enumerate(subs):
                            self.ACT(ab[mi][:, o:o + w], ps[pb + si][:, :w], AF.Copy, [ps_b[pb + si]], [ab_b[mi]])
                    wgi = load_w(3 * AW + SW + CW + ch * 256)
                    for mi in range(2):
                        tix = ch * 2 + mi
                        pb = col_tile(wgi, mi)
                        si_ = cnt["sg"] % 2; cnt["sg"] += 1
                        for si, (o, w) in enumerate(subs):
                            self.ACT(sg[si_][:, o:o + w], ps[pb + si][:, :w], AF.Sigmoid, [ps_b[pb + si]], [sg_b[si_]])
                        oi = new_ob()
                        self.TT("vector", ob[oi][:, :n], ab[mi][:, :n], sg[si_][:, :n], ALU.mult, [ab_b[mi], sg_b[si_]], [ob_b[oi]])
                        self.DMA("gpsimd", f"ipob{oi}", S["gluT"][tix * 128:(tix + 1) * 128, s0:s0 + n], ob[oi][:, :n], [ob_b[oi]], [])
            P.barrier()

    def attention(self, l, S):
        nc, P, cfg = self.nc, self.P, self.cfg
        ps, ps_b, c, cb = self.ps, self.ps_b, self.c, self.cb
        LP, NTT, AT = cfg.LP, cfg.NTT, cfg.AT
        qblocks = cfg.blocks(512)
        with ExitStack() as st:
            kT = self.sb(st, "at_kT", [128, LP], BF16)
            qT = self.sb(st, "at_qT", [128, LP], BF16)
            vz = [self.sb(st, f"at_vz{h}", [128, NTT, 128], BF16) for h in range(2)]
            aT = self.sb(st, "at_aT", [128, LP], BF16)
            NS, DEP = 3, 2
            pq = self.pq
            eb = [self.sb(st, f"at_e{h}", [128, 2, 512], F32) for h in range(NS)]
            lb = [self.sb(st, f"at_l{h}", [128, 2, 512], BF16) for h in range(NS)]
            cum = self.sb(st, "at_cum", [128, 2, 512], BF16)
            att = [self.sb(st, f"at_a{h}", [128, 2, 512], BF16) for h in range(3)]
            kT_b, qT_b, aT_b, cum_b = P.buf(), P.buf(), P.buf(), P.buf()
            vz_b, eb_b, lb_b, att_b = P.bufs(2), P.bufs(NS), P.bufs(NS), P.bufs(3)
            for h in range(2):
                self.MS("vector", vz[h][:, :, :], 0.0, [vz_b[h]])
            for j in range(AT):
                self.DMA("sync", "at_k", kT[:, :], S["kT"][j * 128:(j + 1) * 128, :], [], [kT_b])
                self.DMA("sync", "at_q", qT[:, :], S["qT"][j * 128:(j + 1) * 128, :], [], [qT_b])
                for h in range(2):
                    for t0 in range(0, NTT, 16):
                        t1_ = min(NTT, t0 + 16)
                        self.DMA("gpsimd", f"at_v{h}", vz[h][:, t0:t1_, h * 64:(h + 1) * 64],
                                 S["v"][t0 * 128:t1_ * 128, j * 128 + h * 64:j * 128 + (h + 1) * 64].rearrange("(t p) d -> p t d", p=128),
                                 [], [vz_b[h]])
                steps = []
                for qi, (q0, w) in enumerate(qblocks):
                    kt_max = (q0 + w) // 128 - 1
                    for kt in range(kt_max, -1, -1):
                        steps.append((qi, q0, w, kt, kt == kt_max, kt == 0, kt - q0 // 128))

                def v3(t, w):
                    return t.rearrange("p (h c) -> p h c", h=2)[:, :, :w]

                def stage1(idx):
                    qi, q0, w, kt, first, last, r = steps[idx]
                    sl = idx % NS
                    pbs = [ps_b[2 * sl], ps_b[2 * sl + 1]]
                    for h in range(2):
                        hs = slice(h * 64, (h + 1) * 64)
                        self.MM(pq[sl][:, h * 512:h * 512 + w], kT[hs, kt * 128:(kt + 1) * 128], qT[hs, q0:q0 + w], True, True,
                                [kT_b, qT_b], [pbs[h]])
                    self.ACT(eb[sl][:, :, :w], v3(pq[sl], w), AF.Exp, pbs, [eb_b[sl]])
                    self.ACT(lb[sl][:, :, :w], eb[sl][:, :, :w], AF.Ln, [eb_b[sl]], [lb_b[sl]], bias=1.0)
                    if r >= 0:
                        self.TT("gpsimd", lb[sl][:, :, :w], lb[sl][:, :, :w], c["keep"][r][:, :w].unsqueeze(1).to_broadcast([128, 2, w]),
                                ALU.mult, [lb_b[sl], cb], [lb_b[sl]])

                def stage2(idx):
                    qi, q0, w, kt, first, last, r = steps[idx]
                    sl = idx % NS
                    ai = idx % 3
                    pbs = [ps_b[2 * sl], ps_b[2 * sl + 1]]
                    for h in range(2):
                        terms = [(c["tneg"][:, :], lb[sl][:, h, :w], [lb_b[sl], cb, eb_b[sl]])]
                        if not first:
                            terms.append((c["negones"][:, :], cum[:, h, :w], [cum_b, cb]))
                        if r >= 0:
                            terms.append((c["identb"][:, :], c["negm"][r][:, :w], [cb]))
                        for ti, (lt, rh, rd) in enumerate(terms):
                            self.MM(pq[sl][:, h * 512:h * 512 + w], lt, rh, False, ti == len(terms) - 1, rd, [pbs[h]])
                    self.ACT(att[ai][:, :, :w], v3(pq[sl], w), AF.Exp, pbs, [att_b[ai]])
                    if kt > 0:
                        if first:
                            self.CP("vector", cum[:, :, :w], lb[sl][:, :, :w], [lb_b[sl]], [cum_b])
                        else:
                            self.TT("vector", cum[:, :, :w], cum[:, :, :w], lb[sl][:, :, :w], ALU.add, [cum_b, lb_b[sl]], [cum_b])

                def stage2b(idx):
                    qi, q0, w, kt, first, last, r = steps[idx]
                    ai = idx % 3
                    O = 6 + qi % 2
                    for h in range(2):
                        self.MM(ps[O][:, :w], vz[h][:, kt, :], att[ai][:, h, :w], first and h == 0, last and h == 1,
                                [vz_b[h], att_b[ai]], [ps_b[O]])
                    if last:
                        self.CP("vector", aT[:, q0:q0 + w], ps[O][:, :w], [ps_b[O]], [aT_b])

                for idx in range(len(steps) + DEP + 1):
                    if idx < len(steps):
                        stage1(idx)
                    if 0 <= idx - DEP < len(steps):
                        stage2(idx - DEP)
                    if 0 <= idx - DEP - 1 < len(steps):
                        stage2b(idx - DEP - 1)
                self.DMA("gpsimd", "at_o", S["mixT"][j * 128:(j + 1) * 128, :], aT[:, :], [aT_b], [])
            P.barrier()

    def ssm(self, l, ins, S):
        nc, P, cfg = self.nc, self.P, self.cfg
        ps, ps_b, c, cb = self.ps, self.ps_b, self.c, self.cb
        G, ST, SW, AW, LP = cfg.G, cfg.ST, cfg.SW, cfg.AW, cfg.LP
        T = 512
        V = "vector"
        with ExitStack() as st:
            pb_ = P.buf("ssmprep")
            W, R = [pb_], [pb_, cb]

            def t32(name, shape):
                return self.sb(st, f"ss_{name}", shape, F32)
            lnat = t32("lnat", [G, 256])
            for hf in range(2):
                self.DMA("sync", "ss_p", lnat[:, hf * 64:(hf + 1) * 64], ins["ssm_lambda_re"][l], [], W)
                self.DMA("sync", "ss_p", lnat[:, 128 + hf * 64:128 + (hf + 1) * 64], ins["ssm_lambda_im"][l], [], W)
            lre, lim = t32("lre", [128, G]), t32("lim", [128, G])
            self.TR(ps[0][:, :G], lnat[:, 0:128], c["identf"][0:G, 0:G], R, [ps_b[0]])
            self.TS(V, lre[:, :], ps[0][:, :G], -1e-4, None, ALU.min, None, [ps_b[0]], W)
            self.TR(ps[1][:, :G], lnat[:, 128:256], c["identf"][0:G, 0:G], R, [ps_b[1]])
            self.CP(V, lim[:, :], ps[1][:, :G], [ps_b[1]], W)
            step = t32("step", [128, G])
            self.DMA("sync", "ss_p", step[:, :], ins["ssm_log_step"][l].partition_broadcast(128), [], W)
            self.ACT(step[:, :], step[:, :], AF.Exp, R, W)
            th, thr, rr, tmp, m2 = t32("th", [128, G]), t32("thr", [128, G]), t32("rr", [128, G]), t32("tmp", [128, G]), t32("m2", [128, G])
            pti = self.sb(st, "ss_pti", [128, G], mybir.dt.int32)
            ptf = t32("ptf", [128, G])
            self.TT(V, th[:, :], lim[:, :], step[:, :], ALU.mult, R, W)
            self.TS(V, th[:, :], th[:, :], 1.0 / TWO_PI, None, ALU.mult, None, R, W)
            self.FRAC(thr[:, :], th[:, :], pti[:, :], ptf[:, :], R, W)
            self.TT(V, tmp[:, :], lre[:, :], step[:, :], ALU.mult, R, W)
            self.ACT(rr[:, :], tmp[:, :], AF.Exp, R, W)
            sn, cs = t32("sn", [128, G]), t32("cs", [128, G])
            self.ACT(sn[:, :], thr[:, :], AF.Sin, R, W, scale=SC2PI)
            self.TS(V, m2[:, :], thr[:, :], 0.25, None, ALU.add, None, R, W)
            self.FRAC(m2[:, :], m2[:, :], pti[:, :], ptf[:, :], R, W)
            self.ACT(cs[:, :], m2[:, :], AF.Sin, R, W, scale=SC2PI)
            a, b, den, wre, wim, t2 = (t32("a", [128, G]), t32("b", [128, G]), t32("den", [128, G]), t32("wre", [128, G]),
                                       t32("wim", [128, G]), t32("t2", [128, G]))
            self.TT(V, a[:, :], rr[:, :], cs[:, :], ALU.mult, R, W)
            self.TS(V, a[:, :], a[:, :], -1.0, None, ALU.add, None, R, W)
            self.TT(V, b[:, :], rr[:, :], sn[:, :], ALU.mult, R, W)
            self.TT(V, den[:, :], lre[:, :], lre[:, :], ALU.mult, R, W)
            self.TT(V, tmp[:, :], lim[:, :], lim[:, :], ALU.mult, R, W)
            self.TT(V, den[:, :], den[:, :], tmp[:, :], ALU.add, R, W)
            P.op(V, lambda e: e.reciprocal(out=den[:, :], in_=den[:, :]), R, W)
            self.TT(V, wre[:, :], a[:, :], lre[:, :], ALU.mult, R, W)
            self.TT(V, tmp[:, :], b[:, :], lim[:, :], ALU.mult, R, W)
            self.TT(V, wre[:, :], wre[:, :], tmp[:, :], ALU.add, R, W)
            self.TT(V, wre[:, :], wre[:, :], den[:, :], ALU.mult, R, W)
            self.TT(V, wim[:, :], b[:, :], lre[:, :], ALU.mult, R, W)
            self.TT(V, tmp[:, :], a[:, :], lim[:, :], ALU.mult, R, W)
            self.TT(V, wim[:, :], wim[:, :], tmp[:, :], ALU.subtract, R, W)
            self.TT(V, wim[:, :], wim[:, :], den[:, :], ALU.mult, R, W)
            bre, bim = t32("bre", [128, G, 16]), t32("bim", [128, G, 16])
            for hf in range(2):
                self.DMA("sync", "ss_p", bre[hf * 64:(hf + 1) * 64, :, :], ins["ssm_b_re"][l].rearrange("g p h -> p g h"), [], W)
                self.DMA("sync", "ss_p", bim[hf * 64:(hf + 1) * 64, :, :], ins["ssm_b_im"][l].rearrange("g p h -> p g h"), [], W)
            BB, t3, t4 = t32("BB", [128, G, 16]), t32("t3", [128, G, 16]), t32("t4", [128, G, 16])
            wre_b = wre[:, :].unsqueeze(2).to_broadcast([128, G, 16])
            wim_b = wim[:, :].unsqueeze(2).to_broadcast([128, G, 16])
            self.TT(V, t3[:, :, :], bre[:, :, :], wre_b, ALU.mult, R, W)
            self.TT(V, t4[:, :, :], bim[:, :, :], wim_b, ALU.mult, R, W)
            self.TT(V, BB[0:64, :, :], t3[0:64, :, :], t4[0:64, :, :], ALU.subtract, R, W)
            self.TT(V, t3[:, :, :], bim[:, :, :], wre_b, ALU.mult, R, W)
            self.TT(V, t4[:, :, :], bre[:, :, :], wim_b, ALU.mult, R, W)
            self.TT(V, BB[64:128, :, :], t3[64:128, :, :], t4[64:128, :, :], ALU.add, R, W)
            rmask = t32("rmask", [128, 8])
            self.MS(V, rmask[:, :], 1.0, W)
            P.op("gpsimd", lambda e: e.affine_select(out=rmask[:, :], in_=rmask[:, :], pattern=[[-16, 8]], compare_op=ALU.is_ge,
                                                     fill=0.0, base=0, channel_multiplier=1), R, W)
            P.op("gpsimd", lambda e: e.affine_select(out=rmask[:, :], in_=rmask[:, :], pattern=[[16, 8]], compare_op=ALU.is_ge,
                                                     fill=0.0, base=15, channel_multiplier=-1), R, W)
            lhsB = self.sb(st, "ss_lhsB", [128, G, 128], BF16)
            lhsBs = self.sb(st, "ss_lhsBs", [128, G, 128], BF16)
            lhsC1 = self.sb(st, "ss_lhsC1", [128, G, 128], BF16)
            lhsC2 = self.sb(st, "ss_lhsC2", [128, G, 128], BF16)
            BT = t32("BT", [128, 128])
            for j in range(ST):
                self.TR(ps[2][:, :128], BB[:, 8 * j:8 * j + 8, :].rearrange("p g h -> p (g h)"), c["identf"][:, :], R, [ps_b[2]])
                self.CP(V, BT[:, :], ps[2][:, :128], [ps_b[2]], W)
                for gg in range(8):
                    g = 8 * j + gg
                    self.TS(V, lhsB[:, g, :], BT[:, :], rmask[:, gg:gg + 1], None, ALU.mult, None, R, W)
                    self.CP(V, lhsBs[:, g, 0:64], lhsB[:, g, 64:128], R, W)
                    self.TS(V, lhsBs[:, g, 64:128], lhsB[:, g, 0:64], -1.0, None, ALU.mult, None, R, W)
            cn1, cn2 = t32("cn1", [128, ST, 128]), t32("cn2", [128, ST, 128])
            cre = ins["ssm_c_re"][l].rearrange("(j gg) h p -> (gg h) j p", gg=8)
            cim = ins["ssm_c_im"][l].rearrange("(j gg) h p -> (gg h) j p", gg=8)
            self.DMA("sync", "ss_p", cn1[:, :, 0:64], cre, [], W)
            self.DMA("sync", "ss_p", cn1[:, :, 64:128], cim, [], W)
            self.DMA("sync", "ss_p", cn2[:, :, 0:64], cim, [], W)
            self.DMA("sync", "ss_p", cn2[:, :, 64:128], cre, [], W)
            self.MS(V, lhsC1[:, :, :], 0.0, W)
            self.MS(V, lhsC2[:, :, :], 0.0, W)
            CT1, CT2 = t32("CT1", [128, 128]), t32("CT2", [128, 128])
            for j in range(ST):
                self.TR(ps[3][:, :128], cn1[:, j, :], c["identf"][:, :], R, [ps_b[3]])
                self.CP(V, CT1[0:64, :], ps[3][0:64, :128], [ps_b[3]], W)
                self.TS(V, CT1[64:128, :], ps[3][64:128, :128], -1.0, None, ALU.mult, None, [ps_b[3]], W)
                self.TR(ps[2][:, :128], cn2[:, j, :], c["identf"][:, :], R, [ps_b[2]])
                self.TS(V, CT2[:, :], ps[2][:, :128], -1.0, None, ALU.mult, None, [ps_b[2]], W)
                for gg in range(8):
                    g = 8 * j + gg
                    self.CP(V, lhsC1[:, g, 16 * gg:16 * gg + 16], CT1[:, 16 * gg:16 * gg + 16], R, W)
                    self.CP(V, lhsC2[:, g, 16 * gg:16 * gg + 16], CT2[:, 16 * gg:16 * gg + 16], R, W)
            wglu = self.sb(st, "ss_wglu", [128, ST, SW], BF16)
            self.DMA("sync", "ss_p", wglu[:, :, :], self.w[l]["ssm_w_glu"], [self.w_b], W)
            tt_all = t32("tt", [128, LP])
            P.op("gpsimd", lambda e: e.iota(tt_all[:, :], pattern=[[1, LP]], base=0, channel_multiplier=0,
                                            allow_small_or_imprecise_dtypes=True), [], W)
            vlast = t32("vlast", [128, G])
            self.MS(V, vlast[:, :], 0.0, W)
            uT = [self.sb(st, f"ss_u{i}", [128, ST, T], BF16) for i in range(2)]
            m1 = [t32(f"m1{i}", [128, T]) for i in range(2)]
            m2t = [t32(f"m2{i}", [128, T]) for i in range(2)]
            fi1 = [self.sb(st, f"ss_fi1{i}", [128, T], mybir.dt.int32) for i in range(2)]
            ff1 = [t32(f"ff1{i}", [128, T]) for i in range(2)]
            fi_b, ff_b = P.bufs(2), P.bufs(2)
            snT = [t32(f"snT{i}", [128, T]) for i in range(2)]
            csT = [t32(f"csT{i}", [128, T]) for i in range(2)]
            z1 = [t32(f"z1{i}", [128, T]) for i in range(2)]
            z2 = [t32(f"z2{i}", [128, T]) for i in range(2)]
            vv = [t32(f"vv{i}", [128, T]) for i in range(2)]
            Ab = [self.sb(st, f"ss_A{i}", [128, T], BF16) for i in range(2)]
            Bb = [self.sb(st, f"ss_B{i}", [128, T], BF16) for i in range(2)]
            yf = t32("yf", [128, T])
            yg = self.sb(st, "ss_yg", [128, ST, T], BF16)
            sgl = t32("sgl", [128, T])
            so = [self.sb(st, f"ss_so{i}", [128, T], BF16) for i in range(2)]
            uT_b, m1_b, m2_b, sn_b, cs_b, z1_b, z2_b, vv_b, Ab_b, Bb_b, so_b = [P.bufs(2) for _ in range(11)]
            yf_b, yg_b, sgl_b, vl_b = P.buf(), P.buf(), P.buf(), P.buf()
            dcol = self.pvcol["ssm_d"]
            bgcol = self.pvcol["ssm_b_glu"]
            blocks = cfg.blocks(T)
            gsteps = [(bi, j, gg) for bi in range(len(blocks)) for j in range(ST) for gg in range(8)]

            def stageT(k):
                bi, j, gg = gsteps[k]
                s0, n = blocks[bi]
                g = 8 * j + gg
                i = k % 2
                if j == 0 and gg == 0:
                    ui = bi % 2
                    self.DMA("sync", f"ss_u{ui}", uT[ui][:, :, :n], S["uT"][:, s0:s0 + n].rearrange("(k p) t -> p k t", p=128), [], [uT_b[ui]])
                self.ACT(m1[i][:, :n], tt_all[:, s0:s0 + n], AF.Copy, [pb_], [m1_b[i]], scale=thr[:, g:g + 1])
                self.CP("scalar", fi1[i][:, :n], m1[i][:, :n], [m1_b[i]], [fi_b[i]])
                self.CP("gpsimd", ff1[i][:, :n], fi1[i][:, :n], [fi_b[i]], [ff_b[i]])
                self.TT("gpsimd", m1[i][:, :n], m1[i][:, :n], ff1[i][:, :n], ALU.subtract, [m1_b[i], ff_b[i]], [m1_b[i]])
                self.ACT(snT[i][:, :n], m1[i][:, :n], AF.Sin, [m1_b[i]], [sn_b[i]], scale=SC2PI)
                self.ACT(m2t[i][:, :n], m1[i][:, :n], AF.Abs, [m1_b[i]], [m2_b[i]])
                self.ACT(csT[i][:, :n], m2t[i][:, :n], AF.Sin, [m2_b[i]], [cs_b[i]], bias=self.hpib[:, 0:1], scale=-SC2PI)

            def stageM(k):
                bi, j, gg = gsteps[k]
                s0, n = blocks[bi]
                g = 8 * j + gg
                i = k % 2
                ui = bi % 2
                Y = 4 + (j % 2)
                A_, B_ = 2 * i, 2 * i + 1
                self.MM(ps[A_][:, :n], lhsB[:, g, :], uT[ui][:, j, :n], True, True, [pb_, uT_b[ui]], [ps_b[A_]])
                self.MM(ps[B_][:, :n], lhsBs[:, g, :], uT[ui][:, j, :n], True, True, [pb_, uT_b[ui]], [ps_b[B_]])
                self.TT(V, z1[i][:, :n], ps[A_][:, :n], csT[i][:, :n], ALU.mult, [ps_b[A_], cs_b[i]], [z1_b[i]])
                self.TT(V, z2[i][:, :n], ps[B_][:, :n], snT[i][:, :n], ALU.mult, [ps_b[B_], sn_b[i]], [z2_b[i]])
                self.TT(V, z1[i][:, :n], z1[i][:, :n], z2[i][:, :n], ALU.add, [z1_b[i], z2_b[i]], [z1_b[i]])
                rb = rr[:, g:g + 1].to_broadcast([128, n])
                P.op(V, lambda e: e.tensor_tensor_scan(
                    out=vv[i][:, :n], data0=rb, data1=z1[i][:, :n], initial=vlast[:, g:g + 1], op0=ALU.mult, op1=ALU.add),
                    [z1_b[i], vl_b, pb_], [vv_b[i]])
                self.CP(V, vlast[:, g:g + 1], vv[i][:, n - 1:n], [vv_b[i]], [vl_b])
                self.TT(V, Ab[i][:, :n], vv[i][:, :n], csT[i][:, :n], ALU.mult, [vv_b[i], cs_b[i]], [Ab_b[i]])
                self.TT("gpsimd", Bb[i][:, :n], vv[i][:, :n], snT[i][:, :n], ALU.mult, [vv_b[i], sn_b[i]], [Bb_b[i]])
                self.MM(ps[Y][:, :n], lhsC1[:, g, :], Ab[i][:, :n], gg == 0, False, [pb_, Ab_b[i]], [ps_b[Y]])
                self.MM(ps[Y][:, :n], lhsC2[:, g, :], Bb[i][:, :n], False, gg == 7, [pb_, Bb_b[i]], [ps_b[Y]])
                if gg == 7:
                    self.STT(V, yf[:, :n], uT[ui][:, j, :n], self.pv[l][:, dcol + j:dcol + j + 1], ps[Y][:, :n], ALU.mult, ALU.add,
                             [uT_b[ui], ps_b[Y], self.pv_b], [yf_b])
                    self.ACT(yg[:, j, :n], yf[:, :n], AF.Gelu, [yf_b], [yg_b])
                    if j == ST - 1:
                        for cc in range(ST):
                            Gk = 6 + (cc % 2)
                            for k2 in range(ST):
                                self.MM(ps[Gk][:, :n], wglu[:, k2, cc * 128:(cc + 1) * 128], yg[:, k2, :n], k2 == 0, k2 == ST - 1,
                                        [pb_, yg_b], [ps_b[Gk]])
                            self.ACT(sgl[:, :n], ps[Gk][:, :n], AF.Sigmoid, [ps_b[Gk], self.pv_b], [sgl_b],
                                     bias=self.pv[l][:, bgcol + cc:bgcol + cc + 1])
                            oi = (bi * ST + cc) % 2
                            self.TT(V, so[oi][:, :n], yg[:, cc, :n], sgl[:, :n], ALU.mult, [yg_b, sgl_b], [so_b[oi]])
                            self.DMA("gpsimd", f"ss_so{oi}", S["mixT"][AW + cc * 128:AW + (cc + 1) * 128, s0:s0 + n], so[oi][:, :n],
                                     [so_b[oi]], [])

            stageT(0)
            for k in range(len(gsteps)):
                if k + 1 < len(gsteps):
                    stageT(k + 1)
                stageM(k)
            P.barrier()

    def conv(self, l, S):
        nc, P, cfg = self.nc, self.P, self.cfg
        ps, ps_b, c, cb = self.ps, self.ps_b, self.c, self.cb
        CT, CW, AW, SW = cfg.CT, cfg.CW, cfg.AW, cfg.SW
        T = 512
        V = "vector"
        pv = self.pv[l]
        col = self.pvcol
        with ExitStack() as st:
            Dg = self.sb(st, "cv_Dg", [128, CT * 31, 128], BF16)
            Dg_b = P.buf()
            for ct in range(CT):
                for j in range(31):
                    self.TS(V, Dg[:, ct * 31 + j, :], c["identf"][:, :], self.wdwT[l][:, ct, j:j + 1], None,
                            ALU.mult, None, [cb, self.pv_b], [Dg_b])
            wpw = self.sb(st, "cv_wpw", [128, CT, CW], BF16)
            wpw_b = P.buf()
            self.DMA("sync", "cv_w", wpw[:, :, :], self.w[l]["conv_w_pw"], [self.w_b], [wpw_b])
            gp = [self.sb(st, f"cv_gp{i}", [128, CT, 30 + T], BF16) for i in range(2)]
            hcf = self.sb(st, "cv_hcf", [128, CT, T], F32)
            hcb = self.sb(st, "cv_hcb", [128, CT, T], BF16)
            sqb = [self.sb(st, f"cv_sq{i}", [128, T], BF16) for i in range(2)]
            meanS = self.sb(st, "cv_mean", [128, T], F32)
            m2 = self.sb(st, "cv_m2", [128, T], F32)
            rsS = self.sb(st, "cv_rs", [128, T], F32)
            t1 = [self.sb(st, f"cv_t1{i}", [128, T], F32) for i in range(2)]
            hsb = self.sb(st, "cv_hs", [128, CT, T], BF16)
            ob = [self.sb(st, f"cv_ob{i}", [128, T], BF16) for i in range(2)]
            gp_b, sqb_b, t1_b, ob_b = P.bufs(2), P.bufs(2), P.bufs(2), P.bufs(2)
            hcf_b, hcb_b, mean_b, m2_b, rs_b, hs_b = P.buf(), P.buf(), P.buf(), P.buf(), P.buf(), P.buf()
            cnt = dict(sq=0, t1=0, ob=0)
            for bi, (s0, n) in enumerate(cfg.blocks(T)):
                gi = bi % 2
                if s0 == 0:
                    self.MS(V, gp[gi][:, :, 0:30], 0.0, [gp_b[gi]])
                    self.DMA("sync", f"cv_gp{gi}", gp[gi][:, :, 30:30 + n], S["gluT"][:, 0:n].rearrange("(k p) t -> p k t", p=128), [], [gp_b[gi]])
                else:
                    self.DMA("sync", f"cv_gp{gi}", gp[gi][:, :, 0:30 + n], S["gluT"][:, s0 - 30:s0 + n].rearrange("(k p) t -> p k t", p=128),
                             [], [gp_b[gi]])
                for ct in range(CT):
                    bank = ct % 4
                    for j in range(31):
                        self.MM(ps[bank][:, :n], Dg[:, ct * 31 + j, :], gp[gi][:, ct, j:j + n], j == 0, j == 30, [Dg_b, gp_b[gi]], [ps_b[bank]])
                    self.TS(V, hcf[:, ct, :n], ps[bank][:, :n], pv[:, col["conv_b_dw"] + ct:col["conv_b_dw"] + ct + 1], None, ALU.add, None,
                            [ps_b[bank], self.pv_b], [hcf_b])
                    self.CP("gpsimd", hcb[:, ct, :n], hcf[:, ct, :n], [hcf_b], [hcb_b])
                    qi = cnt["sq"] % 2; cnt["sq"] += 1
                    self.ACT(sqb[qi][:, :n], hcf[:, ct, :n], AF.Square, [hcf_b], [sqb_b[qi]])
                    self.MM(ps[4][:, :n], c["onesCW"][:, :], hcb[:, ct, :n], ct == 0, ct == CT - 1, [hcb_b, cb], [ps_b[4]])
                    self.MM(ps[5][:, :n], c["onesCW"][:, :], sqb[qi][:, :n], ct == 0, ct == CT - 1, [sqb_b[qi], cb], [ps_b[5]])
                self.ACT(meanS[:, :n], ps[4][:, :n], AF.Copy, [ps_b[4]], [mean_b])
                self.TT("gpsimd", m2[:, :n], meanS[:, :n], meanS[:, :n], ALU.mult, [mean_b], [m2_b])
                self.TT(V, m2[:, :n], ps[5][:, :n], m2[:, :n], ALU.subtract, [ps_b[5], m2_b], [m2_b])
                self.RSQ(rsS[:, :n], m2[:, :n], 1e-5, [m2_b], [rs_b])
                for ct in range(CT):
                    ti = cnt["t1"] % 2; cnt["t1"] += 1
                    self.TT(V, t1[ti][:, :n], hcf[:, ct, :n], meanS[:, :n], ALU.subtract, [hcf_b, mean_b], [t1_b[ti]])
                    self.TT("gpsimd", t1[ti][:, :n], t1[ti][:, :n], rsS[:, :n], ALU.mult, [t1_b[ti], rs_b], [t1_b[ti]])
                    self.ACT(hsb[:, ct, :n], t1[ti][:, :n], AF.Silu, [t1_b[ti], self.pv_b], [hs_b],
                             bias=pv[:, col["conv_ln_b"] + ct:col["conv_ln_b"] + ct + 1],
                             scale=pv[:, col["conv_ln_g"] + ct:col["conv_ln_g"] + ct + 1])
                if self.debug == 2 and bi == 0 and l == 0:
                    for nm_, t_, b_, dt_ in (("hcf", hcf, hcf_b, F32), ("hsb", hsb, hs_b, BF16)):
                        dbg = nc.dram_tensor(f"dbg_{nm_}", [128, CT, T], dt_, kind="ExternalOutput").ap()
                        self.DMA("sync", "dbg", dbg[:, :, :], t_[:, :, :], [b_], [])
                    for nm_, t_, b_ in (("mean", meanS, mean_b), ("rs", rsS, rs_b)):
                        dbg = nc.dram_tensor(f"dbg_{nm_}", [128, T], F32, kind="ExternalOutput").ap()
                        self.DMA("sync", "dbg", dbg[:, :], t_[:, :], [b_], [])
                for cc in range(CT):
                    bank = 6 + (cc % 2)
                    for ct in range(CT):
                        self.MM(ps[bank][:, :n], wpw[:, ct, cc * 128:(cc + 1) * 128], hsb[:, ct, :n], ct == 0, ct == CT - 1,
                                [wpw_b, hs_b], [ps_b[bank]])
                    oi = cnt["ob"] % 2; cnt["ob"] += 1
                    self.TS(V, ob[oi][:, :n], ps[bank][:, :n], pv[:, col["conv_b_pw"] + cc:col["conv_b_pw"] + cc + 1], None, ALU.add, None,
                            [ps_b[bank], self.pv_b], [ob_b[oi]])
                    self.DMA("gpsimd", f"cv_ob{oi}", S["mixT"][AW + SW + cc * 128:AW + SW + (cc + 1) * 128, s0:s0 + n], ob[oi][:, :n],
                             [ob_b[oi]], [])
            P.barrier()

    def out_proj(self, l, hin, hout, S):
        nc, P, cfg = self.nc, self.P, self.cfg
        ps, ps_b = self.ps, self.ps_b
        KT = cfg.KT
        T = 1024
        TM = min(T, cfg.LP)
        wout = self.w[l]["w_out"]
        with ExitStack() as st:
            xT = [self.sb(st, f"op_x{i}", [128, KT, TM], BF16) for i in range(2)]
            wt = [self.sb(st, f"op_w{i}", [128, KT, 128], BF16) for i in range(3)]
            hp = [self.sb(st, f"op_hp{i}", [128, TM], F32) for i in range(3)]
            ob = [self.sb(st, f"op_ob{i}", [128, TM], F32) for i in range(2)]
            xT_b, wt_b, hp_b, ob_b = P.bufs(2), P.bufs(3), P.bufs(3), P.bufs(2)
            cnt = 0
            for bi, (s0, n) in enumerate(cfg.blocks(T)):
                subs = subtiles(n)
                xi = bi % 2
                self.DMA("sync", f"op_x{xi}", xT[xi][:, :, :n], S["mixT"][:, s0:s0 + n].rearrange("(k p) t -> p k t", p=128), [], [xT_b[xi]])
                for d in range(KT):
                    wi = cnt % 3
                    i = cnt % 3
                    oi = cnt % 2
                    cnt += 1
                    self.DMA("sync", f"op_w{wi}", wt[wi][:, :, :], wout[:, :, d * 128:(d + 1) * 128], [self.w_b], [wt_b[wi]])
                    self.DMA("sync", f"op_hp{i}", hp[i][:, :n], hin[d * 128:(d + 1) * 128, s0:s0 + n], [], [hp_b[i]])
                    pb = (d % 2) * 2
                    for k in range(KT):
                        for si, (o, w) in enumerate(subs):
                            self.MM(ps[pb + si][:, :w], wt[wi][:, k, :], xT[xi][:, k, o:o + w], k == 0, k == KT - 1,
                                    [wt_b[wi], xT_b[xi]], [ps_b[pb + si]])
                    for si, (o, w) in enumerate(subs):
                        self.TT("vector", ob[oi][:, o:o + w], ps[pb + si][:, :w], hp[i][:, o:o + w], ALU.add,
                                [ps_b[pb + si], hp_b[i]], [ob_b[oi]])
                    self.DMA("gpsimd", f"op_ob{oi}", hout[d * 128:(d + 1) * 128, s0:s0 + n], ob[oi][:, :n], [ob_b[oi]], [])
            P.barrier()


WEIGHT_NAMES = ["ffn1_norm", "ffn1_w_gate", "ffn1_w_up", "ffn1_w_down", "mix_norm", "w_in", "q_norm", "k_norm",
                "ssm_lambda_re", "ssm_lambda_im", "ssm_log_step", "ssm_b_re", "ssm_b_im", "ssm_c_re", "ssm_c_im", "ssm_d",
                "ssm_w_glu", "ssm_b_glu", "conv_w_dw", "conv_b_dw", "conv_ln_g", "conv_ln_b", "conv_w_pw", "conv_b_pw",
                "w_out", "ffn2_norm", "ffn2_w_gate", "ffn2_w_up", "ffn2_w_down", "post_norm"]


def build(cfg, shapes, debug=False, stop_after=None):
    nc = bass.Bass("TRN2", target_bir_lowering=False)
    ins = {nm: nc.dram_tensor(nm, list(shapes[nm]), F32, kind="ExternalInput").ap() for nm in WEIGHT_NAMES}
    x0 = nc.dram_tensor("x0", [cfg.LP, cfg.D], F32, kind="ExternalInput").ap()
    out = nc.dram_tensor("out", [cfg.LP, cfg.D], F32, kind="ExternalOutput").ap()
    knd = "ExternalOutput" if debug else "Internal"
    hA = nc.dram_tensor("hA", [cfg.D, cfg.LP], F32, kind=knd).ap()
    hB = nc.dram_tensor("hB", [cfg.D, cfg.LP], F32, kind=knd).ap()
    S = {
        "qT": nc.dram_tensor("s_qT", [cfg.AW, cfg.LP], BF16, kind=knd).ap(),
        "kT": nc.dram_tensor("s_kT", [cfg.AW, cfg.LP], BF16, kind=knd).ap(),
        "v": nc.dram_tensor("s_v", [cfg.LP, cfg.AW], BF16, kind=knd).ap(),
        "uT": nc.dram_tensor("s_uT", [cfg.SW, cfg.LP], BF16, kind=knd).ap(),
        "gluT": nc.dram_tensor("s_gluT", [cfg.CW, cfg.LP], BF16, kind=knd).ap(),
        "mixT": nc.dram_tensor("s_mixT", [cfg.MIX, cfg.LP], BF16, kind=knd).ap(),
    }
    with ExitStack() as st:
        k = K(nc, cfg, st, debug)
        k.consts()
        k.load_params(ins)
        k.cast_weights(ins)
        k.transpose_in(x0, hA)
        cur, oth = hA, hB
        done = False
        for l in range(cfg.depth):
            w = k.w[l]
            k.ffn(cur, [], oth, [], w["ffn1_w_gate"], w["ffn1_w_up"], w["ffn1_w_down"], k.w_b0 if l == 0 else k.w_b,
                  k.gain(l, "ffn1_norm"), f"a{l}")
            cur, oth = oth, cur
            if stop_after == ("ffn1", l):
                break
            k.in_proj(l, cur, S)
            if stop_after == ("in_proj", l):
                break
            k.attention(l, S)
            k.ssm(l, ins, S)
            k.conv(l, S)
            if stop_after == ("mix", l):
                break
            k.out_proj(l, cur, oth, S)
            cur, oth = oth, cur
            if stop_after == ("out_proj", l):
                break
            k.ffn(cur, [], oth, [], w["ffn2_w_gate"], w["ffn2_w_up"], w["ffn2_w_down"], k.w_b, k.gain(l, "ffn2_norm"), f"b{l}")
            cur, oth = oth, cur
            if l == cfg.depth - 1:
                k.post_norm(cur, k.gain(l, "post_norm"), out_tok=out)
            else:
                k.post_norm(cur, k.gain(l, "post_norm"), hout=oth)
                cur, oth = oth, cur
        k.P.barrier()
        k.P.emit()
    return nc, k


SEQ, NMETA, BATCH = 8192, 16, 4
_CACHE = {}


def kernel(**inputs):
    cfg = Cfg()
    x = np.asarray(inputs["x"], dtype=np.float32)
    meta = np.asarray(inputs["meta"], dtype=np.float32)
    shapes = {nm: inputs[nm].shape for nm in WEIGHT_NAMES}
    if "nc" not in _CACHE:
        _CACHE["nc"] = build(cfg, shapes)[0]
    nc = _CACHE["nc"]
    wts = {nm: np.ascontiguousarray(np.asarray(inputs[nm], dtype=np.float32)) for nm in WEIGHT_NAMES}
    in_maps = []
    real = {0: 0, 1: 1, 4: 2, 5: 3}
    zeros = np.zeros((cfg.LP, cfg.D), np.float32)
    for core in range(8):
        m = dict(wts)
        if core in real:
            x0 = np.zeros((cfg.LP, cfg.D), np.float32)
            x0[:NMETA] = meta
            x0[NMETA:NMETA + SEQ] = x[real[core]]
            m["x0"] = x0
        else:
            m["x0"] = zeros
        in_maps.append(m)
    res = run_bass_kernel_spmd(nc, in_maps, core_ids=list(range(8)))
    outs = [res.results[cidx]["out"][NMETA:NMETA + SEQ] for cidx in (0, 1, 4, 5)]
    return np.stack(outs, axis=0).astype(np.float32)
```

```python
import numpy as np
from contextlib import ExitStack
import concourse.bass as bass
import concourse.mybir as mybir
from concourse.bass_utils import run_bass_kernel_spmd

F32 = mybir.dt.float32
BF16 = mybir.dt.bfloat16
ALU = mybir.AluOpType
AF = mybir.ActivationFunctionType

ENGS = ("tensor", "vector", "scalar", "gpsimd", "sync")


class Buf:
    __slots__ = ("name", "last_w", "readers")

    def __init__(self, name):
        self.name = name
        self.last_w = None
        self.readers = []


class Prog:
    def __init__(self, nc):
        self.nc = nc
        self.ops = {e: [] for e in ENGS}
        self.dma_cnt = {}
        self.nbuf = 0

    def buf(self, name=None):
        self.nbuf += 1
        return Buf(name or f"b{self.nbuf}")

    def bufs(self, n, name="b"):
        return [self.buf(f"{name}{i}") for i in range(n)]

    def _deps(self, reads, writes):
        deps = []
        for b in reads:
            if b.last_w is not None:
                deps.append(b.last_w)
        for b in writes:
            if b.last_w is not None:
                deps.append(b.last_w)
            deps.extend(b.readers)
        return deps

    def _commit(self, tok, reads, writes):
        for b in reads:
            if tok[0] == "c":
                b.readers = [t for t in b.readers if not (t[0] == "c" and t[1] == tok[1])]
            b.readers.append(tok)
        for b in writes:
            b.last_w = tok
            b.readers = []

    def op(self, eng, fn, reads=(), writes=()):
        deps = self._deps(reads, writes)
        if eng == "tensor":
            deps = [t for t in deps if not (t[0] == "c" and t[1] == "tensor")]
        idx = len(self.ops[eng])
        tok = ("c", eng, idx)
        self.ops[eng].append([fn, deps, None])
        self._commit(tok, reads, writes)
        return tok

    def dma(self, eng, sem, fn, reads=(), writes=()):
        deps = self._deps(reads, writes)
        v = self.dma_cnt.get(sem, 0) + 16
        self.dma_cnt[sem] = v
        tok = ("d", sem, v)
        self.ops[eng].append([fn, deps, sem])
        self._commit(tok, reads, writes)
        return tok

    def barrier(self, bufs=(), exclude=()):
        toks = []
        for e in ENGS:
            if self.ops[e]:
                for i in range(len(self.ops[e]) - 1, -1, -1):
                    if self.ops[e][i][2] is None and self.ops[e][i][0] is not None:
                        toks.append(("c", e, i))
                        break
        for s, v in self.dma_cnt.items():
            if s not in exclude:
                toks.append(("d", s, v))
        for e in ENGS:
            self.ops[e].append([None, list(toks), None])

    def emit(self):
        nc = self.nc
        need = {e: set() for e in ENGS}
        for e in ENGS:
            for fn, deps, sem in self.ops[e]:
                for t in deps:
                    if t[0] == "c":
                        if t[1] == e and e == "tensor":
                            continue
                        need[t[1]].add(t[2])
        ms = {e: {} for e in ENGS}
        for e in ENGS:
            for k, i in enumerate(sorted(need[e])):
                ms[e][i] = k + 1
        self.ms_max = {e: len(ms[e]) for e in ENGS}
        with ExitStack() as st:
            csem = {e: st.enter_context(nc.semaphore(f"c_{e}")) for e in ENGS}
            dsem = {s: st.enter_context(nc.semaphore(f"d_{s}")) for s in self.dma_cnt}
            block = st.enter_context(nc.Block())

            def run(e, eng):
                waited = {}
                for i, (fn, deps, sem) in enumerate(self.ops[e]):
                    for t in deps:
                        if t[0] == "c":
                            if t[1] == e and e == "tensor":
                                continue
                            key, val, h = ("c", t[1]), ms[t[1]][t[2]], csem[t[1]]
                        else:
                            key, val, h = ("d", t[1]), t[2], dsem[t[1]]
                        if waited.get(key, 0) >= val:
                            continue
                        waited[key] = val
                        eng.wait_ge(h, val)
                    if fn is None:
                        continue
                    inst = fn(eng)
                    if sem is not None:
                        inst.then_inc(dsem[sem], 16)
                    elif i in ms[e]:
                        inst.then_inc(csem[e], 1)

            @block.tensor
            def _(eng):
                run("tensor", eng)

            @block.vector
            def _(eng):
                run("vector", eng)

            @block.scalar
            def _(eng):
                run("scalar", eng)

            @block.gpsimd
            def _(eng):
                run("gpsimd", eng)

            @block.sync
            def _(eng):
                run("sync", eng)


class Cfg:
    def __init__(self, D=2048, DFF=5632, H=8, G=32, CW=1024, LP=8320, depth=2):
        self.D, self.DFF, self.H, self.G, self.CW, self.LP, self.depth = D, DFF, H, G, CW, LP, depth
        self.KT = D // 128
        self.MT = DFF // 128
        self.AW = H * 64
        self.AT = self.AW // 128
        self.SW = G * 16
        self.ST = self.SW // 128
        self.CT = CW // 128
        self.IN = 3 * self.AW + self.SW + 2 * CW
        self.MIX = self.AW + self.SW + CW
        assert self.MIX == D
        self.NTT = LP // 128

    def blocks(self, T=1024):
        out, s = [], 0
        while s < self.LP:
            n = min(T, self.LP - s)
            out.append((s, n))
            s += n
        return out


def subtiles(n, maxn=512):
    k = (n + maxn - 1) // maxn
    base = n // k
    assert base * k == n
    return [(i * base, base) for i in range(k)]


PI = 3.14159265358979
TWO_PI = 6.28318530717959
SC2PI = 6.28316
HALFPI_ = 1.57079


class K:
    def __init__(self, nc, cfg, st, debug=False):
        self.nc, self.cfg, self.st, self.debug = nc, cfg, st, debug
        self.P = Prog(nc)
        self.pq = [st.enter_context(nc.psum_tensor(f"pq{i}", [128, 1024], F32)) for i in range(4)]
        self.ps = [self.pq[i // 2][:, (i % 2) * 512:(i % 2 + 1) * 512] for i in range(8)]
        self.ps_b = self.P.bufs(8, "ps")

    def sb(self, st, name, shape, dt):
        self.uid = getattr(self, "uid", 0) + 1
        return st.enter_context(self.nc.sbuf_tensor(f"{name}_{self.uid}", shape, dt))

    def MM(self, out, lhsT, rhs, start, stop, R, W):
        self.P.op("tensor", lambda e: e.matmul(out=out, lhsT=lhsT, rhs=rhs, start=start, stop=stop), R, W)

    def TR(self, out, in_, ident, R, W):
        self.P.op("tensor", lambda e: e.transpose(out=out, in_=in_, identity=ident), R, W)

    def ACT(self, out, in_, func, R, W, bias=None, scale=None):
        kw = {}
        if bias is not None:
            kw["bias"] = bias
        if scale is not None:
            kw["scale"] = scale
        self.P.op("scalar", lambda e: e.activation(out=out, in_=in_, func=func, **kw), R, W)

    def TS(self, eng, out, in0, s1, s2, op0, op1, R, W):
        if op1 is None:
            self.P.op(eng, lambda e: e.tensor_scalar(out=out, in0=in0, scalar1=s1, scalar2=None, op0=op0), R, W)
        else:
            self.P.op(eng, lambda e: e.tensor_scalar(out=out, in0=in0, scalar1=s1, scalar2=s2, op0=op0, op1=op1), R, W)

    def RSQ(self, out, in_, eps, R, W):
        self.P.op("scalar", lambda e: e.activation(out=out, in_=in_, func=AF.Sqrt, bias=self.epsb[eps][:, 0:1]), R + [self.cb], W)
        self.P.op("vector", lambda e: e.reciprocal(out=out, in_=out), W, W)

    def FRAC(self, out, y, ti, tf, R, W, e1="gpsimd"):
        self.CP(e1, ti, y, R + W, W)
        self.CP(e1, tf, ti, W, W)
        self.TT(e1, tf, y, tf, ALU.subtract, R + W, W)
        self.STT("vector", out, tf, 0.5, tf, ALU.is_gt, ALU.subtract, W, W)
        self.STT("vector", out, out, 0.5, out, ALU.is_gt, ALU.subtract, W, W)

    def TT(self, eng, out, in0, in1, op, R, W):
        self.P.op(eng, lambda e: e.tensor_tensor(out=out, in0=in0, in1=in1, op=op), R, W)

    def STT(self, eng, out, in0, scalar, in1, op0, op1, R, W):
        self.P.op(eng, lambda e: e.scalar_tensor_tensor(out=out, in0=in0, scalar=scalar, in1=in1, op0=op0, op1=op1), R, W)

    def CP(self, eng, out, in_, R, W):
        if eng == "scalar":
            self.P.op(eng, lambda e: e.activation(out=out, in_=in_, func=AF.Copy), R, W)
        else:
            self.P.op(eng, lambda e: e.tensor_copy(out=out, in_=in_), R, W)

    def MS(self, eng, ap, val, W):
        self.P.op(eng, lambda e: e.memset(ap, val), (), W)

    def DMA(self, q, sem, out, in_, R, W, slow=False):
        if slow:
            self.P.dma(q, sem, lambda e: e.dma_start(out=out, in_=in_, allow_slow_non_contiguous=True), R, W)
        else:
            self.P.dma(q, sem, lambda e: e.dma_start(out=out, in_=in_), R, W)

    def consts(self):
        nc, P, st, cfg = self.nc, self.P, self.st, self.cfg
        c = self.c = {}
        cb = self.cb = P.buf("consts")
        W = [cb]
        self.epsb = {}
        for ev in (1e-6, 1e-5):
            t = self.sb(st, f"eps{len(self.epsb)}", [128, 1], F32)
            self.MS("vector", t[:, :], ev, W)
            self.epsb[ev] = t
        self.hpib = self.sb(st, "hpib", [128, 1], F32)
        self.MS("vector", self.hpib[:, :], HALFPI_, W)
        onesf = self.sb(st, "onesf", [128, 512], F32)
        self.MS("vector", onesf[:, :], 1.0, W)
        identf = c["identf"] = self.sb(st, "identf", [128, 128], F32)
        P.op("gpsimd", lambda e: e.affine_select(out=identf[:, :], in_=onesf[:, :128], pattern=[[-1, 128]],
                                                compare_op=ALU.is_equal, fill=0.0, base=0, channel_multiplier=1), [cb], W)
        identb = c["identb"] = self.sb(st, "identb", [128, 128], BF16)
        self.CP("vector", identb[:, :], identf[:, :], [cb], W)
        for nm, val in (("onesD", 1.0 / cfg.D), ("onesCW", 1.0 / cfg.CW), ("negones", -1.0)):
            t = c[nm] = self.sb(st, nm, [128, 128], BF16)
            self.MS("vector", t[:, :], val, W)
        o64 = c["ones64"] = self.sb(st, "ones64", [128, 128], BF16)
        self.MS("vector", o64[:, :], 0.0, W)
        self.MS("vector", o64[0:64, 0:64], 1.0 / 64, W)
        self.MS("vector", o64[64:128, 64:128], 1.0 / 64, W)
        tnf = self.sb(st, "tnf", [128, 128], F32)
        P.op("gpsimd", lambda e: e.affine_select(out=tnf[:, :], in_=onesf[:, :128], pattern=[[-1, 128]],
                                                compare_op=ALU.is_ge, fill=0.0, base=0, channel_multiplier=1), [cb], W)
        tneg = c["tneg"] = self.sb(st, "tneg", [128, 128], BF16)
        self.TS("vector", tneg[:, :], tnf[:, :], -1.0, None, ALU.mult, None, [cb], W)
        c["keep"], c["negm"] = [], []
        kf = self.sb(st, "keepf", [128, 512], F32)
        for r in range(4):
            P.op("gpsimd", lambda e, r=r: e.affine_select(out=kf[:, :], in_=onesf[:, :], pattern=[[1, 512]],
                                                         compare_op=ALU.is_gt, fill=0.0, base=-r * 128,
                                                         channel_multiplier=-1), [cb], W)
            kp = self.sb(st, f"keep{r}", [128, 512], BF16)
            ng = self.sb(st, f"negm{r}", [128, 512], BF16)
            self.CP("vector", kp[:, :], kf[:, :], [cb], W)
            self.TS("vector", ng[:, :], kf[:, :], -1.0, 30000.0, ALU.add, ALU.mult, [cb], W)
            c["keep"].append(kp)
            c["negm"].append(ng)

    def norm_block(self, hT, hbs, s0, n, gain, xT, xT_b, rstd, rstd_b, hp, hp_b, sq, sq_b, cnt, tag, f32out=None):
        cfg = self.cfg
        KT = cfg.KT
        ps, ps_b, c, cb = self.ps, self.ps_b, self.c, self.cb
        subs = subtiles(n)
        NHP = len(hp)
        for k in range(KT):
            i = cnt["hp"] % NHP; cnt["hp"] += 1
            self.DMA("gpsimd", f"Fhp{i}", hp[i][:, :n], hT[k * 128:(k + 1) * 128, s0:s0 + n], hbs, [hp_b[i]])
            j = cnt["sq"] % 2; cnt["sq"] += 1
            self.ACT(sq[j][:, :n], hp[i][:, :n], AF.Square, [hp_b[i]], [sq_b[j]])
            for si, (o, w) in enumerate(subs):
                self.MM(ps[si][:, :w], c["onesD"][:, :], sq[j][:, o:o + w], k == 0, k == KT - 1, [sq_b[j], cb], [ps_b[si]])
        for si, (o, w) in enumerate(subs):
            self.RSQ(rstd[:, o:o + w], ps[si][:, :w], 1e-6, [ps_b[si]], [rstd_b])
        for k in range(KT):
            i = cnt["hp"] % NHP; cnt["hp"] += 1
            self.DMA("gpsimd", f"Fhp{i}", hp[i][:, :n], hT[k * 128:(k + 1) * 128, s0:s0 + n], hbs, [hp_b[i]])
            dst = xT[:, k, :n]
            self.STT("vector", dst, hp[i][:, :n], gain[:, k:k + 1], rstd[:, :n], ALU.mult, ALU.mult,
                     [hp_b[i], rstd_b, self.pv_b], [xT_b])

    def ffn(self, hin, hin_b, hout, hout_b, wg_s, wu_s, wd_s, w_b, gain, tag, T=512):
        nc, P, cfg = self.nc, self.P, self.cfg
        KT, MT = cfg.KT, cfg.MT
        ps, ps_b = self.ps, self.ps_b
        TM = min(T, cfg.LP)
        MC = 2 if MT % 2 == 0 else 1
        with ExitStack() as st:
            xTs = [self.sb(st, f"xT{tag}{i}", [128, KT, TM], BF16) for i in range(2)]
            hid = self.sb(st, f"hid{tag}", [128, MT, TM], BF16)
            rstd = self.sb(st, f"rstd{tag}", [128, TM], F32)
            NHP, NW = 3, 2
            hp = [self.sb(st, f"T{tag}hp{i}", [128, TM], F32) for i in range(NHP)]
            sq = [self.sb(st, f"sq{tag}{i}", [128, TM], BF16) for i in range(2)]
            sg = [self.sb(st, f"sg{tag}{i}", [128, TM], BF16) for i in range(2)]
            ob = [self.sb(st, f"T{tag}ob{i}", [128, TM], F32) for i in range(2)]
            wg = [self.sb(st, f"T{tag}wg{i}", [128, KT, MC * 128], BF16) for i in range(NW)]
            wu = [self.sb(st, f"T{tag}wu{i}", [128, KT, MC * 128], BF16) for i in range(NW)]
            wd = [self.sb(st, f"T{tag}wd{i}", [128, MT, 128], BF16) for i in range(NW)]
            xTs_b, hid_b, rstd_b = P.bufs(2), P.buf(), P.buf()
            hp_b, sq_b, sg_b, ob_b = P.bufs(NHP), P.bufs(2), P.bufs(2), P.bufs(2)
            wg_b, wu_b, wd_b = P.bufs(NW), P.bufs(NW), P.bufs(NW)
            cnt = dict(hp=0, sq=0, sg=0, ob=0, wgu=0, wd=0)
            blocks = cfg.blocks(T)

            def norm(bi):
                s0, n = blocks[bi]
                self.norm_block(hin, [], s0, n, gain, xTs[bi % 2], xTs_b[bi % 2], rstd, rstd_b, hp, hp_b, sq, sq_b, cnt, tag)

            def gateup(bi):
                s0, n = blocks[bi]
                subs = subtiles(n)
                xT, xT_b = xTs[bi % 2], xTs_b[bi % 2]
                for mc in range(MT // MC):
                    wi = cnt["wgu"] % NW; cnt["wgu"] += 1
                    sl = slice(mc * MC * 128, (mc + 1) * MC * 128)
                    self.DMA("sync", f"Fwg{wi}", wg[wi][:, :, :], wg_s[:, :, sl], [w_b], [wg_b[wi]])
                    self.DMA("sync", f"Fwu{wi}", wu[wi][:, :, :], wu_s[:, :, sl], [w_b], [wu_b[wi]])
                    for mi in range(MC):
                        m = mc * MC + mi
                        pb = (m % 2) * 4
                        for which, (wt, wtb) in enumerate(((wg, wg_b), (wu, wu_b))):
                            for k in range(KT):
                                for si, (o, w) in enumerate(subs):
                                    bank = pb + which * 2 + si
                                    self.MM(ps[bank][:, :w], wt[wi][:, k, mi * 128:(mi + 1) * 128], xT[:, k, o:o + w],
                                            k == 0, k == KT - 1, [wtb[wi], xT_b], [ps_b[bank]])
                        j = cnt["sg"] % 2; cnt["sg"] += 1
                        for si, (o, w) in enumerate(subs):
                            self.ACT(sg[j][:, o:o + w], ps[pb + si][:, :w], AF.Silu, [ps_b[pb + si]], [sg_b[j]])
                        for si, (o, w) in enumerate(subs):
                            self.TT("vector", hid[:, m, o:o + w], ps[pb + 2 + si][:, :w], sg[j][:, o:o + w], ALU.mult,
                                    [ps_b[pb + 2 + si], sg_b[j]], [hid_b])

            def down(bi):
                s0, n = blocks[bi]
                subs = subtiles(n)
                for d in range(KT):
                    wi = cnt["wd"] % NW; cnt["wd"] += 1
                    self.DMA("sync", f"Fwd{wi}", wd[wi][:, :, :], wd_s[:, :, d * 128:(d + 1) * 128], [w_b], [wd_b[wi]])
                    i = cnt["hp"] % NHP; cnt["hp"] += 1
                    self.DMA("gpsimd", f"Fhp{i}", hp[i][:, :n], hin[d * 128:(d + 1) * 128, s0:s0 + n], [], [hp_b[i]])
                    pb = 1 + (d % 2) * 4
                    for m in range(MT):
                        for si, (o, w) in enumerate(subs):
                            self.MM(ps[pb + si][:, :w], wd[wi][:, m, :], hid[:, m, o:o + w], m == 0, m == MT - 1,
                                    [wd_b[wi], hid_b], [ps_b[pb + si]])
                    oi = cnt["ob"] % 2; cnt["ob"] += 1
                    for si, (o, w) in enumerate(subs):
                        self.STT("vector", ob[oi][:, o:o + w], ps[pb + si][:, :w], 0.5, hp[i][:, o:o + w], ALU.mult, ALU.add,
                                 [ps_b[pb + si], hp_b[i]], [ob_b[oi]])
                    self.DMA("gpsimd", f"Fob{oi}", hout[d * 128:(d + 1) * 128, s0:s0 + n], ob[oi][:, :n], [ob_b[oi]], [])

            norm(0)
            for bi in range(len(blocks)):
                gateup(bi)
                if bi + 1 < len(blocks):
                    norm(bi + 1)
                down(bi)
            P.barrier()

    def load_params(self, ins):
        nc, P, st, cfg = self.nc, self.P, self.st, self.cfg
        c, cb = self.c, self.cb
        KT, ST, CT = cfg.KT, cfg.ST, cfg.CT
        names = [("ffn1_norm", KT), ("mix_norm", KT), ("ffn2_norm", KT), ("post_norm", KT),
                 ("ssm_b_glu", ST), ("ssm_d", ST), ("conv_b_dw", CT), ("conv_ln_g", CT), ("conv_ln_b", CT),
                 ("conv_b_pw", CT)]
        R = sum(n for _, n in names)
        assert R <= 128
        self.pvcol = {}
        r = 0
        for nm, n in names:
            self.pvcol[nm] = r
            r += n
        self.pv, self.wdwT, self.qkg = [], [], []
        self.pv_b = P.buf("pv")
        ps, ps_b = self.ps, self.ps_b
        for l in range(cfg.depth):
            rows = self.sb(st, f"pvrows{l}", [128, 128], F32)
            rows_b = P.buf()
            self.MS("vector", rows[:, :], 0.0, [rows_b])
            for nm, n in names:
                src = ins[nm][l]
                if nm == "ssm_d":
                    src = src.rearrange("g h -> (g h)")
                r0 = self.pvcol[nm]
                self.DMA("sync", f"pvl", rows[r0:r0 + n, :], src.rearrange("(n p) -> n p", p=128), [], [rows_b])
            self.TR(ps[0][:, :128], rows[:, :], c["identf"][:, :], [rows_b, cb], [ps_b[0]])
            pv = self.sb(st, f"pv{l}", [128, 128], F32)
            self.CP("vector", pv[:, :], ps[0][:, :128], [ps_b[0]], [self.pv_b])
            self.pv.append(pv)
            wrows = self.sb(st, f"wdwrows{l}", [32, cfg.CW], F32)
            wrows_b = P.buf()
            self.DMA("sync", "pvw", wrows[0:31, :], ins["conv_w_dw"][l], [], [wrows_b])
            wT = self.sb(st, f"wdwT{l}", [128, CT, 32], F32)
            for ct in range(CT):
                self.TR(ps[1][:, :31], wrows[0:31, ct * 128:(ct + 1) * 128], c["identf"][0:31, 0:31], [wrows_b, cb], [ps_b[1]])
                self.CP("vector", wT[:, ct, 0:31], ps[1][:, :31], [ps_b[1]], [self.pv_b])
            self.wdwT.append(wT)
            qg = self.sb(st, f"qkg{l}", [128, 2], F32)
            for hf in range(2):
                self.DMA("sync", "pvq", qg[hf * 64:(hf + 1) * 64, 0:1], ins["q_norm"][l].rearrange("(p o) -> p o", o=1), [], [self.pv_b])
                self.DMA("sync", "pvq", qg[hf * 64:(hf + 1) * 64, 1:2], ins["k_norm"][l].rearrange("(p o) -> p o", o=1), [], [self.pv_b])
            self.TS("vector", qg[:, 0:1], qg[:, 0:1], 0.125, None, ALU.mult, None, [self.pv_b], [self.pv_b])
            self.qkg.append(qg)

    def gain(self, l, nm):
        c0 = self.pvcol[nm]
        return self.pv[l][:, c0:128]

    def cast_weights(self, ins):
        nc, cfg = self.nc, self.cfg
        self.w = []
        self.w_b = self.P.buf("wcast")
        specs = [("ffn1_w_gate", cfg.KT, cfg.DFF), ("ffn1_w_up", cfg.KT, cfg.DFF), ("ffn1_w_down", cfg.MT, cfg.D),
                 ("w_in", cfg.KT, cfg.IN), ("ssm_w_glu", cfg.ST, cfg.SW), ("conv_w_pw", cfg.CT, cfg.CW),
                 ("w_out", cfg.KT, cfg.D),
                 ("ffn2_w_gate", cfg.KT, cfg.DFF), ("ffn2_w_up", cfg.KT, cfg.DFF), ("ffn2_w_down", cfg.MT, cfg.D)]
        self.w_b0 = self.P.buf("wcast0")
        for l in range(cfg.depth):
            d = {}
            for si_, (nm, kt, n) in enumerate(specs):
                t = nc.dram_tensor(f"s_{nm}{l}", [128, kt, n], BF16).ap()
                d[nm] = t
                first = (l == 0 and si_ < 3)
                for k in range(kt):
                    self.DMA("gpsimd", "wcast0" if first else "wcast", t[:, k, :], ins[nm][l, k * 128:(k + 1) * 128, :], [],
                             [self.w_b0 if first else self.w_b])
            self.w.append(d)

    def transpose_in(self, x0, hT):
        nc, P, cfg = self.nc, self.P, self.cfg
        ps, ps_b, c, cb = self.ps, self.ps_b, self.c, self.cb
        KT = cfg.KT
        with ExitStack() as st:
            xin = [self.sb(st, f"xin{i}", [128, cfg.D], F32) for i in range(2)]
            xo = [self.sb(st, f"xo{i}", [128, KT, 512], F32) for i in range(2)]
            xin_b, xo_b = P.bufs(2), P.bufs(2)
            nb = 0
            for (s0, n) in cfg.blocks(512):
                oi = nb % 2; nb += 1
                for tt in range(n // 128):
                    i = (s0 // 128 + tt) % 2
                    self.DMA("sync", f"xin{i}", xin[i][:, :], x0[s0 + tt * 128:s0 + (tt + 1) * 128, :], [], [xin_b[i]])
                    for k in range(KT):
                        bank = k % 8
                        self.TR(ps[bank][:, :128], xin[i][:, k * 128:(k + 1) * 128], c["identf"][:, :], [xin_b[i], cb], [ps_b[bank]])
                        self.CP("vector" if k % 2 == 0 else "scalar", xo[oi][:, k, tt * 128:(tt + 1) * 128], ps[bank][:, :128],
                                [ps_b[bank]], [xo_b[oi]])
                self.DMA("sync", f"xo{oi}", hT[:, s0:s0 + n].rearrange("(k p) t -> p k t", p=128), xo[oi][:, :, :n], [xo_b[oi]], [])
            P.barrier(exclude=("wcast", "wcast0"))

    def post_norm(self, hin, gain, hout=None, out_tok=None):
        nc, P, cfg = self.nc, self.P, self.cfg
        ps, ps_b, c, cb = self.ps, self.ps_b, self.c, self.cb
        KT = cfg.KT
        T = 512
        with ExitStack() as st:
            rstd = self.sb(st, "pn_rstd", [128, T], F32)
            hp = [self.sb(st, f"pn_hp{i}", [128, T], F32) for i in range(3)]
            sq = [self.sb(st, f"pn_sq{i}", [128, T], BF16) for i in range(2)]
            yb = [self.sb(st, f"pn_y{i}", [128, T], F32) for i in range(3)]
            ot = [self.sb(st, f"pn_ot{i}", [128, cfg.D], F32) for i in range(4)]
            rstd_b = P.buf()
            hp_b, sq_b, yb_b, ot_b = P.bufs(3), P.bufs(2), P.bufs(3), P.bufs(4)
            cnt = dict(hp=0, sq=0, y=0)
            for (s0, n) in cfg.blocks(T):
                subs = subtiles(n)
                for k in range(KT):
                    i = cnt["hp"] % 3; cnt["hp"] += 1
                    self.DMA("sync", f"pnhp{i}", hp[i][:, :n], hin[k * 128:(k + 1) * 128, s0:s0 + n], [], [hp_b[i]])
                    j = cnt["sq"] % 2; cnt["sq"] += 1
                    self.ACT(sq[j][:, :n], hp[i][:, :n], AF.Square, [hp_b[i]], [sq_b[j]])
                    self.MM(ps[0][:, :n], c["onesD"][:, :], sq[j][:, :n], k == 0, k == KT - 1, [sq_b[j], cb], [ps_b[0]])
                self.RSQ(rstd[:, :n], ps[0][:, :n], 1e-6, [ps_b[0]], [rstd_b])
                ntt = n // 128
                for k in range(KT):
                    i = cnt["hp"] % 3; cnt["hp"] += 1
                    self.DMA("sync", f"pnhp{i}", hp[i][:, :n], hin[k * 128:(k + 1) * 128, s0:s0 + n], [], [hp_b[i]])
                    yi = cnt["y"] % 3; cnt["y"] += 1
                    self.STT("vector", yb[yi][:, :n], hp[i][:, :n], gain[:, k:k + 1], rstd[:, :n], ALU.mult, ALU.mult,
                             [hp_b[i], rstd_b, self.pv_b], [yb_b[yi]])
                    if hout is not None:
                        self.DMA("gpsimd", f"pny{yi}", hout[k * 128:(k + 1) * 128, s0:s0 + n], yb[yi][:, :n], [yb_b[yi]], [])
                    if out_tok is not None:
                        for tt in range(ntt):
                            bank = 1 + (k * ntt + tt) % 7
                            self.TR(ps[bank][:, :128], yb[yi][:, tt * 128:(tt + 1) * 128], c["identf"][:, :], [yb_b[yi], cb], [ps_b[bank]])
                            self.CP("scalar" if tt % 2 == 0 else "vector", ot[tt][:, k * 128:(k + 1) * 128], ps[bank][:, :128],
                                    [ps_b[bank]], [ot_b[tt]])
                if out_tok is not None:
                    for tt in range(ntt):
                        self.DMA("gpsimd", f"pnot{tt}", out_tok[s0 + tt * 128:s0 + (tt + 1) * 128, :], ot[tt][:, :], [ot_b[tt]], [])
            P.barrier()

    def in_proj(self, l, hin, S):
        nc, P, cfg = self.nc, self.P, self.cfg
        ps, ps_b, c, cb = self.ps, self.ps_b, self.c, self.cb
        KT, AT, ST, CT, AW, SW, CW = cfg.KT, cfg.AT, cfg.ST, cfg.CT, cfg.AW, cfg.SW, cfg.CW
        T = 1024
        TM = min(T, cfg.LP)
        win = self.w[l]["w_in"]
        gain = self.gain(l, "mix_norm")
        qg = self.qkg[l]
        tag = f"ip{l}"
        with ExitStack() as st:
            xT = self.sb(st, "ip_xT", [128, KT, TM], BF16)
            rstd = self.sb(st, "ip_rstd", [128, TM], F32)
            hp = [self.sb(st, f"ip_hp{i}", [128, TM], F32) for i in range(3)]
            sq = [self.sb(st, f"ip_sq{i}", [128, TM], BF16) for i in range(2)]
            wt = [self.sb(st, f"ip_w{i}", [128, KT, 256], BF16) for i in range(3)]
            rs = [self.sb(st, f"ip_rs{i}", [128, TM], F32) for i in range(2)]
            ob = [self.sb(st, f"ip_ob{i}", [128, TM], BF16) for i in range(3)]
            ab = [self.sb(st, f"ip_ab{i}", [128, TM], F32) for i in range(2)]
            sg = [self.sb(st, f"ip_sg{i}", [128, TM], F32) for i in range(2)]
            vb = [self.sb(st, f"ip_vb{i}", [128, 256], BF16) for i in range(3)]
            xT_b, rstd_b = P.buf(), P.buf()
            hp_b, sq_b, wt_b, rs_b, ob_b, ab_b, sg_b, vb_b = (P.bufs(3), P.bufs(2), P.bufs(3), P.bufs(2), P.bufs(3),
                                                               P.bufs(2), P.bufs(2), P.bufs(3))
            cnt = dict(hp=0, sq=0, w=0, rs=0, ob=0, sg=0, vb=0, t=0)

            def load_w(col0):
                wi = cnt["w"] % 3; cnt["w"] += 1
                self.DMA("sync", f"ipw{wi}", wt[wi][:, :, :], win[:, :, col0:col0 + 256], [self.w_b], [wt_b[wi]])
                return wi

            for (s0, n) in cfg.blocks(T):
                subs = subtiles(n)
                self.norm_block(hin, [], s0, n, gain, xT, xT_b, rstd, rstd_b, hp, hp_b, sq, sq_b, cnt, tag)

                def col_tile(wi, mi):
                    pb = (cnt["t"] % 2) * 2; cnt["t"] += 1
                    for k in range(KT):
                        for si, (o, w) in enumerate(subs):
                            self.MM(ps[pb + si][:, :w], wt[wi][:, k, mi * 128:(mi + 1) * 128], xT[:, k, o:o + w],
                                    k == 0, k == KT - 1, [wt_b[wi], xT_b], [ps_b[pb + si]])
                    return pb

                def new_ob():
                    oi = cnt["ob"] % 3; cnt["ob"] += 1
                    return oi

                for which, (dst, col0) in enumerate(((S["qT"], 0), (S["kT"], AW))):
                    for ch in range(AT // 2 if AT >= 2 else 1):
                        ntile = 2 if AT >= 2 else 1
                        wi = load_w(col0 + ch * 256) if ntile == 2 else load_w(col0)
                        for mi in range(ntile):
                            tix = ch * 2 + mi
                            pb = col_tile(wi, mi)
                            j = cnt["sq"] % 2; cnt["sq"] += 1
                            for si, (o, w) in enumerate(subs):
                                self.ACT(sq[j][:, o:o + w], ps[pb + si][:, :w], AF.Square, [ps_b[pb + si]], [sq_b[j]])
                            ri = cnt["rs"] % 2; cnt["rs"] += 1
                            for si, (o, w) in enumerate(subs):
                                self.MM(ps[4 + pb + si][:, :w], c["ones64"][:, :], sq[j][:, o:o + w], True, True,
                                        [sq_b[j], cb], [ps_b[4 + pb + si]])
                                self.RSQ(rs[ri][:, o:o + w], ps[4 + pb + si][:, :w], 1e-6, [ps_b[4 + pb + si]], [rs_b[ri]])
                            oi = new_ob()
                            for si, (o, w) in enumerate(subs):
                                self.STT("vector", ob[oi][:, o:o + w], ps[pb + si][:, :w], qg[:, which:which + 1], rs[ri][:, o:o + w],
                                         ALU.mult, ALU.mult, [ps_b[pb + si], rs_b[ri], self.pv_b], [ob_b[oi]])
                            self.DMA("gpsimd", f"ipob{oi}", dst[tix * 128:(tix + 1) * 128, s0:s0 + n], ob[oi][:, :n], [ob_b[oi]], [])
                for ch in range(max(1, AW // 256)):
                    ncol = min(256, AW)
                    wi = load_w(2 * AW + ch * 256)
                    for tt in range(n // 128):
                        bank = cnt["vb"] % 4
                        for k in range(KT):
                            self.MM(ps[bank][:, :ncol], xT[:, k, tt * 128:(tt + 1) * 128], wt[wi][:, k, :ncol],
                                    k == 0, k == KT - 1, [wt_b[wi], xT_b], [ps_b[bank]])
                        vi = cnt["vb"] % 3; cnt["vb"] += 1
                        self.ACT(vb[vi][:, :ncol], ps[bank][:, :ncol], AF.Copy, [ps_b[bank]], [vb_b[vi]])
                        self.DMA("gpsimd", f"ipvb{vi}", S["v"][s0 + tt * 128:s0 + (tt + 1) * 128, ch * 256:ch * 256 + ncol],
                                 vb[vi][:, :ncol], [vb_b[vi]], [])
                for ch in range(max(1, SW // 256)):
                    ntile = min(2, ST)
                    wi = load_w(3 * AW + ch * 256)
                    for mi in range(ntile):
                        tix = ch * 2 + mi
                        pb = col_tile(wi, mi)
                        oi = new_ob()
                        for si, (o, w) in enumerate(subs):
                            self.ACT(ob[oi][:, o:o + w], ps[pb + si][:, :w], AF.Copy, [ps_b[pb + si]], [ob_b[oi]])
                        self.DMA("gpsimd", f"ipob{oi}", S["uT"][tix * 128:(tix + 1) * 128, s0:s0 + n], ob[oi][:, :n], [ob_b[oi]], [])
                for ch in range(CT // 2):
                    wa = load_w(3 * AW + SW + ch * 256)
                    for mi in range(2):
                        pb = col_tile(wa, mi)
                        for si, (o, w) in enumerate(subs):
                            self.ACT(ab[mi][:, o:o + w], ps[pb + si][:, :w], AF.Copy, [ps_b[pb + si]], [ab_b[mi]])
                    wgi = load_w(3 * AW + SW + CW + ch * 256)
                    for mi in range(2):
                        tix = ch * 2 + mi
                        pb = col_tile(wgi, mi)
                        si_ = cnt["sg"] % 2; cnt["sg"] += 1
                        for si, (o, w) in enumerate(subs):
                            self.ACT(sg[si_][:, o:o + w], ps[pb + si][:, :w], AF.Sigmoid, [ps_b[pb + si]], [sg_b[si_]])
                        oi = new_ob()
                        self.TT("vector", ob[oi][:, :n], ab[mi][:, :n], sg[si_][:, :n], ALU.mult, [ab_b[mi], sg_b[si_]], [ob_b[oi]])
                        self.DMA("gpsimd", f"ipob{oi}", S["gluT"][tix * 128:(tix + 1) * 128, s0:s0 + n], ob[oi][:, :n], [ob_b[oi]], [])
            P.barrier()

    def attention(self, l, S):
        nc, P, cfg = self.nc, self.P, self.cfg
        ps, ps_b, c, cb = self.ps, self.ps_b, self.c, self.cb
        LP, NTT, AT = cfg.LP, cfg.NTT, cfg.AT
        qblocks = cfg.blocks(512)
        with ExitStack() as st:
            kT = self.sb(st, "at_kT", [128, LP], BF16)
            qT = self.sb(st, "at_qT", [128, LP], BF16)
            vz = [self.sb(st, f"at_vz{h}", [128, NTT, 128], BF16) for h in range(2)]
            aT = self.sb(st, "at_aT", [128, LP], BF16)
            NS, DEP = 3, 2
            pq = self.pq
            eb = [self.sb(st, f"at_e{h}", [128, 2, 512], F32) for h in range(NS)]
            lb = [self.sb(st, f"at_l{h}", [128, 2, 512], BF16) for h in range(NS)]
            cum = self.sb(st, "at_cum", [128, 2, 512], BF16)
            att = [self.sb(st, f"at_a{h}", [128, 2, 512], BF16) for h in range(3)]
            kT_b, qT_b, aT_b, cum_b = P.buf(), P.buf(), P.buf(), P.buf()
            vz_b, eb_b, lb_b, att_b = P.bufs(2), P.bufs(NS), P.bufs(NS), P.bufs(3)
            for h in range(2):
                self.MS("vector", vz[h][:, :, :], 0.0, [vz_b[h]])
            for j in range(AT):
                self.DMA("sync", "at_k", kT[:, :], S["kT"][j * 128:(j + 1) * 128, :], [], [kT_b])
                self.DMA("sync", "at_q", qT[:, :], S["qT"][j * 128:(j + 1) * 128, :], [], [qT_b])
                for h in range(2):
                    for t0 in range(0, NTT, 16):
                        t1_ = min(NTT, t0 + 16)
                        self.DMA("gpsimd", f"at_v{h}", vz[h][:, t0:t1_, h * 64:(h + 1) * 64],
                                 S["v"][t0 * 128:t1_ * 128, j * 128 + h * 64:j * 128 + (h + 1) * 64].rearrange("(t p) d -> p t d", p=128),
                                 [], [vz_b[h]])
                steps = []
                for qi, (q0, w) in enumerate(qblocks):
                    kt_max = (q0 + w) // 128 - 1
                    for kt in range(kt_max, -1, -1):
                        steps.append((qi, q0, w, kt, kt == kt_max, kt == 0, kt - q0 // 128))

                def v3(t, w):
                    return t.rearrange("p (h c) -> p h c", h=2)[:, :, :w]

                def stage1(idx):
                    qi, q0, w, kt, first, last, r = steps[idx]
                    sl = idx % NS
                    pbs = [ps_b[2 * sl], ps_b[2 * sl + 1]]
                    for h in range(2):
                        hs = slice(h * 64, (h + 1) * 64)
                        self.MM(pq[sl][:, h * 512:h * 512 + w], kT[hs, kt * 128:(kt + 1) * 128], qT[hs, q0:q0 + w], True, True,
                                [kT_b, qT_b], [pbs[h]])
                    self.ACT(eb[sl][:, :, :w], v3(pq[sl], w), AF.Exp, pbs, [eb_b[sl]])
                    self.ACT(lb[sl][:, :, :w], eb[sl][:, :, :w], AF.Ln, [eb_b[sl]], [lb_b[sl]], bias=1.0)
                    if r >= 0:
                        self.TT("gpsimd", lb[sl][:, :, :w], lb[sl][:, :, :w], c["keep"][r][:, :w].unsqueeze(1).to_broadcast([128, 2, w]),
                                ALU.mult, [lb_b[sl], cb], [lb_b[sl]])

                def stage2(idx):
                    qi, q0, w, kt, first, last, r = steps[idx]
                    sl = idx % NS
                    ai = idx % 3
                    pbs = [ps_b[2 * sl], ps_b[2 * sl + 1]]
                    for h in range(2):
                        terms = [(c["tneg"][:, :], lb[sl][:, h, :w], [lb_b[sl], cb, eb_b[sl]])]
                        if not first:
                            terms.append((c["negones"][:, :], cum[:, h, :w], [cum_b, cb]))
                        if r >= 0:
                            terms.append((c["identb"][:, :], c["negm"][r][:, :w], [cb]))
                        for ti, (lt, rh, rd) in enumerate(terms):
                            self.MM(pq[sl][:, h * 512:h * 512 + w], lt, rh, False, ti == len(terms) - 1, rd, [pbs[h]])
                    self.ACT(att[ai][:, :, :w], v3(pq[sl], w), AF.Exp, pbs, [att_b[ai]])
                    if kt > 0:
                        if first:
                            self.CP("vector", cum[:, :, :w], lb[sl][:, :, :w], [lb_b[sl]], [cum_b])
                        else:
                            self.TT("vector", cum[:, :, :w], cum[:, :, :w], lb[sl][:, :, :w], ALU.add, [cum_b, lb_b[sl]], [cum_b])

                def stage2b(idx):
                    qi, q0, w, kt, first, last, r = steps[idx]
                    ai = idx % 3
                    O = 6 + qi % 2
                    for h in range(2):
                        self.MM(ps[O][:, :w], vz[h][:, kt, :], att[ai][:, h, :w], first and h == 0, last and h == 1,
                                [vz_b[h], att_b[ai]], [ps_b[O]])
                    if last:
                        self.CP("vector", aT[:, q0:q0 + w], ps[O][:, :w], [ps_b[O]], [aT_b])

                for idx in range(len(steps) + DEP + 1):
                    if idx < len(steps):
                        stage1(idx)
                    if 0 <= idx - DEP - 1 < len(steps):
                        stage2b(idx - DEP - 1)
                    if 0 <= idx - DEP < len(steps):
                        stage2(idx - DEP)
                self.DMA("gpsimd", "at_o", S["mixT"][j * 128:(j + 1) * 128, :], aT[:, :], [aT_b], [])
            P.barrier()

    def ssm(self, l, ins, S):
        nc, P, cfg = self.nc, self.P, self.cfg
        ps, ps_b, c, cb = self.ps, self.ps_b, self.c, self.cb
        G, ST, SW, AW, LP = cfg.G, cfg.ST, cfg.SW, cfg.AW, cfg.LP
        T = 512
        V = "vector"
        with ExitStack() as st:
            pb_ = P.buf("ssmprep")
            W, R = [pb_], [pb_, cb]

            def t32(name, shape):
                return self.sb(st, f"ss_{name}", shape, F32)
            lnat = t32("lnat", [G, 256])
            for hf in range(2):
                self.DMA("sync", "ss_p", lnat[:, hf * 64:(hf + 1) * 64], ins["ssm_lambda_re"][l], [], W)
                self.DMA("sync", "ss_p", lnat[:, 128 + hf * 64:128 + (hf + 1) * 64], ins["ssm_lambda_im"][l], [], W)
            lre, lim = t32("lre", [128, G]), t32("lim", [128, G])
            self.TR(ps[0][:, :G], lnat[:, 0:128], c["identf"][0:G, 0:G], R, [ps_b[0]])
            self.TS(V, lre[:, :], ps[0][:, :G], -1e-4, None, ALU.min, None, [ps_b[0]], W)
            self.TR(ps[1][:, :G], lnat[:, 128:256], c["identf"][0:G, 0:G], R, [ps_b[1]])
            self.CP(V, lim[:, :], ps[1][:, :G], [ps_b[1]], W)
            step = t32("step", [128, G])
            self.DMA("sync", "ss_p", step[:, :], ins["ssm_log_step"][l].partition_broadcast(128), [], W)
            self.ACT(step[:, :], step[:, :], AF.Exp, R, W)
            th, thr, rr, tmp, m2 = t32("th", [128, G]), t32("thr", [128, G]), t32("rr", [128, G]), t32("tmp", [128, G]), t32("m2", [128, G])
            pti = self.sb(st, "ss_pti", [128, G], mybir.dt.int32)
            ptf = t32("ptf", [128, G])
            self.TT(V, th[:, :], lim[:, :], step[:, :], ALU.mult, R, W)
            self.TS(V, th[:, :], th[:, :], 1.0 / TWO_PI, None, ALU.mult, None, R, W)
            self.FRAC(thr[:, :], th[:, :], pti[:, :], ptf[:, :], R, W)
            self.TT(V, tmp[:, :], lre[:, :], step[:, :], ALU.mult, R, W)
            self.ACT(rr[:, :], tmp[:, :], AF.Exp, R, W)
            sn, cs = t32("sn", [128, G]), t32("cs", [128, G])
            self.ACT(sn[:, :], thr[:, :], AF.Sin, R, W, scale=SC2PI)
            self.TS(V, m2[:, :], thr[:, :], 0.25, None, ALU.add, None, R, W)
            self.FRAC(m2[:, :], m2[:, :], pti[:, :], ptf[:, :], R, W)
            self.ACT(cs[:, :], m2[:, :], AF.Sin, R, W, scale=SC2PI)
            a, b, den, wre, wim, t2 = (t32("a", [128, G]), t32("b", [128, G]), t32("den", [128, G]), t32("wre", [128, G]),
                                       t32("wim", [128, G]), t32("t2", [128, G]))
            self.TT(V, a[:, :], rr[:, :], cs[:, :], ALU.mult, R, W)
            self.TS(V, a[:, :], a[:, :], -1.0, None, ALU.add, None, R, W)
            self.TT(V, b[:, :], rr[:, :], sn[:, :], ALU.mult, R, W)
            self.TT(V, den[:, :], lre[:, :], lre[:, :], ALU.mult, R, W)
            self.TT(V, tmp[:, :], lim[:, :], lim[:, :], ALU.mult, R, W)
            self.TT(V, den[:, :], den[:, :], tmp[:, :], ALU.add, R, W)
            P.op(V, lambda e: e.reciprocal(out=den[:, :], in_=den[:, :]), R, W)
            self.TT(V, wre[:, :], a[:, :], lre[:, :], ALU.mult, R, W)
            self.TT(V, tmp[:, :], b[:, :], lim[:, :], ALU.mult, R, W)
            self.TT(V, wre[:, :], wre[:, :], tmp[:, :], ALU.add, R, W)
            self.TT(V, wre[:, :], wre[:, :], den[:, :], ALU.mult, R, W)
            self.TT(V, wim[:, :], b[:, :], lre[:, :], ALU.mult, R, W)
            self.TT(V, tmp[:, :], a[:, :], lim[:, :], ALU.mult, R, W)
            self.TT(V, wim[:, :], wim[:, :], tmp[:, :], ALU.subtract, R, W)
            self.TT(V, wim[:, :], wim[:, :], den[:, :], ALU.mult, R, W)
            bre, bim = t32("bre", [128, G, 16]), t32("bim", [128, G, 16])
            for hf in range(2):
                self.DMA("sync", "ss_p", bre[hf * 64:(hf + 1) * 64, :, :], ins["ssm_b_re"][l].rearrange("g p h -> p g h"), [], W)
                self.DMA("sync", "ss_p", bim[hf * 64:(hf + 1) * 64, :, :], ins["ssm_b_im"][l].rearrange("g p h -> p g h"), [], W)
            BB, t3, t4 = t32("BB", [128, G, 16]), t32("t3", [128, G, 16]), t32("t4", [128, G, 16])
            wre_b = wre[:, :].unsqueeze(2).to_broadcast([128, G, 16])
            wim_b = wim[:, :].unsqueeze(2).to_broadcast([128, G, 16])
            self.TT(V, t3[:, :, :], bre[:, :, :], wre_b, ALU.mult, R, W)
            self.TT(V, t4[:, :, :], bim[:, :, :], wim_b, ALU.mult, R, W)
            self.TT(V, BB[0:64, :, :], t3[0:64, :, :], t4[0:64, :, :], ALU.subtract, R, W)
            self.TT(V, t3[:, :, :], bim[:, :, :], wre_b, ALU.mult, R, W)
            self.TT(V, t4[:, :, :], bre[:, :, :], wim_b, ALU.mult, R, W)
            self.TT(V, BB[64:128, :, :], t3[64:128, :, :], t4[64:128, :, :], ALU.add, R, W)
            rmask = t32("rmask", [128, 8])
            self.MS(V, rmask[:, :], 1.0, W)
            P.op("gpsimd", lambda e: e.affine_select(out=rmask[:, :], in_=rmask[:, :], pattern=[[-16, 8]], compare_op=ALU.is_ge,
                                                     fill=0.0, base=0, channel_multiplier=1), R, W)
            P.op("gpsimd", lambda e: e.affine_select(out=rmask[:, :], in_=rmask[:, :], pattern=[[16, 8]], compare_op=ALU.is_ge,
                                                     fill=0.0, base=15, channel_multiplier=-1), R, W)
            lhsB = self.sb(st, "ss_lhsB", [128, G, 128], BF16)
            lhsBs = self.sb(st, "ss_lhsBs", [128, G, 128], BF16)
            lhsC1 = self.sb(st, "ss_lhsC1", [128, G, 128], BF16)
            lhsC2 = self.sb(st, "ss_lhsC2", [128, G, 128], BF16)
            BT = t32("BT", [128, 128])
            for j in range(ST):
                self.TR(ps[2][:, :128], BB[:, 8 * j:8 * j + 8, :].rearrange("p g h -> p (g h)"), c["identf"][:, :], R, [ps_b[2]])
                self.CP(V, BT[:, :], ps[2][:, :128], [ps_b[2]], W)
                for gg in range(8):
                    g = 8 * j + gg
                    self.TS(V, lhsB[:, g, :], BT[:, :], rmask[:, gg:gg + 1], None, ALU.mult, None, R, W)
                    self.CP(V, lhsBs[:, g, 0:64], lhsB[:, g, 64:128], R, W)
                    self.TS(V, lhsBs[:, g, 64:128], lhsB[:, g, 0:64], -1.0, None, ALU.mult, None, R, W)
            cn1, cn2 = t32("cn1", [128, ST, 128]), t32("cn2", [128, ST, 128])
            cre = ins["ssm_c_re"][l].rearrange("(j gg) h p -> (gg h) j p", gg=8)
            cim = ins["ssm_c_im"][l].rearrange("(j gg) h p -> (gg h) j p", gg=8)
            self.DMA("sync", "ss_p", cn1[:, :, 0:64], cre, [], W)
            self.DMA("sync", "ss_p", cn1[:, :, 64:128], cim, [], W)
            self.DMA("sync", "ss_p", cn2[:, :, 0:64], cim, [], W)
            self.DMA("sync", "ss_p", cn2[:, :, 64:128], cre, [], W)
            self.MS(V, lhsC1[:, :, :], 0.0, W)
            self.MS(V, lhsC2[:, :, :], 0.0, W)
            CT1, CT2 = t32("CT1", [128, 128]), t32("CT2", [128, 128])
            for j in range(ST):
                self.TR(ps[3][:, :128], cn1[:, j, :], c["identf"][:, :], R, [ps_b[3]])
                self.CP(V, CT1[0:64, :], ps[3][0:64, :128], [ps_b[3]], W)
                self.TS(V, CT1[64:128, :], ps[3][64:128, :128], -1.0, None, ALU.mult, None, [ps_b[3]], W)
                self.TR(ps[2][:, :128], cn2[:, j, :], c["identf"][:, :], R, [ps_b[2]])
                self.TS(V, CT2[:, :], ps[2][:, :128], -1.0, None, ALU.mult, None, [ps_b[2]], W)
                for gg in range(8):
                    g = 8 * j + gg
                    self.CP(V, lhsC1[:, g, 16 * gg:16 * gg + 16], CT1[:, 16 * gg:16 * gg + 16], R, W)
                    self.CP(V, lhsC2[:, g, 16 * gg:16 * gg + 16], CT2[:, 16 * gg:16 * gg + 16], R, W)
            wglu = self.sb(st, "ss_wglu", [128, ST, SW], BF16)
            self.DMA("sync", "ss_p", wglu[:, :, :], self.w[l]["ssm_w_glu"], [self.w_b], W)
            tt_all = t32("tt", [128, LP])
            P.op("gpsimd", lambda e: e.iota(tt_all[:, :], pattern=[[1, LP]], base=0, channel_multiplier=0,
                                            allow_small_or_imprecise_dtypes=True), [], W)
            vlast = t32("vlast", [128, G])
            self.MS(V, vlast[:, :], 0.0, W)
            uT = [self.sb(st, f"ss_u{i}", [128, ST, T], BF16) for i in range(2)]
            m1 = [t32(f"m1{i}", [128, T]) for i in range(4)]
            m2t = [t32(f"m2{i}", [128, T]) for i in range(4)]
            fi1 = [self.sb(st, f"ss_fi1{i}", [128, T], mybir.dt.int32) for i in range(4)]
            ff1 = [t32(f"ff1{i}", [128, T]) for i in range(4)]
            fi_b, ff_b = P.bufs(4), P.bufs(4)
            snT = [t32(f"snT{i}", [128, T]) for i in range(4)]
            csT = [t32(f"csT{i}", [128, T]) for i in range(4)]
            z1 = [t32(f"z1{i}", [128, T]) for i in range(2)]
            z2 = [t32(f"z2{i}", [128, T]) for i in range(2)]
            vv = [t32(f"vv{i}", [128, T]) for i in range(2)]
            Ab = [self.sb(st, f"ss_A{i}", [128, T], BF16) for i in range(2)]
            Bb = [self.sb(st, f"ss_B{i}", [128, T], BF16) for i in range(2)]
            yf = t32("yf", [128, T])
            yg = self.sb(st, "ss_yg", [128, ST, T], BF16)
            sgl = t32("sgl", [128, T])
            so = [self.sb(st, f"ss_so{i}", [128, T], BF16) for i in range(2)]
            uT_b, m1_b, m2_b, sn_b, cs_b, z1_b, z2_b, vv_b, Ab_b, Bb_b, so_b = [P.bufs(4) for _ in range(11)]
            yf_b, yg_b, sgl_b, vl_b = P.buf(), P.buf(), P.buf(), P.buf()
            dcol = self.pvcol["ssm_d"]
            bgcol = self.pvcol["ssm_b_glu"]
            blocks = cfg.blocks(T)
            gsteps = [(bi, j, gg) for bi in range(len(blocks)) for j in range(ST) for gg in range(8)]

            NB = 4

            def stageT1(k):
                bi, j, gg = gsteps[k]
                s0, n = blocks[bi]
                g = 8 * j + gg
                i = k % NB
                if j == 0 and gg == 0:
                    ui = bi % 2
                    self.DMA("sync", f"ss_u{ui}", uT[ui][:, :, :n], S["uT"][:, s0:s0 + n].rearrange("(k p) t -> p k t", p=128), [], [uT_b[ui]])
                self.ACT(m1[i][:, :n], tt_all[:, s0:s0 + n], AF.Copy, [pb_], [m1_b[i]], scale=thr[:, g:g + 1])
                self.CP("scalar", fi1[i][:, :n], m1[i][:, :n], [m1_b[i]], [fi_b[i]])

            def stageT2(k):
                bi, j, gg = gsteps[k]
                s0, n = blocks[bi]
                i = k % NB
                self.CP("gpsimd", ff1[i][:, :n], fi1[i][:, :n], [fi_b[i]], [ff_b[i]])
                self.TT("gpsimd", m1[i][:, :n], m1[i][:, :n], ff1[i][:, :n], ALU.subtract, [m1_b[i], ff_b[i]], [m1_b[i]])

            def stageT3(k):
                bi, j, gg = gsteps[k]
                s0, n = blocks[bi]
                i = k % NB
                self.ACT(snT[i][:, :n], m1[i][:, :n], AF.Sin, [m1_b[i]], [sn_b[i]], scale=SC2PI)
                self.ACT(m2t[i][:, :n], m1[i][:, :n], AF.Abs, [m1_b[i]], [m2_b[i]])
                self.ACT(csT[i][:, :n], m2t[i][:, :n], AF.Sin, [m2_b[i]], [cs_b[i]], bias=self.hpib[:, 0:1], scale=-SC2PI)

            def stageM(k):
                bi, j, gg = gsteps[k]
                s0, n = blocks[bi]
                g = 8 * j + gg
                i = k % NB
                i2 = k % 2
                ui = bi % 2
                Y = 4 + (j % 2)
                A_, B_ = 2 * (k % 2), 2 * (k % 2) + 1
                self.MM(ps[A_][:, :n], lhsB[:, g, :], uT[ui][:, j, :n], True, True, [pb_, uT_b[ui]], [ps_b[A_]])
                self.MM(ps[B_][:, :n], lhsBs[:, g, :], uT[ui][:, j, :n], True, True, [pb_, uT_b[ui]], [ps_b[B_]])
                self.TT(V, z1[i2][:, :n], ps[A_][:, :n], csT[i][:, :n], ALU.mult, [ps_b[A_], cs_b[i]], [z1_b[i2]])
                self.TT(V, z2[i2][:, :n], ps[B_][:, :n], snT[i][:, :n], ALU.mult, [ps_b[B_], sn_b[i]], [z2_b[i2]])
                self.TT(V, z1[i2][:, :n], z1[i2][:, :n], z2[i2][:, :n], ALU.add, [z1_b[i2], z2_b[i2]], [z1_b[i2]])
                rb = rr[:, g:g + 1].to_broadcast([128, n])
                P.op(V, lambda e: e.tensor_tensor_scan(
                    out=vv[i2][:, :n], data0=rb, data1=z1[i2][:, :n], initial=vlast[:, g:g + 1], op0=ALU.mult, op1=ALU.add),
                    [z1_b[i2], vl_b, pb_], [vv_b[i2]])
                self.CP(V, vlast[:, g:g + 1], vv[i2][:, n - 1:n], [vv_b[i2]], [vl_b])
                self.TT(V, Ab[i2][:, :n], vv[i2][:, :n], csT[i][:, :n], ALU.mult, [vv_b[i2], cs_b[i]], [Ab_b[i2]])
                self.TT("gpsimd", Bb[i2][:, :n], vv[i2][:, :n], snT[i][:, :n], ALU.mult, [vv_b[i2], sn_b[i]], [Bb_b[i2]])
                self.MM(ps[Y][:, :n], lhsC1[:, g, :], Ab[i2][:, :n], gg == 0, False, [pb_, Ab_b[i2]], [ps_b[Y]])
                self.MM(ps[Y][:, :n], lhsC2[:, g, :], Bb[i2][:, :n], False, gg == 7, [pb_, Bb_b[i2]], [ps_b[Y]])
                if gg == 7:
                    self.STT(V, yf[:, :n], uT[ui][:, j, :n], self.pv[l][:, dcol + j:dcol + j + 1], ps[Y][:, :n], ALU.mult, ALU.add,
                             [uT_b[ui], ps_b[Y], self.pv_b], [yf_b])
                    self.ACT(yg[:, j, :n], yf[:, :n], AF.Gelu, [yf_b], [yg_b])
                    if j == ST - 1:
                        for cc in range(ST):
                            Gk = 6 + (cc % 2)
                            for k2 in range(ST):
                                self.MM(ps[Gk][:, :n], wglu[:, k2, cc * 128:(cc + 1) * 128], yg[:, k2, :n], k2 == 0, k2 == ST - 1,
                                        [pb_, yg_b], [ps_b[Gk]])
                            self.ACT(sgl[:, :n], ps[Gk][:, :n], AF.Sigmoid, [ps_b[Gk], self.pv_b], [sgl_b],
                                     bias=self.pv[l][:, bgcol + cc:bgcol + cc + 1])
                            oi = (bi * ST + cc) % 2
                            self.TT(V, so[oi][:, :n], yg[:, cc, :n], sgl[:, :n], ALU.mult, [yg_b, sgl_b], [so_b[oi]])
                            self.DMA("gpsimd", f"ss_so{oi}", S["mixT"][AW + cc * 128:AW + (cc + 1) * 128, s0:s0 + n], so[oi][:, :n],
                                     [so_b[oi]], [])

            ng = len(gsteps)
            for k in range(-3, ng):
                if 0 <= k + 3 < ng:
                    stageT1(k + 3)
                if 0 <= k + 2 < ng:
                    stageT2(k + 2)
                if 0 <= k + 1 < ng:
                    stageT3(k + 1)
                if k >= 0:
                    stageM(k)
            P.barrier()

    def conv(self, l, S):
        nc, P, cfg = self.nc, self.P, self.cfg
        ps, ps_b, c, cb = self.ps, self.ps_b, self.c, self.cb
        CT, CW, AW, SW = cfg.CT, cfg.CW, cfg.AW, cfg.SW
        T = 512
        V = "vector"
        pv = self.pv[l]
        col = self.pvcol
        with ExitStack() as st:
            Dg = self.sb(st, "cv_Dg", [128, CT * 31, 128], BF16)
            Dg_b = P.buf()
            for ct in range(CT):
                for j in range(31):
                    self.TS(V, Dg[:, ct * 31 + j, :], c["identf"][:, :], self.wdwT[l][:, ct, j:j + 1], None,
                            ALU.mult, None, [cb, self.pv_b], [Dg_b])
            wpw = self.sb(st, "cv_wpw", [128, CT, CW], BF16)
            wpw_b = P.buf()
            self.DMA("sync", "cv_w", wpw[:, :, :], self.w[l]["conv_w_pw"], [self.w_b], [wpw_b])
            gp = [self.sb(st, f"cv_gp{i}", [128, CT, 30 + T], BF16) for i in range(2)]
            hcf = self.sb(st, "cv_hcf", [128, CT, T], F32)
            hcb = self.sb(st, "cv_hcb", [128, CT, T], BF16)
            sqb = [self.sb(st, f"cv_sq{i}", [128, T], BF16) for i in range(2)]
            meanS = self.sb(st, "cv_mean", [128, T], F32)
            m2 = self.sb(st, "cv_m2", [128, T], F32)
            rsS = self.sb(st, "cv_rs", [128, T], F32)
            t1 = [self.sb(st, f"cv_t1{i}", [128, T], F32) for i in range(2)]
            hsb = self.sb(st, "cv_hs", [128, CT, T], BF16)
            ob = [self.sb(st, f"cv_ob{i}", [128, T], BF16) for i in range(2)]
            gp_b, sqb_b, t1_b, ob_b = P.bufs(2), P.bufs(2), P.bufs(2), P.bufs(2)
            hcf_b, hcb_b, mean_b, m2_b, rs_b, hs_b = P.buf(), P.buf(), P.buf(), P.buf(), P.buf(), P.buf()
            cnt = dict(sq=0, t1=0, ob=0)
            for bi, (s0, n) in enumerate(cfg.blocks(T)):
                gi = bi % 2
                if s0 == 0:
                    self.MS(V, gp[gi][:, :, 0:30], 0.0, [gp_b[gi]])
                    self.DMA("sync", f"cv_gp{gi}", gp[gi][:, :, 30:30 + n], S["gluT"][:, 0:n].rearrange("(k p) t -> p k t", p=128), [], [gp_b[gi]])
                else:
                    self.DMA("sync", f"cv_gp{gi}", gp[gi][:, :, 0:30 + n], S["gluT"][:, s0 - 30:s0 + n].rearrange("(k p) t -> p k t", p=128),
                             [], [gp_b[gi]])
                for ct in range(CT):
                    bank = ct % 4
                    for j in range(31):
                        self.MM(ps[bank][:, :n], Dg[:, ct * 31 + j, :], gp[gi][:, ct, j:j + n], j == 0, j == 30, [Dg_b, gp_b[gi]], [ps_b[bank]])
                    self.TS(V, hcf[:, ct, :n], ps[bank][:, :n], pv[:, col["conv_b_dw"] + ct:col["conv_b_dw"] + ct + 1], None, ALU.add, None,
                            [ps_b[bank], self.pv_b], [hcf_b])
                    self.CP("gpsimd", hcb[:, ct, :n], hcf[:, ct, :n], [hcf_b], [hcb_b])
                    qi = cnt["sq"] % 2; cnt["sq"] += 1
                    self.ACT(sqb[qi][:, :n], hcf[:, ct, :n], AF.Square, [hcf_b], [sqb_b[qi]])
                    self.MM(ps[4][:, :n], c["onesCW"][:, :], hcb[:, ct, :n], ct == 0, ct == CT - 1, [hcb_b, cb], [ps_b[4]])
                    self.MM(ps[5][:, :n], c["onesCW"][:, :], sqb[qi][:, :n], ct == 0, ct == CT - 1, [sqb_b[qi], cb], [ps_b[5]])
                self.ACT(meanS[:, :n], ps[4][:, :n], AF.Copy, [ps_b[4]], [mean_b])
                self.TT("gpsimd", m2[:, :n], meanS[:, :n], meanS[:, :n], ALU.mult, [mean_b], [m2_b])
                self.TT(V, m2[:, :n], ps[5][:, :n], m2[:, :n], ALU.subtract, [ps_b[5], m2_b], [m2_b])
                self.RSQ(rsS[:, :n], m2[:, :n], 1e-5, [m2_b], [rs_b])
                for ct in range(CT):
                    ti = cnt["t1"] % 2; cnt["t1"] += 1
                    self.TT(V, t1[ti][:, :n], hcf[:, ct, :n], meanS[:, :n], ALU.subtract, [hcf_b, mean_b], [t1_b[ti]])
                    self.TT("gpsimd", t1[ti][:, :n], t1[ti][:, :n], rsS[:, :n], ALU.mult, [t1_b[ti], rs_b], [t1_b[ti]])
                    self.ACT(hsb[:, ct, :n], t1[ti][:, :n], AF.Silu, [t1_b[ti], self.pv_b], [hs_b],
                             bias=pv[:, col["conv_ln_b"] + ct:col["conv_ln_b"] + ct + 1],
                             scale=pv[:, col["conv_ln_g"] + ct:col["conv_ln_g"] + ct + 1])
                if self.debug == 2 and bi == 0 and l == 0:
                    for nm_, t_, b_, dt_ in (("hcf", hcf, hcf_b, F32), ("hsb", hsb, hs_b, BF16)):
                        dbg = nc.dram_tensor(f"dbg_{nm_}", [128, CT, T], dt_, kind="ExternalOutput").ap()
                        self.DMA("sync", "dbg", dbg[:, :, :], t_[:, :, :], [b_], [])
                    for nm_, t_, b_ in (("mean", meanS, mean_b), ("rs", rsS, rs_b)):
                        dbg = nc.dram_tensor(f"dbg_{nm_}", [128, T], F32, kind="ExternalOutput").ap()
                        self.DMA("sync", "dbg", dbg[:, :], t_[:, :], [b_], [])
                for cc in range(CT):
                    bank = 6 + (cc % 2)
                    for ct in range(CT):
                        self.MM(ps[bank][:, :n], wpw[:, ct, cc * 128:(cc + 1) * 128], hsb[:, ct, :n], ct == 0, ct == CT - 1,
                                [wpw_b, hs_b], [ps_b[bank]])
                    oi = cnt["ob"] % 2; cnt["ob"] += 1
                    self.TS(V, ob[oi][:, :n], ps[bank][:, :n], pv[:, col["conv_b_pw"] + cc:col["conv_b_pw"] + cc + 1], None, ALU.add, None,
                            [ps_b[bank], self.pv_b], [ob_b[oi]])
                    self.DMA("gpsimd", f"cv_ob{oi}", S["mixT"][AW + SW + cc * 128:AW + SW + (cc + 1) * 128, s0:s0 + n], ob[oi][:, :n],
                             [ob_b[oi]], [])
            P.barrier()

    def out_proj(self, l, hin, hout, S):
        nc, P, cfg = self.nc, self.P, self.cfg
        ps, ps_b = self.ps, self.ps_b
        KT = cfg.KT
        T = 1024
        TM = min(T, cfg.LP)
        wout = self.w[l]["w_out"]
        with ExitStack() as st:
            xT = [self.sb(st, f"op_x{i}", [128, KT, TM], BF16) for i in range(2)]
            wt = [self.sb(st, f"op_w{i}", [128, KT, 128], BF16) for i in range(3)]
            hp = [self.sb(st, f"op_hp{i}", [128, TM], F32) for i in range(3)]
            ob = [self.sb(st, f"op_ob{i}", [128, TM], F32) for i in range(2)]
            xT_b, wt_b, hp_b, ob_b = P.bufs(2), P.bufs(3), P.bufs(3), P.bufs(2)
            cnt = 0
            for bi, (s0, n) in enumerate(cfg.blocks(T)):
                subs = subtiles(n)
                xi = bi % 2
                self.DMA("sync", f"op_x{xi}", xT[xi][:, :, :n], S["mixT"][:, s0:s0 + n].rearrange("(k p) t -> p k t", p=128), [], [xT_b[xi]])
                for d in range(KT):
                    wi = cnt % 3
                    i = cnt % 3
                    oi = cnt % 2
                    cnt += 1
                    self.DMA("sync", f"op_w{wi}", wt[wi][:, :, :], wout[:, :, d * 128:(d + 1) * 128], [self.w_b], [wt_b[wi]])
                    self.DMA("sync", f"op_hp{i}", hp[i][:, :n], hin[d * 128:(d + 1) * 128, s0:s0 + n], [], [hp_b[i]])
                    pb = (d % 2) * 2
                    for k in range(KT):
                        for si, (o, w) in enumerate(subs):
                            self.MM(ps[pb + si][:, :w], wt[wi][:, k, :], xT[xi][:, k, o:o + w], k == 0, k == KT - 1,
                                    [wt_b[wi], xT_b[xi]], [ps_b[pb + si]])
                    for si, (o, w) in enumerate(subs):
                        self.TT("vector", ob[oi][:, o:o + w], ps[pb + si][:, :w], hp[i][:, o:o + w], ALU.add,
                                [ps_b[pb + si], hp_b[i]], [ob_b[oi]])
                    self.DMA("gpsimd", f"op_ob{oi}", hout[d * 128:(d + 1) * 128, s0:s0 + n], ob[oi][:, :n], [ob_b[oi]], [])
            P.barrier()


WEIGHT_NAMES = ["ffn1_norm", "ffn1_w_gate", "ffn1_w_up", "ffn1_w_down", "mix_norm", "w_in", "q_norm", "k_norm",
                "ssm_lambda_re", "ssm_lambda_im", "ssm_log_step", "ssm_b_re", "ssm_b_im", "ssm_c_re", "ssm_c_im", "ssm_d",
                "ssm_w_glu", "ssm_b_glu", "conv_w_dw", "conv_b_dw", "conv_ln_g", "conv_ln_b", "conv_w_pw", "conv_b_pw",
                "w_out", "ffn2_norm", "ffn2_w_gate", "ffn2_w_up", "ffn2_w_down", "post_norm"]


def build(cfg, shapes, debug=False, stop_after=None):
    nc = bass.Bass("TRN2", target_bir_lowering=False)
    ins = {nm: nc.dram_tensor(nm, list(shapes[nm]), F32, kind="ExternalInput").ap() for nm in WEIGHT_NAMES}
    x0 = nc.dram_tensor("x0", [cfg.LP, cfg.D], F32, kind="ExternalInput").ap()
    out = nc.dram_tensor("out", [cfg.LP, cfg.D], F32, kind="ExternalOutput").ap()
    knd = "ExternalOutput" if debug else "Internal"
    hA = nc.dram_tensor("hA", [cfg.D, cfg.LP], F32, kind=knd).ap()
    hB = nc.dram_tensor("hB", [cfg.D, cfg.LP], F32, kind=knd).ap()
    S = {
        "qT": nc.dram_tensor("s_qT", [cfg.AW, cfg.LP], BF16, kind=knd).ap(),
        "kT": nc.dram_tensor("s_kT", [cfg.AW, cfg.LP], BF16, kind=knd).ap(),
        "v": nc.dram_tensor("s_v", [cfg.LP, cfg.AW], BF16, kind=knd).ap(),
        "uT": nc.dram_tensor("s_uT", [cfg.SW, cfg.LP], BF16, kind=knd).ap(),
        "gluT": nc.dram_tensor("s_gluT", [cfg.CW, cfg.LP], BF16, kind=knd).ap(),
        "mixT": nc.dram_tensor("s_mixT", [cfg.MIX, cfg.LP], BF16, kind=knd).ap(),
    }
    with ExitStack() as st:
        k = K(nc, cfg, st, debug)
        k.consts()
        k.load_params(ins)
        k.cast_weights(ins)
        k.transpose_in(x0, hA)
        cur, oth = hA, hB
        done = False
        for l in range(cfg.depth):
            w = k.w[l]
            k.ffn(cur, [], oth, [], w["ffn1_w_gate"], w["ffn1_w_up"], w["ffn1_w_down"], k.w_b0 if l == 0 else k.w_b,
                  k.gain(l, "ffn1_norm"), f"a{l}")
            cur, oth = oth, cur
            if stop_after == ("ffn1", l):
                break
            k.in_proj(l, cur, S)
            if stop_after == ("in_proj", l):
                break
            k.attention(l, S)
            k.ssm(l, ins, S)
            k.conv(l, S)
            if stop_after == ("mix", l):
                break
            k.out_proj(l, cur, oth, S)
            cur, oth = oth, cur
            if stop_after == ("out_proj", l):
                break
            k.ffn(cur, [], oth, [], w["ffn2_w_gate"], w["ffn2_w_up"], w["ffn2_w_down"], k.w_b, k.gain(l, "ffn2_norm"), f"b{l}")
            cur, oth = oth, cur
            if l == cfg.depth - 1:
                k.post_norm(cur, k.gain(l, "post_norm"), out_tok=out)
            else:
                k.post_norm(cur, k.gain(l, "post_norm"), hout=oth)
                cur, oth = oth, cur
        k.P.barrier()
        k.P.emit()
    return nc, k


SEQ, NMETA, BATCH = 8192, 16, 4
_CACHE = {}


def kernel(**inputs):
    cfg = Cfg()
    x = np.asarray(inputs["x"], dtype=np.float32)
    meta = np.asarray(inputs["meta"], dtype=np.float32)
    shapes = {nm: inputs[nm].shape for nm in WEIGHT_NAMES}
    if "nc" not in _CACHE:
        _CACHE["nc"] = build(cfg, shapes)[0]
    nc = _CACHE["nc"]
    wts = {nm: np.ascontiguousarray(np.asarray(inputs[nm], dtype=np.float32)) for nm in WEIGHT_NAMES}
    in_maps = []
    real = {0: 0, 1: 1, 4: 2, 5: 3}
    zeros = np.zeros((cfg.LP, cfg.D), np.float32)
    for core in range(8):
        m = dict(wts)
        if core in real:
            x0 = np.zeros((cfg.LP, cfg.D), np.float32)
            x0[:NMETA] = meta
            x0[NMETA:NMETA + SEQ] = x[real[core]]
            m["x0"] = x0
        else:
            m["x0"] = zeros
        in_maps.append(m)
    res = run_bass_kernel_spmd(nc, in_maps, core_ids=list(range(8)))
    outs = [res.results[cidx]["out"][NMETA:NMETA + SEQ] for cidx in (0, 1, 4, 5)]
    return np.stack(outs, axis=0).astype(np.float32)
```

```python
import numpy as np
from contextlib import ExitStack
import concourse.bass as bass
import concourse.mybir as mybir
from concourse.bass_utils import run_bass_kernel_spmd

F32 = mybir.dt.float32
BF16 = mybir.dt.bfloat16
ALU = mybir.AluOpType
AF = mybir.ActivationFunctionType

ENGS = ("tensor", "vector", "scalar", "gpsimd", "sync")


class Buf:
    __slots__ = ("name", "last_w", "readers")

    def __init__(self, name):
        self.name = name
        self.last_w = None
        self.readers = []


class Prog:
    def __init__(self, nc):
        self.nc = nc
        self.ops = {e: [] for e in ENGS}
        self.dma_cnt = {}
        self.nbuf = 0

    def buf(self, name=None):
        self.nbuf += 1
        return Buf(name or f"b{self.nbuf}")

    def bufs(self, n, name="b"):
        return [self.buf(f"{name}{i}") for i in range(n)]

    def _deps(self, reads, writes):
        deps = []
        for b in reads:
            if b.last_w is not None:
                deps.append(b.last_w)
        for b in writes:
            if b.last_w is not None:
                deps.append(b.last_w)
            deps.extend(b.readers)
        return deps

    def _commit(self, tok, reads, writes):
        for b in reads:
            if tok[0] == "c":
                b.readers = [t for t in b.readers if not (t[0] == "c" and t[1] == tok[1])]
            b.readers.append(tok)
        for b in writes:
            b.last_w = tok
            b.readers = []

    def op(self, eng, fn, reads=(), writes=()):
        deps = self._deps(reads, writes)
        if eng == "tensor":
            deps = [t for t in deps if not (t[0] == "c" and t[1] == "tensor")]
        idx = len(self.ops[eng])
        tok = ("c", eng, idx)
        self.ops[eng].append([fn, deps, None])
        self._commit(tok, reads, writes)
        return tok

    def dma(self, eng, sem, fn, reads=(), writes=()):
        deps = self._deps(reads, writes)
        v = self.dma_cnt.get(sem, 0) + 16
        self.dma_cnt[sem] = v
        tok = ("d", sem, v)
        self.ops[eng].append([fn, deps, sem])
        self._commit(tok, reads, writes)
        return tok

    def barrier(self, bufs=(), exclude=()):
        toks = []
        for e in ENGS:
            if self.ops[e]:
                for i in range(len(self.ops[e]) - 1, -1, -1):
                    if self.ops[e][i][2] is None and self.ops[e][i][0] is not None:
                        toks.append(("c", e, i))
                        break
        for s, v in self.dma_cnt.items():
            if s not in exclude:
                toks.append(("d", s, v))
        for e in ENGS:
            self.ops[e].append([None, list(toks), None])

    def emit(self):
        nc = self.nc
        need = {e: set() for e in ENGS}
        for e in ENGS:
            for fn, deps, sem in self.ops[e]:
                for t in deps:
                    if t[0] == "c":
                        if t[1] == e and e == "tensor":
                            continue
                        need[t[1]].add(t[2])
        ms = {e: {} for e in ENGS}
        for e in ENGS:
            for k, i in enumerate(sorted(need[e])):
                ms[e][i] = k + 1
        self.ms_max = {e: len(ms[e]) for e in ENGS}
        with ExitStack() as st:
            csem = {e: st.enter_context(nc.semaphore(f"c_{e}")) for e in ENGS}
            dsem = {s: st.enter_context(nc.semaphore(f"d_{s}")) for s in self.dma_cnt}
            block = st.enter_context(nc.Block())

            def run(e, eng):
                waited = {}
                for i, (fn, deps, sem) in enumerate(self.ops[e]):
                    for t in deps:
                        if t[0] == "c":
                            if t[1] == e and e == "tensor":
                                continue
                            key, val, h = ("c", t[1]), ms[t[1]][t[2]], csem[t[1]]
                        else:
                            key, val, h = ("d", t[1]), t[2], dsem[t[1]]
                        if waited.get(key, 0) >= val:
                            continue
                        waited[key] = val
                        eng.wait_ge(h, val)
                    if fn is None:
                        continue
                    inst = fn(eng)
                    if sem is not None:
                        inst.then_inc(dsem[sem], 16)
                    elif i in ms[e]:
                        inst.then_inc(csem[e], 1)

            @block.tensor
            def _(eng):
                run("tensor", eng)

            @block.vector
            def _(eng):
                run("vector", eng)

            @block.scalar
            def _(eng):
                run("scalar", eng)

            @block.gpsimd
            def _(eng):
                run("gpsimd", eng)

            @block.sync
            def _(eng):
                run("sync", eng)


class Cfg:
    def __init__(self, D=2048, DFF=5632, H=8, G=32, CW=1024, LP=8320, depth=2):
        self.D, self.DFF, self.H, self.G, self.CW, self.LP, self.depth = D, DFF, H, G, CW, LP, depth
        self.KT = D // 128
        self.MT = DFF // 128
        self.AW = H * 64
        self.AT = self.AW // 128
        self.SW = G * 16
        self.ST = self.SW // 128
        self.CT = CW // 128
        self.IN = 3 * self.AW + self.SW + 2 * CW
        self.MIX = self.AW + self.SW + CW
        assert self.MIX == D
        self.NTT = LP // 128

    def blocks(self, T=1024):
        out, s = [], 0
        while s < self.LP:
            n = min(T, self.LP - s)
            out.append((s, n))
            s += n
        return out


def subtiles(n, maxn=512):
    k = (n + maxn - 1) // maxn
    base = n // k
    assert base * k == n
    return [(i * base, base) for i in range(k)]


PI = 3.14159265358979
TWO_PI = 6.28318530717959
SC2PI = 6.28316
HALFPI_ = 1.57079


class K:
    def __init__(self, nc, cfg, st, debug=False):
        self.nc, self.cfg, self.st, self.debug = nc, cfg, st, debug
        self.P = Prog(nc)
        self.pq = [st.enter_context(nc.psum_tensor(f"pq{i}", [128, 1024], F32)) for i in range(4)]
        self.ps = [self.pq[i // 2][:, (i % 2) * 512:(i % 2 + 1) * 512] for i in range(8)]
        self.ps_b = self.P.bufs(8, "ps")

    def sb(self, st, name, shape, dt):
        self.uid = getattr(self, "uid", 0) + 1
        return st.enter_context(self.nc.sbuf_tensor(f"{name}_{self.uid}", shape, dt))

    def MM(self, out, lhsT, rhs, start, stop, R, W):
        self.P.op("tensor", lambda e: e.matmul(out=out, lhsT=lhsT, rhs=rhs, start=start, stop=stop), R, W)

    def TR(self, out, in_, ident, R, W):
        self.P.op("tensor", lambda e: e.transpose(out=out, in_=in_, identity=ident), R, W)

    def ACT(self, out, in_, func, R, W, bias=None, scale=None):
        kw = {}
        if bias is not None:
            kw["bias"] = bias
        if scale is not None:
            kw["scale"] = scale
        self.P.op("scalar", lambda e: e.activation(out=out, in_=in_, func=func, **kw), R, W)

    def TS(self, eng, out, in0, s1, s2, op0, op1, R, W):
        if op1 is None:
            self.P.op(eng, lambda e: e.tensor_scalar(out=out, in0=in0, scalar1=s1, scalar2=None, op0=op0), R, W)
        else:
            self.P.op(eng, lambda e: e.tensor_scalar(out=out, in0=in0, scalar1=s1, scalar2=s2, op0=op0, op1=op1), R, W)

    def RSQ(self, out, in_, eps, R, W):
        self.P.op("scalar", lambda e: e.activation(out=out, in_=in_, func=AF.Sqrt, bias=self.epsb[eps][:, 0:1]), R + [self.cb], W)
        self.P.op("vector", lambda e: e.reciprocal(out=out, in_=out), W, W)

    def FRAC(self, out, y, ti, tf, R, W, e1="gpsimd"):
        self.CP(e1, ti, y, R + W, W)
        self.CP(e1, tf, ti, W, W)
        self.TT(e1, tf, y, tf, ALU.subtract, R + W, W)
        self.STT("vector", out, tf, 0.5, tf, ALU.is_gt, ALU.subtract, W, W)
        self.STT("vector", out, out, 0.5, out, ALU.is_gt, ALU.subtract, W, W)

    def TT(self, eng, out, in0, in1, op, R, W):
        self.P.op(eng, lambda e: e.tensor_tensor(out=out, in0=in0, in1=in1, op=op), R, W)

    def STT(self, eng, out, in0, scalar, in1, op0, op1, R, W):
        self.P.op(eng, lambda e: e.scalar_tensor_tensor(out=out, in0=in0, scalar=scalar, in1=in1, op0=op0, op1=op1), R, W)

    def CP(self, eng, out, in_, R, W):
        if eng == "scalar":
            self.P.op(eng, lambda e: e.activation(out=out, in_=in_, func=AF.Copy), R, W)
        else:
            self.P.op(eng, lambda e: e.tensor_copy(out=out, in_=in_), R, W)

    def MS(self, eng, ap, val, W):
        self.P.op(eng, lambda e: e.memset(ap, val), (), W)

    def DMA(self, q, sem, out, in_, R, W, slow=False):
        if slow:
            self.P.dma(q, sem, lambda e: e.dma_start(out=out, in_=in_, allow_slow_non_contiguous=True), R, W)
        else:
            self.P.dma(q, sem, lambda e: e.dma_start(out=out, in_=in_), R, W)

    def consts(self):
        nc, P, st, cfg = self.nc, self.P, self.st, self.cfg
        c = self.c = {}
        cb = self.cb = P.buf("consts")
        W = [cb]
        self.epsb = {}
        for ev in (1e-6, 1e-5):
            t = self.sb(st, f"eps{len(self.epsb)}", [128, 1], F32)
            self.MS("vector", t[:, :], ev, W)
            self.epsb[ev] = t
        self.hpib = self.sb(st, "hpib", [128, 1], F32)
        self.MS("vector", self.hpib[:, :], HALFPI_, W)
        onesf = self.sb(st, "onesf", [128, 512], F32)
        self.MS("vector", onesf[:, :], 1.0, W)
        identf = c["identf"] = self.sb(st, "identf", [128, 128], F32)
        P.op("gpsimd", lambda e: e.affine_select(out=identf[:, :], in_=onesf[:, :128], pattern=[[-1, 128]],
                                                compare_op=ALU.is_equal, fill=0.0, base=0, channel_multiplier=1), [cb], W)
        identb = c["identb"] = self.sb(st, "identb", [128, 128], BF16)
        self.CP("vector", identb[:, :], identf[:, :], [cb], W)
        for nm, val in (("onesD", 1.0 / cfg.D), ("onesCW", 1.0 / cfg.CW), ("negones", -1.0)):
            t = c[nm] = self.sb(st, nm, [128, 128], BF16)
            self.MS("vector", t[:, :], val, W)
        o64 = c["ones64"] = self.sb(st, "ones64", [128, 128], BF16)
        self.MS("vector", o64[:, :], 0.0, W)
        self.MS("vector", o64[0:64, 0:64], 1.0 / 64, W)
        self.MS("vector", o64[64:128, 64:128], 1.0 / 64, W)
        tnf = self.sb(st, "tnf", [128, 128], F32)
        P.op("gpsimd", lambda e: e.affine_select(out=tnf[:, :], in_=onesf[:, :128], pattern=[[-1, 128]],
                                                compare_op=ALU.is_ge, fill=0.0, base=0, channel_multiplier=1), [cb], W)
        tneg = c["tneg"] = self.sb(st, "tneg", [128, 128], BF16)
        self.TS("vector", tneg[:, :], tnf[:, :], -1.0, None, ALU.mult, None, [cb], W)
        c["keep"], c["negm"] = [], []
        kf = self.sb(st, "keepf", [128, 512], F32)
        for r in range(4):
            P.op("gpsimd", lambda e, r=r: e.affine_select(out=kf[:, :], in_=onesf[:, :], pattern=[[1, 512]],
                                                         compare_op=ALU.is_gt, fill=0.0, base=-r * 128,
                                                         channel_multiplier=-1), [cb], W)
            kp = self.sb(st, f"keep{r}", [128, 512], BF16)
            ng = self.sb(st, f"negm{r}", [128, 512], BF16)
            self.CP("vector", kp[:, :], kf[:, :], [cb], W)
            self.TS("vector", ng[:, :], kf[:, :], -1.0, 30000.0, ALU.add, ALU.mult, [cb], W)
            c["keep"].append(kp)
            c["negm"].append(ng)

    def norm_block(self, hT, hbs, s0, n, gain, xT, xT_b, rstd, rstd_b, hp, hp_b, sq, sq_b, cnt, tag, f32out=None):
        cfg = self.cfg
        KT = cfg.KT
        ps, ps_b, c, cb = self.ps, self.ps_b, self.c, self.cb
        subs = subtiles(n)
        NHP = len(hp)
        for k in range(KT):
            i = cnt["hp"] % NHP; cnt["hp"] += 1
            self.DMA("gpsimd", f"Fhp{i}", hp[i][:, :n], hT[k * 128:(k + 1) * 128, s0:s0 + n], hbs, [hp_b[i]])
            j = cnt["sq"] % 2; cnt["sq"] += 1
            self.ACT(sq[j][:, :n], hp[i][:, :n], AF.Square, [hp_b[i]], [sq_b[j]])
            for si, (o, w) in enumerate(subs):
                self.MM(ps[si][:, :w], c["onesD"][:, :], sq[j][:, o:o + w], k == 0, k == KT - 1, [sq_b[j], cb], [ps_b[si]])
        for si, (o, w) in enumerate(subs):
            self.RSQ(rstd[:, o:o + w], ps[si][:, :w], 1e-6, [ps_b[si]], [rstd_b])
        for k in range(KT):
            i = cnt["hp"] % NHP; cnt["hp"] += 1
            self.DMA("gpsimd", f"Fhp{i}", hp[i][:, :n], hT[k * 128:(k + 1) * 128, s0:s0 + n], hbs, [hp_b[i]])
            dst = xT[:, k, :n]
            self.STT("vector", dst, hp[i][:, :n], gain[:, k:k + 1], rstd[:, :n], ALU.mult, ALU.mult,
                     [hp_b[i], rstd_b, self.pv_b], [xT_b])

    def ffn(self, hin, hin_b, hout, hout_b, wg_s, wu_s, wd_s, w_b, gain, tag, T=512):
        nc, P, cfg = self.nc, self.P, self.cfg
        KT, MT = cfg.KT, cfg.MT
        ps, ps_b = self.ps, self.ps_b
        TM = min(T, cfg.LP)
        MC = 2 if MT % 2 == 0 else 1
        with ExitStack() as st:
            xTs = [self.sb(st, f"xT{tag}{i}", [128, KT, TM], BF16) for i in range(2)]
            hid = self.sb(st, f"hid{tag}", [128, MT, TM], BF16)
            rstd = self.sb(st, f"rstd{tag}", [128, TM], F32)
            NHP, NW = 3, 2
            hp = [self.sb(st, f"T{tag}hp{i}", [128, TM], F32) for i in range(NHP)]
            sq = [self.sb(st, f"sq{tag}{i}", [128, TM], BF16) for i in range(2)]
            sg = [self.sb(st, f"sg{tag}{i}", [128, TM], BF16) for i in range(2)]
            ob = [self.sb(st, f"T{tag}ob{i}", [128, TM], F32) for i in range(2)]
            wg = [self.sb(st, f"T{tag}wg{i}", [128, KT, MC * 128], BF16) for i in range(NW)]
            wu = [self.sb(st, f"T{tag}wu{i}", [128, KT, MC * 128], BF16) for i in range(NW)]
            wd = [self.sb(st, f"T{tag}wd{i}", [128, MT, 128], BF16) for i in range(NW)]
            xTs_b, hid_b, rstd_b = P.bufs(2), P.buf(), P.buf()
            hp_b, sq_b, sg_b, ob_b = P.bufs(NHP), P.bufs(2), P.bufs(2), P.bufs(2)
            wg_b, wu_b, wd_b = P.bufs(NW), P.bufs(NW), P.bufs(NW)
            cnt = dict(hp=0, sq=0, sg=0, ob=0, wgu=0, wd=0)
            blocks = cfg.blocks(T)

            def norm(bi):
                s0, n = blocks[bi]
                self.norm_block(hin, [], s0, n, gain, xTs[bi % 2], xTs_b[bi % 2], rstd, rstd_b, hp, hp_b, sq, sq_b, cnt, tag)

            def gateup(bi):
                s0, n = blocks[bi]
                subs = subtiles(n)
                xT, xT_b = xTs[bi % 2], xTs_b[bi % 2]
                for mc in range(MT // MC):
                    wi = cnt["wgu"] % NW; cnt["wgu"] += 1
                    sl = slice(mc * MC * 128, (mc + 1) * MC * 128)
                    self.DMA("sync", f"Fwg{wi}", wg[wi][:, :, :], wg_s[:, :, sl], [w_b], [wg_b[wi]])
                    self.DMA("sync", f"Fwu{wi}", wu[wi][:, :, :], wu_s[:, :, sl], [w_b], [wu_b[wi]])
                    for mi in range(MC):
                        m = mc * MC + mi
                        pb = (m % 2) * 4
                        for which, (wt, wtb) in enumerate(((wg, wg_b), (wu, wu_b))):
                            for k in range(KT):
                                for si, (o, w) in enumerate(subs):
                                    bank = pb + which * 2 + si
                                    self.MM(ps[bank][:, :w], wt[wi][:, k, mi * 128:(mi + 1) * 128], xT[:, k, o:o + w],
                                            k == 0, k == KT - 1, [wtb[wi], xT_b], [ps_b[bank]])
                        j = cnt["sg"] % 2; cnt["sg"] += 1
                        for si, (o, w) in enumerate(subs):
                            self.ACT(sg[j][:, o:o + w], ps[pb + si][:, :w], AF.Silu, [ps_b[pb + si]], [sg_b[j]])
                        for si, (o, w) in enumerate(subs):
                            self.TT("vector", hid[:, m, o:o + w], ps[pb + 2 + si][:, :w], sg[j][:, o:o + w], ALU.mult,
                                    [ps_b[pb + 2 + si], sg_b[j]], [hid_b])

            def down(bi):
                s0, n = blocks[bi]
                subs = subtiles(n)
                for d in range(KT):
                    wi = cnt["wd"] % NW; cnt["wd"] += 1
                    self.DMA("sync", f"Fwd{wi}", wd[wi][:, :, :], wd_s[:, :, d * 128:(d + 1) * 128], [w_b], [wd_b[wi]])
                    i = cnt["hp"] % NHP; cnt["hp"] += 1
                    self.DMA("gpsimd", f"Fhp{i}", hp[i][:, :n], hin[d * 128:(d + 1) * 128, s0:s0 + n], [], [hp_b[i]])
                    pb = 1 + (d % 2) * 4
                    for m in range(MT):
                        for si, (o, w) in enumerate(subs):
                            self.MM(ps[pb + si][:, :w], wd[wi][:, m, :], hid[:, m, o:o + w], m == 0, m == MT - 1,
                                    [wd_b[wi], hid_b], [ps_b[pb + si]])
                    oi = cnt["ob"] % 2; cnt["ob"] += 1
                    for si, (o, w) in enumerate(subs):
                        self.STT("vector", ob[oi][:, o:o + w], ps[pb + si][:, :w], 0.5, hp[i][:, o:o + w], ALU.mult, ALU.add,
                                 [ps_b[pb + si], hp_b[i]], [ob_b[oi]])
                    self.DMA("gpsimd", f"Fob{oi}", hout[d * 128:(d + 1) * 128, s0:s0 + n], ob[oi][:, :n], [ob_b[oi]], [])

            norm(0)
            for bi in range(len(blocks)):
                gateup(bi)
                if bi + 1 < len(blocks):
                    norm(bi + 1)
                down(bi)
            P.barrier()

    def load_params(self, ins):
        nc, P, st, cfg = self.nc, self.P, self.st, self.cfg
        c, cb = self.c, self.cb
        KT, ST, CT = cfg.KT, cfg.ST, cfg.CT
        names = [("ffn1_norm", KT), ("mix_norm", KT), ("ffn2_norm", KT), ("post_norm", KT),
                 ("ssm_b_glu", ST), ("ssm_d", ST), ("conv_b_dw", CT), ("conv_ln_g", CT), ("conv_ln_b", CT),
                 ("conv_b_pw", CT)]
        R = sum(n for _, n in names)
        assert R <= 128
        self.pvcol = {}
        r = 0
        for nm, n in names:
            self.pvcol[nm] = r
            r += n
        self.pv, self.wdwT, self.qkg = [], [], []
        self.pv_b = P.buf("pv")
        ps, ps_b = self.ps, self.ps_b
        for l in range(cfg.depth):
            rows = self.sb(st, f"pvrows{l}", [128, 128], F32)
            rows_b = P.buf()
            self.MS("vector", rows[:, :], 0.0, [rows_b])
            for nm, n in names:
                src = ins[nm][l]
                if nm == "ssm_d":
                    src = src.rearrange("g h -> (g h)")
                r0 = self.pvcol[nm]
                self.DMA("sync", f"pvl", rows[r0:r0 + n, :], src.rearrange("(n p) -> n p", p=128), [], [rows_b])
            self.TR(ps[0][:, :128], rows[:, :], c["identf"][:, :], [rows_b, cb], [ps_b[0]])
            pv = self.sb(st, f"pv{l}", [128, 128], F32)
            self.CP("vector", pv[:, :], ps[0][:, :128], [ps_b[0]], [self.pv_b])
            self.pv.append(pv)
            wrows = self.sb(st, f"wdwrows{l}", [32, cfg.CW], F32)
            wrows_b = P.buf()
            self.DMA("sync", "pvw", wrows[0:31, :], ins["conv_w_dw"][l], [], [wrows_b])
            wT = self.sb(st, f"wdwT{l}", [128, CT, 32], F32)
            for ct in range(CT):
                self.TR(ps[1][:, :31], wrows[0:31, ct * 128:(ct + 1) * 128], c["identf"][0:31, 0:31], [wrows_b, cb], [ps_b[1]])
                self.CP("vector", wT[:, ct, 0:31], ps[1][:, :31], [ps_b[1]], [self.pv_b])
            self.wdwT.append(wT)
            qg = self.sb(st, f"qkg{l}", [128, 2], F32)
            for hf in range(2):
                self.DMA("sync", "pvq", qg[hf * 64:(hf + 1) * 64, 0:1], ins["q_norm"][l].rearrange("(p o) -> p o", o=1), [], [self.pv_b])
                self.DMA("sync", "pvq", qg[hf * 64:(hf + 1) * 64, 1:2], ins["k_norm"][l].rearrange("(p o) -> p o", o=1), [], [self.pv_b])
            self.TS("vector", qg[:, 0:1], qg[:, 0:1], 0.125, None, ALU.mult, None, [self.pv_b], [self.pv_b])
            self.qkg.append(qg)

    def gain(self, l, nm):
        c0 = self.pvcol[nm]
        return self.pv[l][:, c0:128]

    def cast_weights(self, ins):
        nc, cfg = self.nc, self.cfg
        self.w = []
        self.w_b = self.P.buf("wcast")
        specs = [("ffn1_w_gate", cfg.KT, cfg.DFF), ("ffn1_w_up", cfg.KT, cfg.DFF), ("ffn1_w_down", cfg.MT, cfg.D),
                 ("w_in", cfg.KT, cfg.IN), ("ssm_w_glu", cfg.ST, cfg.SW), ("conv_w_pw", cfg.CT, cfg.CW),
                 ("w_out", cfg.KT, cfg.D),
                 ("ffn2_w_gate", cfg.KT, cfg.DFF), ("ffn2_w_up", cfg.KT, cfg.DFF), ("ffn2_w_down", cfg.MT, cfg.D)]
        self.w_b0 = self.P.buf("wcast0")
        for l in range(cfg.depth):
            d = {}
            for si_, (nm, kt, n) in enumerate(specs):
                t = nc.dram_tensor(f"s_{nm}{l}", [128, kt, n], BF16).ap()
                d[nm] = t
                first = (l == 0 and si_ < 3)
                for k in range(kt):
                    self.DMA("gpsimd", "wcast0" if first else "wcast", t[:, k, :], ins[nm][l, k * 128:(k + 1) * 128, :], [],
                             [self.w_b0 if first else self.w_b])
            self.w.append(d)

    def transpose_in(self, x0, hT):
        nc, P, cfg = self.nc, self.P, self.cfg
        ps, ps_b, c, cb = self.ps, self.ps_b, self.c, self.cb
        KT = cfg.KT
        with ExitStack() as st:
            xin = [self.sb(st, f"xin{i}", [128, cfg.D], F32) for i in range(2)]
            xo = [self.sb(st, f"xo{i}", [128, KT, 512], F32) for i in range(2)]
            xin_b, xo_b = P.bufs(2), P.bufs(2)
            nb = 0
            for (s0, n) in cfg.blocks(512):
                oi = nb % 2; nb += 1
                for tt in range(n // 128):
                    i = (s0 // 128 + tt) % 2
                    self.DMA("sync", f"xin{i}", xin[i][:, :], x0[s0 + tt * 128:s0 + (tt + 1) * 128, :], [], [xin_b[i]])
                    for k in range(KT):
                        bank = k % 8
                        self.TR(ps[bank][:, :128], xin[i][:, k * 128:(k + 1) * 128], c["identf"][:, :], [xin_b[i], cb], [ps_b[bank]])
                        self.CP("vector" if k % 2 == 0 else "scalar", xo[oi][:, k, tt * 128:(tt + 1) * 128], ps[bank][:, :128],
                                [ps_b[bank]], [xo_b[oi]])
                self.DMA("sync", f"xo{oi}", hT[:, s0:s0 + n].rearrange("(k p) t -> p k t", p=128), xo[oi][:, :, :n], [xo_b[oi]], [])
            P.barrier(exclude=("wcast", "wcast0"))

    def post_norm(self, hin, gain, hout=None, out_tok=None):
        nc, P, cfg = self.nc, self.P, self.cfg
        ps, ps_b, c, cb = self.ps, self.ps_b, self.c, self.cb
        KT = cfg.KT
        T = 512
        with ExitStack() as st:
            rstd = self.sb(st, "pn_rstd", [128, T], F32)
            hp = [self.sb(st, f"pn_hp{i}", [128, T], F32) for i in range(3)]
            sq = [self.sb(st, f"pn_sq{i}", [128, T], BF16) for i in range(2)]
            yb = [self.sb(st, f"pn_y{i}", [128, T], F32) for i in range(3)]
            ot = [self.sb(st, f"pn_ot{i}", [128, cfg.D], F32) for i in range(4)]
            rstd_b = P.buf()
            hp_b, sq_b, yb_b, ot_b = P.bufs(3), P.bufs(2), P.bufs(3), P.bufs(4)
            cnt = dict(hp=0, sq=0, y=0)
            for (s0, n) in cfg.blocks(T):
                subs = subtiles(n)
                for k in range(KT):
                    i = cnt["hp"] % 3; cnt["hp"] += 1
                    self.DMA("sync", f"pnhp{i}", hp[i][:, :n], hin[k * 128:(k + 1) * 128, s0:s0 + n], [], [hp_b[i]])
                    j = cnt["sq"] % 2; cnt["sq"] += 1
                    self.ACT(sq[j][:, :n], hp[i][:, :n], AF.Square, [hp_b[i]], [sq_b[j]])
                    self.MM(ps[0][:, :n], c["onesD"][:, :], sq[j][:, :n], k == 0, k == KT - 1, [sq_b[j], cb], [ps_b[0]])
                self.RSQ(rstd[:, :n], ps[0][:, :n], 1e-6, [ps_b[0]], [rstd_b])
                ntt = n // 128
                for k in range(KT):
                    i = cnt["hp"] % 3; cnt["hp"] += 1
                    self.DMA("sync", f"pnhp{i}", hp[i][:, :n], hin[k * 128:(k + 1) * 128, s0:s0 + n], [], [hp_b[i]])
                    yi = cnt["y"] % 3; cnt["y"] += 1
                    self.STT("vector", yb[yi][:, :n], hp[i][:, :n], gain[:, k:k + 1], rstd[:, :n], ALU.mult, ALU.mult,
                             [hp_b[i], rstd_b, self.pv_b], [yb_b[yi]])
                    if hout is not None:
                        self.DMA("gpsimd", f"pny{yi}", hout[k * 128:(k + 1) * 128, s0:s0 + n], yb[yi][:, :n], [yb_b[yi]], [])
                    if out_tok is not None:
                        for tt in range(ntt):
                            bank = 1 + (k * ntt + tt) % 7
                            self.TR(ps[bank][:, :128], yb[yi][:, tt * 128:(tt + 1) * 128], c["identf"][:, :], [yb_b[yi], cb], [ps_b[bank]])
                            self.CP("scalar" if tt % 2 == 0 else "vector", ot[tt][:, k * 128:(k + 1) * 128], ps[bank][:, :128],
                                    [ps_b[bank]], [ot_b[tt]])
                if out_tok is not None:
                    for tt in range(ntt):
                        self.DMA("gpsimd", f"pnot{tt}", out_tok[s0 + tt * 128:s0 + (tt + 1) * 128, :], ot[tt][:, :], [ot_b[tt]], [])
            P.barrier()

    def in_proj(self, l, hin, S):
        nc, P, cfg = self.nc, self.P, self.cfg
        ps, ps_b, c, cb = self.ps, self.ps_b, self.c, self.cb
        KT, AT, ST, CT, AW, SW, CW = cfg.KT, cfg.AT, cfg.ST, cfg.CT, cfg.AW, cfg.SW, cfg.CW
        T = 1024
        TM = min(T, cfg.LP)
        win = self.w[l]["w_in"]
        gain = self.gain(l, "mix_norm")
        qg = self.qkg[l]
        tag = f"ip{l}"
        with ExitStack() as st:
            xT = self.sb(st, "ip_xT", [128, KT, TM], BF16)
            rstd = self.sb(st, "ip_rstd", [128, TM], F32)
            hp = [self.sb(st, f"ip_hp{i}", [128, TM], F32) for i in range(3)]
            sq = [self.sb(st, f"ip_sq{i}", [128, TM], BF16) for i in range(2)]
            wt = [self.sb(st, f"ip_w{i}", [128, KT, 256], BF16) for i in range(3)]
            rs = [self.sb(st, f"ip_rs{i}", [128, TM], F32) for i in range(2)]
            ob = [self.sb(st, f"ip_ob{i}", [128, TM], BF16) for i in range(3)]
            ab = [self.sb(st, f"ip_ab{i}", [128, TM], F32) for i in range(2)]
            sg = [self.sb(st, f"ip_sg{i}", [128, TM], F32) for i in range(2)]
            vb = [self.sb(st, f"ip_vb{i}", [128, 256], BF16) for i in range(3)]
            xT_b, rstd_b = P.buf(), P.buf()
            hp_b, sq_b, wt_b, rs_b, ob_b, ab_b, sg_b, vb_b = (P.bufs(3), P.bufs(2), P.bufs(3), P.bufs(2), P.bufs(3),
                                                               P.bufs(2), P.bufs(2), P.bufs(3))
            cnt = dict(hp=0, sq=0, w=0, rs=0, ob=0, sg=0, vb=0, t=0)

            def load_w(col0):
                wi = cnt["w"] % 3; cnt["w"] += 1
                self.DMA("sync", f"ipw{wi}", wt[wi][:, :, :], win[:, :, col0:col0 + 256], [self.w_b], [wt_b[wi]])
                return wi

            for (s0, n) in cfg.blocks(T):
                subs = subtiles(n)
                self.norm_block(hin, [], s0, n, gain, xT, xT_b, rstd, rstd_b, hp, hp_b, sq, sq_b, cnt, tag)

                def col_tile(wi, mi):
                    pb = (cnt["t"] % 2) * 2; cnt["t"] += 1
                    for k in range(KT):
                        for si, (o, w) in enumerate(subs):
                            self.MM(ps[pb + si][:, :w], wt[wi][:, k, mi * 128:(mi + 1) * 128], xT[:, k, o:o + w],
                                    k == 0, k == KT - 1, [wt_b[wi], xT_b], [ps_b[pb + si]])
                    return pb

                def new_ob():
                    oi = cnt["ob"] % 3; cnt["ob"] += 1
                    return oi

                for which, (dst, col0) in enumerate(((S["qT"], 0), (S["kT"], AW))):
                    for ch in range(AT // 2 if AT >= 2 else 1):
                        ntile = 2 if AT >= 2 else 1
                        wi = load_w(col0 + ch * 256) if ntile == 2 else load_w(col0)
                        for mi in range(ntile):
                            tix = ch * 2 + mi
                            pb = col_tile(wi, mi)
                            j = cnt["sq"] % 2; cnt["sq"] += 1
                            for si, (o, w) in enumerate(subs):
                                self.ACT(sq[j][:, o:o + w], ps[pb + si][:, :w], AF.Square, [ps_b[pb + si]], [sq_b[j]])
                            ri = cnt["rs"] % 2; cnt["rs"] += 1
                            for si, (o, w) in enumerate(subs):
                                self.MM(ps[4 + pb + si][:, :w], c["ones64"][:, :], sq[j][:, o:o + w], True, True,
                                        [sq_b[j], cb], [ps_b[4 + pb + si]])
                                self.RSQ(rs[ri][:, o:o + w], ps[4 + pb + si][:, :w], 1e-6, [ps_b[4 + pb + si]], [rs_b[ri]])
                            oi = new_ob()
                            for si, (o, w) in enumerate(subs):
                                self.STT("vector", ob[oi][:, o:o + w], ps[pb + si][:, :w], qg[:, which:which + 1], rs[ri][:, o:o + w],
                                         ALU.mult, ALU.mult, [ps_b[pb + si], rs_b[ri], self.pv_b], [ob_b[oi]])
                            self.DMA("gpsimd", f"ipob{oi}", dst[tix * 128:(tix + 1) * 128, s0:s0 + n], ob[oi][:, :n], [ob_b[oi]], [])
                for ch in range(max(1, AW // 256)):
                    ncol = min(256, AW)
                    wi = load_w(2 * AW + ch * 256)
                    for tt in range(n // 128):
                        bank = cnt["vb"] % 4
                        for k in range(KT):
                            self.MM(ps[bank][:, :ncol], xT[:, k, tt * 128:(tt + 1) * 128], wt[wi][:, k, :ncol],
                                    k == 0, k == KT - 1, [wt_b[wi], xT_b], [ps_b[bank]])
                        vi = cnt["vb"] % 3; cnt["vb"] += 1
                        self.ACT(vb[vi][:, :ncol], ps[bank][:, :ncol], AF.Copy, [ps_b[bank]], [vb_b[vi]])
                        self.DMA("gpsimd", f"ipvb{vi}", S["v"][s0 + tt * 128:s0 + (tt + 1) * 128, ch * 256:ch * 256 + ncol],
                                 vb[vi][:, :ncol], [vb_b[vi]], [])
                for ch in range(max(1, SW // 256)):
                    ntile = min(2, ST)
                    wi = load_w(3 * AW + ch * 256)
                    for mi in range(ntile):
                        tix = ch * 2 + mi
                        pb = col_tile(wi, mi)
                        oi = new_ob()
                        for si, (o, w) in enumerate(subs):
                            self.ACT(ob[oi][:, o:o + w], ps[pb + si][:, :w], AF.Copy, [ps_b[pb + si]], [ob_b[oi]])
                        self.DMA("gpsimd", f"ipob{oi}", S["uT"][tix * 128:(tix + 1) * 128, s0:s0 + n], ob[oi][:, :n], [ob_b[oi]], [])
                for ch in range(CT // 2):
                    wa = load_w(3 * AW + SW + ch * 256)
                    for mi in range(2):
                        pb = col_tile(wa, mi)
                        for si, (o, w) in enumerate(subs):
                            self.ACT(ab[mi][:, o:o + w], ps[pb + si][:, :w], AF.Copy, [ps_b[pb + si]], [ab_b[mi]])
                    wgi = load_w(3 * AW + SW + CW + ch * 256)
                    for mi in range(2):
                        tix = ch * 2 + mi
                        pb = col_tile(wgi, mi)
                        si_ = cnt["sg"] % 2; cnt["sg"] += 1
                        for si, (o, w) in enumerate(subs):
                            self.ACT(sg[si_][:, o:o + w], ps[pb + si][:, :w], AF.Sigmoid, [ps_b[pb + si]], [sg_b[si_]])
                        oi = new_ob()
                        self.TT("vector", ob[oi][:, :n], ab[mi][:, :n], sg[si_][:, :n], ALU.mult, [ab_b[mi], sg_b[si_]], [ob_b[oi]])
                        self.DMA("gpsimd", f"ipob{oi}", S["gluT"][tix * 128:(tix + 1) * 128, s0:s0 + n], ob[oi][:, :n], [ob_b[oi]], [])
            P.barrier()

    def attention(self, l, S):
        nc, P, cfg = self.nc, self.P, self.cfg
        ps, ps_b, c, cb = self.ps, self.ps_b, self.c, self.cb
        LP, NTT, AT = cfg.LP, cfg.NTT, cfg.AT
        qblocks = cfg.blocks(512)
        with ExitStack() as st:
            kT = self.sb(st, "at_kT", [128, LP], BF16)
            qT = self.sb(st, "at_qT", [128, LP], BF16)
            vz = [self.sb(st, f"at_vz{h}", [128, NTT, 128], BF16) for h in range(2)]
            aT = self.sb(st, "at_aT", [128, LP], BF16)
            NS, DEP = 3, 1
            pq = self.pq
            eb = [self.sb(st, f"at_e{h}", [128, 2, 512], F32) for h in range(NS)]
            lb = [self.sb(st, f"at_l{h}", [128, 2, 512], BF16) for h in range(NS)]
            cum = self.sb(st, "at_cum", [128, 2, 512], BF16)
            att = [self.sb(st, f"at_a{h}", [128, 2, 512], BF16) for h in range(3)]
            kT_b, qT_b, aT_b, cum_b = P.buf(), P.buf(), P.buf(), P.buf()
            vz_b, eb_b, lb_b, att_b = P.bufs(2), P.bufs(NS), P.bufs(NS), P.bufs(3)
            for h in range(2):
                self.MS("vector", vz[h][:, :, :], 0.0, [vz_b[h]])
            for j in range(AT):
                self.DMA("sync", "at_k", kT[:, :], S["kT"][j * 128:(j + 1) * 128, :], [], [kT_b])
                self.DMA("sync", "at_q", qT[:, :], S["qT"][j * 128:(j + 1) * 128, :], [], [qT_b])
                for h in range(2):
                    for t0 in range(0, NTT, 16):
                        t1_ = min(NTT, t0 + 16)
                        self.DMA("gpsimd", f"at_v{h}", vz[h][:, t0:t1_, h * 64:(h + 1) * 64],
                                 S["v"][t0 * 128:t1_ * 128, j * 128 + h * 64:j * 128 + (h + 1) * 64].rearrange("(t p) d -> p t d", p=128),
                                 [], [vz_b[h]])
                steps = []
                for qi, (q0, w) in enumerate(qblocks):
                    kt_max = (q0 + w) // 128 - 1
                    for kt in range(kt_max, -1, -1):
                        steps.append((qi, q0, w, kt, kt == kt_max, kt == 0, kt - q0 // 128))

                def v3(t, w):
                    return t.rearrange("p (h c) -> p h c", h=2)[:, :, :w]

                def stage1(idx):
                    qi, q0, w, kt, first, last, r = steps[idx]
                    sl = idx % NS
                    pbs = [ps_b[2 * sl], ps_b[2 * sl + 1]]
                    for h in range(2):
                        hs = slice(h * 64, (h + 1) * 64)
                        self.MM(pq[sl][:, h * 512:h * 512 + w], kT[hs, kt * 128:(kt + 1) * 128], qT[hs, q0:q0 + w], True, True,
                                [kT_b, qT_b], [pbs[h]])
                    self.ACT(eb[sl][:, :, :w], v3(pq[sl], w), AF.Exp, pbs, [eb_b[sl]])
                    self.ACT(lb[sl][:, :, :w], eb[sl][:, :, :w], AF.Ln, [eb_b[sl]], [lb_b[sl]], bias=1.0)
                    if r >= 0:
                        self.TT("gpsimd", lb[sl][:, :, :w], lb[sl][:, :, :w], c["keep"][r][:, :w].unsqueeze(1).to_broadcast([128, 2, w]),
                                ALU.mult, [lb_b[sl], cb], [lb_b[sl]])

                def stage2(idx):
                    qi, q0, w, kt, first, last, r = steps[idx]
                    sl = idx % NS
                    ai = idx % 3
                    pbs = [ps_b[2 * sl], ps_b[2 * sl + 1]]
                    for h in range(2):
                        terms = [(c["tneg"][:, :], lb[sl][:, h, :w], [lb_b[sl], cb, eb_b[sl]])]
                        if not first:
                            terms.append((c["negones"][:, :], cum[:, h, :w], [cum_b, cb]))
                        if r >= 0:
                            terms.append((c["identb"][:, :], c["negm"][r][:, :w], [cb]))
                        for ti, (lt, rh, rd) in enumerate(terms):
                            self.MM(pq[sl][:, h * 512:h * 512 + w], lt, rh, False, ti == len(terms) - 1, rd, [pbs[h]])
                    self.ACT(att[ai][:, :, :w], v3(pq[sl], w), AF.Exp, pbs, [att_b[ai]])
                    if kt > 0:
                        if first:
                            self.CP("vector", cum[:, :, :w], lb[sl][:, :, :w], [lb_b[sl]], [cum_b])
                        else:
                            self.TT("vector", cum[:, :, :w], cum[:, :, :w], lb[sl][:, :, :w], ALU.add, [cum_b, lb_b[sl]], [cum_b])

                def stage2b(idx):
                    qi, q0, w, kt, first, last, r = steps[idx]
                    ai = idx % 3
                    O = 6 + qi % 2
                    for h in range(2):
                        self.MM(ps[O][:, :w], vz[h][:, kt, :], att[ai][:, h, :w], first and h == 0, last and h == 1,
                                [vz_b[h], att_b[ai]], [ps_b[O]])
                    if last:
                        self.CP("vector", aT[:, q0:q0 + w], ps[O][:, :w], [ps_b[O]], [aT_b])

                for idx in range(len(steps) + DEP + 1):
                    if idx < len(steps):
                        stage1(idx)
                    if 0 <= idx - DEP < len(steps):
                        stage2(idx - DEP)
                    if 0 <= idx - DEP - 1 < len(steps):
                        stage2b(idx - DEP - 1)
                self.DMA("gpsimd", "at_o", S["mixT"][j * 128:(j + 1) * 128, :], aT[:, :], [aT_b], [])
            P.barrier()

    def ssm(self, l, ins, S):
        nc, P, cfg = self.nc, self.P, self.cfg
        ps, ps_b, c, cb = self.ps, self.ps_b, self.c, self.cb
        G, ST, SW, AW, LP = cfg.G, cfg.ST, cfg.SW, cfg.AW, cfg.LP
        T = 512
        V = "vector"
        with ExitStack() as st:
            pb_ = P.buf("ssmprep")
            W, R = [pb_], [pb_, cb]

            def t32(name, shape):
                return self.sb(st, f"ss_{name}", shape, F32)
            lnat = t32("lnat", [G, 256])
            for hf in range(2):
                self.DMA("sync", "ss_p", lnat[:, hf * 64:(hf + 1) * 64], ins["ssm_lambda_re"][l], [], W)
                self.DMA("sync", "ss_p", lnat[:, 128 + hf * 64:128 + (hf + 1) * 64], ins["ssm_lambda_im"][l], [], W)
            lre, lim = t32("lre", [128, G]), t32("lim", [128, G])
            self.TR(ps[0][:, :G], lnat[:, 0:128], c["identf"][0:G, 0:G], R, [ps_b[0]])
            self.TS(V, lre[:, :], ps[0][:, :G], -1e-4, None, ALU.min, None, [ps_b[0]], W)
            self.TR(ps[1][:, :G], lnat[:, 128:256], c["identf"][0:G, 0:G], R, [ps_b[1]])
            self.CP(V, lim[:, :], ps[1][:, :G], [ps_b[1]], W)
            step = t32("step", [128, G])
            self.DMA("sync", "ss_p", step[:, :], ins["ssm_log_step"][l].partition_broadcast(128), [], W)
            self.ACT(step[:, :], step[:, :], AF.Exp, R, W)
            th, thr, rr, tmp, m2 = t32("th", [128, G]), t32("thr", [128, G]), t32("rr", [128, G]), t32("tmp", [128, G]), t32("m2", [128, G])
            pti = self.sb(st, "ss_pti", [128, G], mybir.dt.int32)
            ptf = t32("ptf", [128, G])
            self.TT(V, th[:, :], lim[:, :], step[:, :], ALU.mult, R, W)
            self.TS(V, th[:, :], th[:, :], 1.0 / TWO_PI, None, ALU.mult, None, R, W)
            self.FRAC(thr[:, :], th[:, :], pti[:, :], ptf[:, :], R, W)
            self.TT(V, tmp[:, :], lre[:, :], step[:, :], ALU.mult, R, W)
            self.ACT(rr[:, :], tmp[:, :], AF.Exp, R, W)
            sn, cs = t32("sn", [128, G]), t32("cs", [128, G])
            self.ACT(sn[:, :], thr[:, :], AF.Sin, R, W, scale=SC2PI)
            self.TS(V, m2[:, :], thr[:, :], 0.25, None, ALU.add, None, R, W)
            self.FRAC(m2[:, :], m2[:, :], pti[:, :], ptf[:, :], R, W)
            self.ACT(cs[:, :], m2[:, :], AF.Sin, R, W, scale=SC2PI)
            a, b, den, wre, wim, t2 = (t32("a", [128, G]), t32("b", [128, G]), t32("den", [128, G]), t32("wre", [128, G]),
                                       t32("wim", [128, G]), t32("t2", [128, G]))
            self.TT(V, a[:, :], rr[:, :], cs[:, :], ALU.mult, R, W)
            self.TS(V, a[:, :], a[:, :], -1.0, None, ALU.add, None, R, W)
            self.TT(V, b[:, :], rr[:, :], sn[:, :], ALU.mult, R, W)
            self.TT(V, den[:, :], lre[:, :], lre[:, :], ALU.mult, R, W)
            self.TT(V, tmp[:, :], lim[:, :], lim[:, :], ALU.mult, R, W)
            self.TT(V, den[:, :], den[:, :], tmp[:, :], ALU.add, R, W)
            P.op(V, lambda e: e.reciprocal(out=den[:, :], in_=den[:, :]), R, W)
            self.TT(V, wre[:, :], a[:, :], lre[:, :], ALU.mult, R, W)
            self.TT(V, tmp[:, :], b[:, :], lim[:, :], ALU.mult, R, W)
            self.TT(V, wre[:, :], wre[:, :], tmp[:, :], ALU.add, R, W)
            self.TT(V, wre[:, :], wre[:, :], den[:, :], ALU.mult, R, W)
            self.TT(V, wim[:, :], b[:, :], lre[:, :], ALU.mult, R, W)
            self.TT(V, tmp[:, :], a[:, :], lim[:, :], ALU.mult, R, W)
            self.TT(V, wim[:, :], wim[:, :], tmp[:, :], ALU.subtract, R, W)
            self.TT(V, wim[:, :], wim[:, :], den[:, :], ALU.mult, R, W)
            bre, bim = t32("bre", [128, G, 16]), t32("bim", [128, G, 16])
            for hf in range(2):
                self.DMA("sync", "ss_p", bre[hf * 64:(hf + 1) * 64, :, :], ins["ssm_b_re"][l].rearrange("g p h -> p g h"), [], W)
                self.DMA("sync", "ss_p", bim[hf * 64:(hf + 1) * 64, :, :], ins["ssm_b_im"][l].rearrange("g p h -> p g h"), [], W)
            BB, t3, t4 = t32("BB", [128, G, 16]), t32("t3", [128, G, 16]), t32("t4", [128, G, 16])
            wre_b = wre[:, :].unsqueeze(2).to_broadcast([128, G, 16])
            wim_b = wim[:, :].unsqueeze(2).to_broadcast([128, G, 16])
            self.TT(V, t3[:, :, :], bre[:, :, :], wre_b, ALU.mult, R, W)
            self.TT(V, t4[:, :, :], bim[:, :, :], wim_b, ALU.mult, R, W)
            self.TT(V, BB[0:64, :, :], t3[0:64, :, :], t4[0:64, :, :], ALU.subtract, R, W)
            self.TT(V, t3[:, :, :], bim[:, :, :], wre_b, ALU.mult, R, W)
            self.TT(V, t4[:, :, :], bre[:, :, :], wim_b, ALU.mult, R, W)
            self.TT(V, BB[64:128, :, :], t3[64:128, :, :], t4[64:128, :, :], ALU.add, R, W)
            rmask = t32("rmask", [128, 8])
            self.MS(V, rmask[:, :], 1.0, W)
            P.op("gpsimd", lambda e: e.affine_select(out=rmask[:, :], in_=rmask[:, :], pattern=[[-16, 8]], compare_op=ALU.is_ge,
                                                     fill=0.0, base=0, channel_multiplier=1), R, W)
            P.op("gpsimd", lambda e: e.affine_select(out=rmask[:, :], in_=rmask[:, :], pattern=[[16, 8]], compare_op=ALU.is_ge,
                                                     fill=0.0, base=15, channel_multiplier=-1), R, W)
            lhsB = self.sb(st, "ss_lhsB", [128, G, 128], BF16)
            lhsBs = self.sb(st, "ss_lhsBs", [128, G, 128], BF16)
            lhsC1 = self.sb(st, "ss_lhsC1", [128, G, 128], BF16)
            lhsC2 = self.sb(st, "ss_lhsC2", [128, G, 128], BF16)
            BT = t32("BT", [128, 128])
            for j in range(ST):
                self.TR(ps[2][:, :128], BB[:, 8 * j:8 * j + 8, :].rearrange("p g h -> p (g h)"), c["identf"][:, :], R, [ps_b[2]])
                self.CP(V, BT[:, :], ps[2][:, :128], [ps_b[2]], W)
                for gg in range(8):
                    g = 8 * j + gg
                    self.TS(V, lhsB[:, g, :], BT[:, :], rmask[:, gg:gg + 1], None, ALU.mult, None, R, W)
                    self.CP(V, lhsBs[:, g, 0:64], lhsB[:, g, 64:128], R, W)
                    self.TS(V, lhsBs[:, g, 64:128], lhsB[:, g, 0:64], -1.0, None, ALU.mult, None, R, W)
            cn1, cn2 = t32("cn1", [128, ST, 128]), t32("cn2", [128, ST, 128])
            cre = ins["ssm_c_re"][l].rearrange("(j gg) h p -> (gg h) j p", gg=8)
            cim = ins["ssm_c_im"][l].rearrange("(j gg) h p -> (gg h) j p", gg=8)
            self.DMA("sync", "ss_p", cn1[:, :, 0:64], cre, [], W)
            self.DMA("sync", "ss_p", cn1[:, :, 64:128], cim, [], W)
            self.DMA("sync", "ss_p", cn2[:, :, 0:64], cim, [], W)
            self.DMA("sync", "ss_p", cn2[:, :, 64:128], cre, [], W)
            self.MS(V, lhsC1[:, :, :], 0.0, W)
            self.MS(V, lhsC2[:, :, :], 0.0, W)
            CT1, CT2 = t32("CT1", [128, 128]), t32("CT2", [128, 128])
            for j in range(ST):
                self.TR(ps[3][:, :128], cn1[:, j, :], c["identf"][:, :], R, [ps_b[3]])
                self.CP(V, CT1[0:64, :], ps[3][0:64, :128], [ps_b[3]], W)
                self.TS(V, CT1[64:128, :], ps[3][64:128, :128], -1.0, None, ALU.mult, None, [ps_b[3]], W)
                self.TR(ps[2][:, :128], cn2[:, j, :], c["identf"][:, :], R, [ps_b[2]])
                self.TS(V, CT2[:, :], ps[2][:, :128], -1.0, None, ALU.mult, None, [ps_b[2]], W)
                for gg in range(8):
                    g = 8 * j + gg
                    self.CP(V, lhsC1[:, g, 16 * gg:16 * gg + 16], CT1[:, 16 * gg:16 * gg + 16], R, W)
                    self.CP(V, lhsC2[:, g, 16 * gg:16 * gg + 16], CT2[:, 16 * gg:16 * gg + 16], R, W)
            wglu = self.sb(st, "ss_wglu", [128, ST, SW], BF16)
            self.DMA("sync", "ss_p", wglu[:, :, :], self.w[l]["ssm_w_glu"], [self.w_b], W)
            tt_all = t32("tt", [128, LP])
            P.op("gpsimd", lambda e: e.iota(tt_all[:, :], pattern=[[1, LP]], base=0, channel_multiplier=0,
                                            allow_small_or_imprecise_dtypes=True), [], W)
            vlast = t32("vlast", [128, G])
            self.MS(V, vlast[:, :], 0.0, W)
            uT = [self.sb(st, f"ss_u{i}", [128, ST, T], BF16) for i in range(2)]
            m1 = [t32(f"m1{i}", [128, T]) for i in range(4)]
            m2t = [t32(f"m2{i}", [128, T]) for i in range(4)]
            fi1 = [self.sb(st, f"ss_fi1{i}", [128, T], mybir.dt.int32) for i in range(4)]
            ff1 = [t32(f"ff1{i}", [128, T]) for i in range(4)]
            fi_b, ff_b = P.bufs(4), P.bufs(4)
            snT = [t32(f"snT{i}", [128, T]) for i in range(4)]
            csT = [t32(f"csT{i}", [128, T]) for i in range(4)]
            z1 = [t32(f"z1{i}", [128, T]) for i in range(2)]
            z2 = [t32(f"z2{i}", [128, T]) for i in range(2)]
            vv = [t32(f"vv{i}", [128, T]) for i in range(2)]
            Ab = [self.sb(st, f"ss_A{i}", [128, T], BF16) for i in range(2)]
            Bb = [self.sb(st, f"ss_B{i}", [128, T], BF16) for i in range(2)]
            yf = t32("yf", [128, T])
            yg = self.sb(st, "ss_yg", [128, ST, T], BF16)
            sgl = t32("sgl", [128, T])
            so = [self.sb(st, f"ss_so{i}", [128, T], BF16) for i in range(2)]
            uT_b, m1_b, m2_b, sn_b, cs_b, z1_b, z2_b, vv_b, Ab_b, Bb_b, so_b = [P.bufs(4) for _ in range(11)]
            yf_b, yg_b, sgl_b, vl_b = P.buf(), P.buf(), P.buf(), P.buf()
            dcol = self.pvcol["ssm_d"]
            bgcol = self.pvcol["ssm_b_glu"]
            blocks = cfg.blocks(T)
            gsteps = [(bi, j, gg) for bi in range(len(blocks)) for j in range(ST) for gg in range(8)]

            NB = 4

            def stageT1(k):
                bi, j, gg = gsteps[k]
                s0, n = blocks[bi]
                g = 8 * j + gg
                i = k % NB
                if j == 0 and gg == 0:
                    ui = bi % 2
                    self.DMA("sync", f"ss_u{ui}", uT[ui][:, :, :n], S["uT"][:, s0:s0 + n].rearrange("(k p) t -> p k t", p=128), [], [uT_b[ui]])
                self.ACT(m1[i][:, :n], tt_all[:, s0:s0 + n], AF.Copy, [pb_], [m1_b[i]], scale=thr[:, g:g + 1])
                self.CP("scalar", fi1[i][:, :n], m1[i][:, :n], [m1_b[i]], [fi_b[i]])

            def stageT2(k):
                bi, j, gg = gsteps[k]
                s0, n = blocks[bi]
                i = k % NB
                self.CP("gpsimd", ff1[i][:, :n], fi1[i][:, :n], [fi_b[i]], [ff_b[i]])
                self.TT("gpsimd", m1[i][:, :n], m1[i][:, :n], ff1[i][:, :n], ALU.subtract, [m1_b[i], ff_b[i]], [m1_b[i]])

            def stageT3(k):
                bi, j, gg = gsteps[k]
                s0, n = blocks[bi]
                i = k % NB
                self.ACT(snT[i][:, :n], m1[i][:, :n], AF.Sin, [m1_b[i]], [sn_b[i]], scale=SC2PI)
                self.ACT(m2t[i][:, :n], m1[i][:, :n], AF.Abs, [m1_b[i]], [m2_b[i]])
                self.ACT(csT[i][:, :n], m2t[i][:, :n], AF.Sin, [m2_b[i]], [cs_b[i]], bias=self.hpib[:, 0:1], scale=-SC2PI)

            def stageM(k):
                bi, j, gg = gsteps[k]
                s0, n = blocks[bi]
                g = 8 * j + gg
                i = k % NB
                i2 = k % 2
                ui = bi % 2
                Y = 4 + (j % 2)
                A_, B_ = 2 * (k % 2), 2 * (k % 2) + 1
                self.MM(ps[A_][:, :n], lhsB[:, g, :], uT[ui][:, j, :n], True, True, [pb_, uT_b[ui]], [ps_b[A_]])
                self.MM(ps[B_][:, :n], lhsBs[:, g, :], uT[ui][:, j, :n], True, True, [pb_, uT_b[ui]], [ps_b[B_]])
                self.TT(V, z1[i2][:, :n], ps[A_][:, :n], csT[i][:, :n], ALU.mult, [ps_b[A_], cs_b[i]], [z1_b[i2]])
                self.TT(V, z2[i2][:, :n], ps[B_][:, :n], snT[i][:, :n], ALU.mult, [ps_b[B_], sn_b[i]], [z2_b[i2]])
                self.TT(V, z1[i2][:, :n], z1[i2][:, :n], z2[i2][:, :n], ALU.add, [z1_b[i2], z2_b[i2]], [z1_b[i2]])
                rb = rr[:, g:g + 1].to_broadcast([128, n])
                P.op(V, lambda e: e.tensor_tensor_scan(
                    out=vv[i2][:, :n], data0=rb, data1=z1[i2][:, :n], initial=vlast[:, g:g + 1], op0=ALU.mult, op1=ALU.add),
                    [z1_b[i2], vl_b, pb_], [vv_b[i2]])
                self.CP(V, vlast[:, g:g + 1], vv[i2][:, n - 1:n], [vv_b[i2]], [vl_b])
                self.TT(V, Ab[i2][:, :n], vv[i2][:, :n], csT[i][:, :n], ALU.mult, [vv_b[i2], cs_b[i]], [Ab_b[i2]])
                self.TT("gpsimd", Bb[i2][:, :n], vv[i2][:, :n], snT[i][:, :n], ALU.mult, [vv_b[i2], sn_b[i]], [Bb_b[i2]])
                self.MM(ps[Y][:, :n], lhsC1[:, g, :], Ab[i2][:, :n], gg == 0, False, [pb_, Ab_b[i2]], [ps_b[Y]])
                self.MM(ps[Y][:, :n], lhsC2[:, g, :], Bb[i2][:, :n], False, gg == 7, [pb_, Bb_b[i2]], [ps_b[Y]])
                if gg == 7:
                    self.STT(V, yf[:, :n], uT[ui][:, j, :n], self.pv[l][:, dcol + j:dcol + j + 1], ps[Y][:, :n], ALU.mult, ALU.add,
                             [uT_b[ui], ps_b[Y], self.pv_b], [yf_b])
                    self.ACT(yg[:, j, :n], yf[:, :n], AF.Gelu, [yf_b], [yg_b])
                    if j == ST - 1:
                        for cc in range(ST):
                            Gk = 6 + (cc % 2)
                            for k2 in range(ST):
                                self.MM(ps[Gk][:, :n], wglu[:, k2, cc * 128:(cc + 1) * 128], yg[:, k2, :n], k2 == 0, k2 == ST - 1,
                                        [pb_, yg_b], [ps_b[Gk]])
                            self.ACT(sgl[:, :n], ps[Gk][:, :n], AF.Sigmoid, [ps_b[Gk], self.pv_b], [sgl_b],
                                     bias=self.pv[l][:, bgcol + cc:bgcol + cc + 1])
                            oi = (bi * ST + cc) % 2
                            self.TT(V, so[oi][:, :n], yg[:, cc, :n], sgl[:, :n], ALU.mult, [yg_b, sgl_b], [so_b[oi]])
                            self.DMA("gpsimd", f"ss_so{oi}", S["mixT"][AW + cc * 128:AW + (cc + 1) * 128, s0:s0 + n], so[oi][:, :n],
                                     [so_b[oi]], [])

            ng = len(gsteps)
            for k in range(-3, ng):
                if 0 <= k + 3 < ng:
                    stageT1(k + 3)
                if 0 <= k + 2 < ng:
                    stageT2(k + 2)
                if 0 <= k + 1 < ng:
                    stageT3(k + 1)
                if k >= 0:
                    stageM(k)
            P.barrier()

    def conv(self, l, S):
        nc, P, cfg = self.nc, self.P, self.cfg
        ps, ps_b, c, cb = self.ps, self.ps_b, self.c, self.cb
        CT, CW, AW, SW = cfg.CT, cfg.CW, cfg.AW, cfg.SW
        T = 512
        V = "vector"
        pv = self.pv[l]
        col = self.pvcol
        with ExitStack() as st:
            Dg = self.sb(st, "cv_Dg", [128, CT * 31, 128], BF16)
            Dg_b = P.buf()
            for ct in range(CT):
                for j in range(31):
                    self.TS(V, Dg[:, ct * 31 + j, :], c["identf"][:, :], self.wdwT[l][:, ct, j:j + 1], None,
                            ALU.mult, None, [cb, self.pv_b], [Dg_b])
            wpw = self.sb(st, "cv_wpw", [128, CT, CW], BF16)
            wpw_b = P.buf()
            self.DMA("sync", "cv_w", wpw[:, :, :], self.w[l]["conv_w_pw"], [self.w_b], [wpw_b])
            gp = [self.sb(st, f"cv_gp{i}", [128, CT, 30 + T], BF16) for i in range(2)]
            hcf = self.sb(st, "cv_hcf", [128, CT, T], F32)
            hcb = self.sb(st, "cv_hcb", [128, CT, T], BF16)
            sqb = [self.sb(st, f"cv_sq{i}", [128, T], BF16) for i in range(2)]
            meanS = self.sb(st, "cv_mean", [128, T], F32)
            m2 = self.sb(st, "cv_m2", [128, T], F32)
            rsS = self.sb(st, "cv_rs", [128, T], F32)
            t1 = [self.sb(st, f"cv_t1{i}", [128, T], F32) for i in range(2)]
            hsb = self.sb(st, "cv_hs", [128, CT, T], BF16)
            ob = [self.sb(st, f"cv_ob{i}", [128, T], BF16) for i in range(2)]
            gp_b, sqb_b, t1_b, ob_b = P.bufs(2), P.bufs(2), P.bufs(2), P.bufs(2)
            hcf_b, hcb_b, mean_b, m2_b, rs_b, hs_b = P.buf(), P.buf(), P.buf(), P.buf(), P.buf(), P.buf()
            cnt = dict(sq=0, t1=0, ob=0)
            for bi, (s0, n) in enumerate(cfg.blocks(T)):
                gi = bi % 2
                if s0 == 0:
                    self.MS(V, gp[gi][:, :, 0:30], 0.0, [gp_b[gi]])
                    self.DMA("sync", f"cv_gp{gi}", gp[gi][:, :, 30:30 + n], S["gluT"][:, 0:n].rearrange("(k p) t -> p k t", p=128), [], [gp_b[gi]])
                else:
                    self.DMA("sync", f"cv_gp{gi}", gp[gi][:, :, 0:30 + n], S["gluT"][:, s0 - 30:s0 + n].rearrange("(k p) t -> p k t", p=128),
                             [], [gp_b[gi]])
                for ct in range(CT):
                    bank = ct % 4
                    for j in range(31):
                        self.MM(ps[bank][:, :n], Dg[:, ct * 31 + j, :], gp[gi][:, ct, j:j + n], j == 0, j == 30, [Dg_b, gp_b[gi]], [ps_b[bank]])
                    self.TS(V, hcf[:, ct, :n], ps[bank][:, :n], pv[:, col["conv_b_dw"] + ct:col["conv_b_dw"] + ct + 1], None, ALU.add, None,
                            [ps_b[bank], self.pv_b], [hcf_b])
                    self.CP("gpsimd", hcb[:, ct, :n], hcf[:, ct, :n], [hcf_b], [hcb_b])
                    qi = cnt["sq"] % 2; cnt["sq"] += 1
                    self.ACT(sqb[qi][:, :n], hcf[:, ct, :n], AF.Square, [hcf_b], [sqb_b[qi]])
                    self.MM(ps[4][:, :n], c["onesCW"][:, :], hcb[:, ct, :n], ct == 0, ct == CT - 1, [hcb_b, cb], [ps_b[4]])
                    self.MM(ps[5][:, :n], c["onesCW"][:, :], sqb[qi][:, :n], ct == 0, ct == CT - 1, [sqb_b[qi], cb], [ps_b[5]])
                self.ACT(meanS[:, :n], ps[4][:, :n], AF.Copy, [ps_b[4]], [mean_b])
                self.TT("gpsimd", m2[:, :n], meanS[:, :n], meanS[:, :n], ALU.mult, [mean_b], [m2_b])
                self.TT(V, m2[:, :n], ps[5][:, :n], m2[:, :n], ALU.subtract, [ps_b[5], m2_b], [m2_b])
                self.RSQ(rsS[:, :n], m2[:, :n], 1e-5, [m2_b], [rs_b])
                for ct in range(CT):
                    ti = cnt["t1"] % 2; cnt["t1"] += 1
                    self.TT(V, t1[ti][:, :n], hcf[:, ct, :n], meanS[:, :n], ALU.subtract, [hcf_b, mean_b], [t1_b[ti]])
                    self.TT("gpsimd", t1[ti][:, :n], t1[ti][:, :n], rsS[:, :n], ALU.mult, [t1_b[ti], rs_b], [t1_b[ti]])
                    self.ACT(hsb[:, ct, :n], t1[ti][:, :n], AF.Silu, [t1_b[ti], self.pv_b], [hs_b],
                             bias=pv[:, col["conv_ln_b"] + ct:col["conv_ln_b"] + ct + 1],
                             scale=pv[:, col["conv_ln_g"] + ct:col["conv_ln_g"] + ct + 1])
                if self.debug == 2 and bi == 0 and l == 0:
                    for nm_, t_, b_, dt_ in (("hcf", hcf, hcf_b, F32), ("hsb", hsb, hs_b, BF16)):
                        dbg = nc.dram_tensor(f"dbg_{nm_}", [128, CT, T], dt_, kind="ExternalOutput").ap()
                        self.DMA("sync", "dbg", dbg[:, :, :], t_[:, :, :], [b_], [])
                    for nm_, t_, b_ in (("mean", meanS, mean_b), ("rs", rsS, rs_b)):
                        dbg = nc.dram_tensor(f"dbg_{nm_}", [128, T], F32, kind="ExternalOutput").ap()
                        self.DMA("sync", "dbg", dbg[:, :], t_[:, :], [b_], [])
                for cc in range(CT):
                    bank = 6 + (cc % 2)
                    for ct in range(CT):
                        self.MM(ps[bank][:, :n], wpw[:, ct, cc * 128:(cc + 1) * 128], hsb[:, ct, :n], ct == 0, ct == CT - 1,
                                [wpw_b, hs_b], [ps_b[bank]])
                    oi = cnt["ob"] % 2; cnt["ob"] += 1
                    self.TS(V, ob[oi][:, :n], ps[bank][:, :n], pv[:, col["conv_b_pw"] + cc:col["conv_b_pw"] + cc + 1], None, ALU.add, None,
                            [ps_b[bank], self.pv_b], [ob_b[oi]])
                    self.DMA("gpsimd", f"cv_ob{oi}", S["mixT"][AW + SW + cc * 128:AW + SW + (cc + 1) * 128, s0:s0 + n], ob[oi][:, :n],
                             [ob_b[oi]], [])
            P.barrier()

    def out_proj(self, l, hin, hout, S):
        nc, P, cfg = self.nc, self.P, self.cfg
        ps, ps_b = self.ps, self.ps_b
        KT = cfg.KT
        T = 1024
        TM = min(T, cfg.LP)
        wout = self.w[l]["w_out"]
        with ExitStack() as st:
            xT = [self.sb(st, f"op_x{i}", [128, KT, TM], BF16) for i in range(2)]
            wt = [self.sb(st, f"op_w{i}", [128, KT, 128], BF16) for i in range(3)]
            hp = [self.sb(st, f"op_hp{i}", [128, TM], F32) for i in range(3)]
            ob = [self.sb(st, f"op_ob{i}", [128, TM], F32) for i in range(2)]
            xT_b, wt_b, hp_b, ob_b = P.bufs(2), P.bufs(3), P.bufs(3), P.bufs(2)
            cnt = 0
            for bi, (s0, n) in enumerate(cfg.blocks(T)):
                subs = subtiles(n)
                xi = bi % 2
                self.DMA("sync", f"op_x{xi}", xT[xi][:, :, :n], S["mixT"][:, s0:s0 + n].rearrange("(k p) t -> p k t", p=128), [], [xT_b[xi]])
                for d in range(KT):
                    wi = cnt % 3
                    i = cnt % 3
                    oi = cnt % 2
                    cnt += 1
                    self.DMA("sync", f"op_w{wi}", wt[wi][:, :, :], wout[:, :, d * 128:(d + 1) * 128], [self.w_b], [wt_b[wi]])
                    self.DMA("sync", f"op_hp{i}", hp[i][:, :n], hin[d * 128:(d + 1) * 128, s0:s0 + n], [], [hp_b[i]])
                    pb = (d % 2) * 2
                    for k in range(KT):
                        for si, (o, w) in enumerate(subs):
                            self.MM(ps[pb + si][:, :w], wt[wi][:, k, :], xT[xi][:, k, o:o + w], k == 0, k == KT - 1,
                                    [wt_b[wi], xT_b[xi]], [ps_b[pb + si]])
                    for si, (o, w) in enumerate(subs):
                        self.TT("vector", ob[oi][:, o:o + w], ps[pb + si][:, :w], hp[i][:, o:o + w], ALU.add,
                                [ps_b[pb + si], hp_b[i]], [ob_b[oi]])
                    self.DMA("gpsimd", f"op_ob{oi}", hout[d * 128:(d + 1) * 128, s0:s0 + n], ob[oi][:, :n], [ob_b[oi]], [])
            P.barrier()


WEIGHT_NAMES = ["ffn1_norm", "ffn1_w_gate", "ffn1_w_up", "ffn1_w_down", "mix_norm", "w_in", "q_norm", "k_norm",
                "ssm_lambda_re", "ssm_lambda_im", "ssm_log_step", "ssm_b_re", "ssm_b_im", "ssm_c_re", "ssm_c_im", "ssm_d",
                "ssm_w_glu", "ssm_b_glu", "conv_w_dw", "conv_b_dw", "conv_ln_g", "conv_ln_b", "conv_w_pw", "conv_b_pw",
                "w_out", "ffn2_norm", "ffn2_w_gate", "ffn2_w_up", "ffn2_w_down", "post_norm"]


def build(cfg, shapes, debug=False, stop_after=None):
    nc = bass.Bass("TRN2", target_bir_lowering=False)
    ins = {nm: nc.dram_tensor(nm, list(shapes[nm]), F32, kind="ExternalInput").ap() for nm in WEIGHT_NAMES}
    x0 = nc.dram_tensor("x0", [cfg.LP, cfg.D], F32, kind="ExternalInput").ap()
    out = nc.dram_tensor("out", [cfg.LP, cfg.D], F32, kind="ExternalOutput").ap()
    knd = "ExternalOutput" if debug else "Internal"
    hA = nc.dram_tensor("hA", [cfg.D, cfg.LP], F32, kind=knd).ap()
    hB = nc.dram_tensor("hB", [cfg.D, cfg.LP], F32, kind=knd).ap()
    S = {
        "qT": nc.dram_tensor("s_qT", [cfg.AW, cfg.LP], BF16, kind=knd).ap(),
        "kT": nc.dram_tensor("s_kT", [cfg.AW, cfg.LP], BF16, kind=knd).ap(),
        "v": nc.dram_tensor("s_v", [cfg.LP, cfg.AW], BF16, kind=knd).ap(),
        "uT": nc.dram_tensor("s_uT", [cfg.SW, cfg.LP], BF16, kind=knd).ap(),
        "gluT": nc.dram_tensor("s_gluT", [cfg.CW, cfg.LP], BF16, kind=knd).ap(),
        "mixT": nc.dram_tensor("s_mixT", [cfg.MIX, cfg.LP], BF16, kind=knd).ap(),
    }
    with ExitStack() as st:
        k = K(nc, cfg, st, debug)
        k.consts()
        k.load_params(ins)
        k.cast_weights(ins)
        k.transpose_in(x0, hA)
        cur, oth = hA, hB
        done = False
        for l in range(cfg.depth):
            w = k.w[l]
            k.ffn(cur, [], oth, [], w["ffn1_w_gate"], w["ffn1_w_up"], w["ffn1_w_down"], k.w_b0 if l == 0 else k.w_b,
                  k.gain(l, "ffn1_norm"), f"a{l}")
            cur, oth = oth, cur
            if stop_after == ("ffn1", l):
                break
            k.in_proj(l, cur, S)
            if stop_after == ("in_proj", l):
                break
            k.attention(l, S)
            k.ssm(l, ins, S)
            k.conv(l, S)
            if stop_after == ("mix", l):
                break
            k.out_proj(l, cur, oth, S)
            cur, oth = oth, cur
            if stop_after == ("out_proj", l):
                break
            k.ffn(cur, [], oth, [], w["ffn2_w_gate"], w["ffn2_w_up"], w["ffn2_w_down"], k.w_b, k.gain(l, "ffn2_norm"), f"b{l}")
            cur, oth = oth, cur
            if l == cfg.depth - 1:
                k.post_norm(cur, k.gain(l, "post_norm"), out_tok=out)
            else:
                k.post_norm(cur, k.gain(l, "post_norm"), hout=oth)
                cur, oth = oth, cur
        k.P.barrier()
        k.P.emit()
    return nc, k


SEQ, NMETA, BATCH = 8192, 16, 4
_CACHE = {}


def kernel(**inputs):
    cfg = Cfg()
    x = np.asarray(inputs["x"], dtype=np.float32)
    meta = np.asarray(inputs["meta"], dtype=np.float32)
    shapes = {nm: inputs[nm].shape for nm in WEIGHT_NAMES}
    if "nc" not in _CACHE:
        _CACHE["nc"] = build(cfg, shapes)[0]
    nc = _CACHE["nc"]
    wts = {nm: np.ascontiguousarray(np.asarray(inputs[nm], dtype=np.float32)) for nm in WEIGHT_NAMES}
    in_maps = []
    real = {0: 0, 1: 1, 4: 2, 5: 3}
    zeros = np.zeros((cfg.LP, cfg.D), np.float32)
    for core in range(8):
        m = dict(wts)
        if core in real:
            x0 = np.zeros((cfg.LP, cfg.D), np.float32)
            x0[:NMETA] = meta
            x0[NMETA:NMETA + SEQ] = x[real[core]]
            m["x0"] = x0
        else:
            m["x0"] = zeros
        in_maps.append(m)
    res = run_bass_kernel_spmd(nc, in_maps, core_ids=list(range(8)))
    outs = [res.results[cidx]["out"][NMETA:NMETA + SEQ] for cidx in (0, 1, 4, 5)]
    return np.stack(outs, axis=0).astype(np.float32)
```
